# Optimizing a Trainium2 kernel written in Bass

```python
import math
import numpy as np
import jax
import jax.numpy as jnp
from jax import lax

D_MODEL = 2048
BATCH = 1
SEQ = 8192
DEPTH = 2

GRID_W = 64
CTX_LEN = 256
HEAD_DIM = 128
NA_HEADS = 8
NB_Q_HEADS = 8
NB_KV_HEADS = 2
NA_ROWS = 8
NA_COLS = 16
NA_QCOLS = 16
NA_KCOLS = NA_QCOLS + NA_COLS
SW_RADIUS = 128
SW_BLOCK = 128
ROPE_BASE = 10000.0
SSM_GROUP = 16
SSM_GROUPS = D_MODEL // SSM_GROUP
SSM_STATE = 64
D_FF = -(-8 * D_MODEL // (3 * 256)) * 256
A_WIDTH = NA_HEADS * HEAD_DIM
B_Q_WIDTH = NB_Q_HEADS * HEAD_DIM
B_KV_WIDTH = NB_KV_HEADS * HEAD_DIM
IN_WIDTH = 3 * A_WIDTH + B_Q_WIDTH + 2 * B_KV_WIDTH
MIX_WIDTH = A_WIDTH + B_Q_WIDTH
IN_SPLITS = (A_WIDTH, 2 * A_WIDTH, 3 * A_WIDTH, 3 * A_WIDTH + B_Q_WIDTH, 3 * A_WIDTH + B_Q_WIDTH + B_KV_WIDTH)
EPS = 1e-6
NEG_INF = -1e30

kernel_name = 'hybrid_natten_swa_s5_prefix_dit'


def rms_norm(x, g):
    xf = x.astype(jnp.float32)
    y = xf * lax.rsqrt(jnp.mean(xf * xf, axis=-1, keepdims=True) + EPS)
    return (y * g.astype(jnp.float32)).astype(x.dtype)


def ada_mod(cvec, w, b):
    return jnp.split(jax.nn.silu(cvec) @ w + b, 6, axis=-1)


def swiglu(h, w1, w3, w2):
    return (jax.nn.silu(h @ w1) * (h @ w3)) @ w2


def axial_rope(x, pos_r, pos_c):
    half = x.shape[-1] // 2
    quarter = half // 2
    inv_freq = ROPE_BASE ** (-jnp.arange(quarter, dtype=jnp.float32) / quarter)
    xf = x.astype(jnp.float32)

    def rotate(xa, pos):
        ang = pos[:, None] * inv_freq[None, :]
        cos = jnp.cos(ang)[None, :, None, :]
        sin = jnp.sin(ang)[None, :, None, :]
        x1, x2 = xa[..., :quarter], xa[..., quarter:]
        return jnp.concatenate([x1 * cos - x2 * sin, x2 * cos + x1 * sin], axis=-1)

    return jnp.concatenate([rotate(xf[..., :half], pos_r), rotate(xf[..., half:], pos_c)], axis=-1).astype(x.dtype)


def context_attention(q, k, v, sink):
    b, m, hq, d = q.shape
    hkv = k.shape[2]
    grp = hq // hkv
    qg = q.reshape(b, m, hkv, grp, d)
    s = jnp.einsum('bqhgd,bkhd->bhgqk', qg, k, preferred_element_type=jnp.float32) * (d ** -0.5)
    if sink is not None:
        s_sink = jnp.broadcast_to(sink.astype(jnp.float32).reshape(1, hkv, grp, 1, 1), s.shape[:-1] + (1,))
        s = jnp.concatenate([s, s_sink], axis=-1)
    p = jax.nn.softmax(s, axis=-1)[..., :m].astype(v.dtype)
    return jnp.einsum('bhgqk,bkhd->bqhgd', p, v).reshape(b, m, hq, d)


def neighbourhood_attention(q, k, v, k_ctx, v_ctx, rpb):
    b, seq, h, d = q.shape
    rows = seq // GRID_W
    kh = min(NA_ROWS, rows)
    ncb = GRID_W // NA_QCOLS
    r = jnp.arange(rows)
    key_rows = jnp.clip(r - kh // 2, 0, rows - kh)[:, None] + jnp.arange(kh)[None, :]
    blk = jnp.arange(ncb)
    key_cols = (jnp.clip(blk * NA_QCOLS - NA_COLS // 2, 0, GRID_W - NA_KCOLS)[:, None]
                + jnp.arange(NA_KCOLS)[None, :])
    q_cols = blk[:, None] * NA_QCOLS + jnp.arange(NA_QCOLS)[None, :]
    win_start = jnp.clip(q_cols - NA_COLS // 2, 0, GRID_W - NA_COLS)[..., None]
    kc_b = key_cols[:, None, :]
    col_valid = (kc_b >= win_start) & (kc_b < win_start + NA_COLS)
    row_idx = key_rows - r[:, None] + NA_ROWS - 1
    col_idx = jnp.clip(kc_b - q_cols[..., None] + NA_COLS - 1, 0, 2 * NA_COLS - 2)
    bias = rpb.astype(jnp.float32)[:, row_idx[:, None, None, :, None], col_idx[None, :, :, None, :]]

    gather_r = key_rows[:, None, :, None]
    gather_c = key_cols[None, :, None, :]
    kg = k.reshape(b, rows, GRID_W, h, d)[:, gather_r, gather_c]
    vg = v.reshape(b, rows, GRID_W, h, d)[:, gather_r, gather_c]
    qg = q.reshape(b, rows, ncb, NA_QCOLS, h, d)
    scale = d ** -0.5
    n_loc = kh * NA_KCOLS
    s_loc = jnp.einsum('brnqhd,brnikhd->bhrnqik', qg, kg, preferred_element_type=jnp.float32) * scale + bias[None]
    s_loc = jnp.where(col_valid[:, :, None, :], s_loc, NEG_INF).reshape(b, h, rows, ncb, NA_QCOLS, n_loc)
    s_ctx = jnp.einsum('brnqhd,bmhd->bhrnqm', qg, k_ctx, preferred_element_type=jnp.float32) * scale
    p = jax.nn.softmax(jnp.concatenate([s_loc, s_ctx], axis=-1), axis=-1).astype(v.dtype)
    p_loc = p[..., :n_loc].reshape(b, h, rows, ncb, NA_QCOLS, kh, NA_KCOLS)
    o = (jnp.einsum('bhrnqik,brnikhd->brnqhd', p_loc, vg)
         + jnp.einsum('bhrnqm,bmhd->brnqhd', p[..., n_loc:], v_ctx))
    return o.reshape(b, seq, h, d)


def window_attention(q, k, v, k_ctx, v_ctx, sink):
    b, seq, hq, d = q.shape
    hkv = k.shape[2]
    grp = hq // hkv
    nb = seq // SW_BLOCK
    qb = q.reshape(b, nb, SW_BLOCK, hkv, grp, d)

    def band(t):
        tb = jnp.pad(t, ((0, 0), (SW_BLOCK, SW_BLOCK), (0, 0), (0, 0))).reshape(b, nb + 2, SW_BLOCK, hkv, d)
        return jnp.concatenate([tb[:, :-2], tb[:, 1:-1], tb[:, 2:]], axis=2)

    kb, vb = band(k), band(v)
    blocks = jnp.arange(nb)[:, None]
    q_pos = blocks * SW_BLOCK + jnp.arange(SW_BLOCK)[None, :]
    k_pos = (blocks - 1) * SW_BLOCK + jnp.arange(3 * SW_BLOCK)[None, :]
    k_pos_b = k_pos[:, None, :]
    valid = (jnp.abs(k_pos_b - q_pos[:, :, None]) <= SW_RADIUS) & (k_pos_b >= 0) & (k_pos_b < seq)
    scale = d ** -0.5
    s_loc = jnp.einsum('bnqhgd,bnkhd->bhgnqk', qb, kb, preferred_element_type=jnp.float32) * scale
    s_loc = jnp.where(valid, s_loc, NEG_INF)
    s_ctx = jnp.einsum('bnqhgd,bmhd->bhgnqm', qb, k_ctx, preferred_element_type=jnp.float32) * scale
    s_sink = jnp.broadcast_to(sink.astype(jnp.float32).reshape(1, hkv, grp, 1, 1, 1), s_loc.shape[:-1] + (1,))
    p = jax.nn.softmax(jnp.concatenate([s_loc, s_ctx, s_sink], axis=-1), axis=-1).astype(v.dtype)
    n_loc = 3 * SW_BLOCK
    m = k_ctx.shape[1]
    o = (jnp.einsum('bhgnqk,bnkhd->bnqhgd', p[..., :n_loc], vb)
         + jnp.einsum('bhgnqm,bmhd->bnqhgd', p[..., n_loc:n_loc + m], v_ctx))
    return o.reshape(b, seq, hq, d)


def hybrid_attention(hx, hc, w_in, w_out, rpb, sink, pos_r, pos_c, need_ctx):
    def project(h):
        b, t, _ = h.shape
        parts = jnp.split(h @ w_in, IN_SPLITS, axis=-1)
        return [z.reshape(b, t, -1, HEAD_DIM) for z in parts]

    b, seq, _ = hx.shape
    qa, ka, va, qb, kb, vb = project(hx)
    qa_c, ka_c, va_c, qb_c, kb_c, vb_c = project(hc)
    oa = neighbourhood_attention(qa, ka, va, ka_c, va_c, rpb)
    ob = window_attention(axial_rope(qb, pos_r, pos_c), axial_rope(kb, pos_r, pos_c), vb, kb_c, vb_c, sink)
    yx = jnp.concatenate([oa.reshape(b, seq, A_WIDTH), ob.reshape(b, seq, B_Q_WIDTH)], axis=-1) @ w_out
    if not need_ctx:
        return yx, None
    m = hc.shape[1]
    oa_c = context_attention(qa_c, ka_c, va_c, None)
    ob_c = context_attention(qb_c, kb_c, vb_c, sink)
    yc = jnp.concatenate([oa_c.reshape(b, m, A_WIDTH), ob_c.reshape(b, m, B_Q_WIDTH)], axis=-1) @ w_out
    return yx, yc


def s5_discretise(a_re, a_im, log_dt, b_re, b_im):
    ar = a_re.astype(jnp.float32)
    ai = a_im.astype(jnp.float32)
    dt = jnp.exp(log_dt.astype(jnp.float32))[:, None]
    mag = jnp.exp(ar * dt)
    lam_r, lam_i = mag * jnp.cos(ai * dt), mag * jnp.sin(ai * dt)
    den = ar * ar + ai * ai
    nr = lam_r - 1.0
    coef_r = (nr * ar + lam_i * ai) / den
    coef_i = (lam_i * ar - nr * ai) / den
    br, bi = b_re.astype(jnp.float32), b_im.astype(jnp.float32)
    bbar_r = coef_r[..., None] * br - coef_i[..., None] * bi
    bbar_i = coef_r[..., None] * bi + coef_i[..., None] * br
    return lam_r, lam_i, bbar_r, bbar_i


def complex_diag_scan(lam_r, lam_i, u_r, u_i, reverse):
    a_r = jnp.broadcast_to(lam_r, u_r.shape)
    a_i = jnp.broadcast_to(lam_i, u_i.shape)

    def combine(e1, e2):
        a1r, a1i, b1r, b1i = e1
        a2r, a2i, b2r, b2i = e2
        return (a2r * a1r - a2i * a1i, a2r * a1i + a2i * a1r,
                a2r * b1r - a2i * b1i + b2r, a2r * b1i + a2i * b1r + b2i)

    return lax.associative_scan(combine, (a_r, a_i, u_r, u_i), reverse=reverse, axis=1)


def s5_direction(ux, uc, a_re, a_im, log_dt, b_re, b_im, c_re, c_im, reverse, need_ctx):
    lam_r, lam_i, bbar_r, bbar_i = s5_discretise(a_re, a_im, log_dt, b_re, b_im)
    cr, ci = c_re.astype(jnp.float32), c_im.astype(jnp.float32)

    def drive(u):
        ug = u.astype(jnp.float32).reshape(u.shape[0], u.shape[1], SSM_GROUPS, SSM_GROUP)
        return jnp.einsum('btgh,gph->btgp', ug, bbar_r), jnp.einsum('btgh,gph->btgp', ug, bbar_i)

    def readout(s_r, s_i):
        y = jnp.einsum('btgp,ghp->btgh', s_r, cr) - jnp.einsum('btgp,ghp->btgh', s_i, ci)
        return y.reshape(y.shape[0], y.shape[1], -1)

    uc_r, uc_i = drive(uc)
    _, _, sc_r, sc_i = complex_diag_scan(lam_r, lam_i, uc_r, uc_i, reverse)
    end = 0 if reverse else -1
    s0_r, s0_i = sc_r[:, end][:, None], sc_i[:, end][:, None]
    ux_r, ux_i = drive(ux)
    pw_r, pw_i, sx_r, sx_i = complex_diag_scan(lam_r, lam_i, ux_r, ux_i, reverse)
    sx_r = sx_r + pw_r * s0_r - pw_i * s0_i
    sx_i = sx_i + pw_r * s0_i + pw_i * s0_r
    y_ctx = readout(sc_r, sc_i) if need_ctx else None
    return readout(sx_r, sx_i), y_ctx


def s5_glu_mixer(ux, uc, a_re, a_im, log_dt, b_re, b_im, c_re, c_im, d_skip, w_glu, b_glu, need_ctx):
    d = d_skip.astype(jnp.float32)
    y_x = d * ux.astype(jnp.float32)
    y_c = d * uc.astype(jnp.float32) if need_ctx else None
    for direction in range(2):
        yx_d, yc_d = s5_direction(ux, uc, a_re[direction], a_im[direction], log_dt[direction],
                                  b_re[direction], b_im[direction], c_re[direction], c_im[direction],
                                  direction == 1, need_ctx)
        y_x = y_x + yx_d
        if need_ctx:
            y_c = y_c + yc_d

    def glu(y, dtype):
        z = jax.nn.gelu(y).astype(dtype) @ w_glu + b_glu
        val, gate = jnp.split(z, 2, axis=-1)
        return val * jax.nn.sigmoid(gate)

    return glu(y_x, ux.dtype), (glu(y_c, uc.dtype) if need_ctx else None)


def setup_inputs(seed: int = 0) -> dict:
    key = jax.random.key(seed)
    keys = iter(jax.random.split(key, 32))
    f32 = jnp.float32

    def normal(shape, scale):
        return jax.random.normal(next(keys), shape, f32) * scale

    d, g, h, p = D_MODEL, SSM_GROUPS, SSM_GROUP, SSM_STATE
    n_attn, n_ssm = (DEPTH + 1) // 2, DEPTH // 2
    a_im0 = math.pi * jnp.arange(p, dtype=f32)
    return {
        'x': normal((BATCH, SEQ, d), 1.0),
        'c': normal((BATCH, d), 1.0),
        'ctx': normal((BATCH, CTX_LEN, d), 1.0),
        'c_ctx': normal((d,), 1.0),
        'ada_w': normal((DEPTH, d, 6 * d), 0.5 * d ** -0.5),
        'ada_b': normal((DEPTH, 6 * d), 0.01),
        'norm_mix': 1.0 + normal((DEPTH, d), 0.02),
        'norm_ffn': 1.0 + normal((DEPTH, d), 0.02),
        'ffn_w1': normal((DEPTH, d, D_FF), d ** -0.5),
        'ffn_w3': normal((DEPTH, d, D_FF), d ** -0.5),
        'ffn_w2': normal((DEPTH, D_FF, d), D_FF ** -0.5),
        'attn_w_in': normal((n_attn, d, IN_WIDTH), d ** -0.5),
        'attn_w_out': normal((n_attn, MIX_WIDTH, d), MIX_WIDTH ** -0.5),
        'attn_rpb': normal((n_attn, NA_HEADS, 2 * NA_ROWS - 1, 2 * NA_COLS - 1), 0.02),
        'attn_sink': normal((n_attn, NB_Q_HEADS), 0.5),
        'ssm_a_re': -0.5 + normal((n_ssm, 2, g, p), 0.01),
        'ssm_a_im': a_im0 + normal((n_ssm, 2, g, p), 0.01),
        'ssm_log_dt': jax.random.uniform(next(keys), (n_ssm, 2, g), f32, math.log(1e-3), math.log(1e-1)),
        'ssm_b_re': normal((n_ssm, 2, g, p, h), (2 * h) ** -0.5),
        'ssm_b_im': normal((n_ssm, 2, g, p, h), (2 * h) ** -0.5),
        'ssm_c_re': normal((n_ssm, 2, g, h, p), p ** -0.5),
        'ssm_c_im': normal((n_ssm, 2, g, h, p), p ** -0.5),
        'ssm_d': normal((n_ssm, d), 0.5),
        'ssm_w_glu': normal((n_ssm, d, 2 * d), d ** -0.5),
        'ssm_b_glu': normal((n_ssm, 2 * d), 0.01),
        'norm_final': 1.0 + normal((d,), 0.02),
    }


def reference(x, c, ctx, c_ctx, ada_w, ada_b, norm_mix, norm_ffn, ffn_w1, ffn_w3, ffn_w2,
              attn_w_in, attn_w_out, attn_rpb, attn_sink,
              ssm_a_re, ssm_a_im, ssm_log_dt, ssm_b_re, ssm_b_im, ssm_c_re, ssm_c_im,
              ssm_d, ssm_w_glu, ssm_b_glu, norm_final):
    seq = x.shape[1]
    t = jnp.arange(seq)
    pos_r = (t // GRID_W).astype(jnp.float32)
    pos_c = (t % GRID_W).astype(jnp.float32)
    for layer in range(DEPTH):
        need_ctx = layer < DEPTH - 1
        i = layer // 2
        sh1, sc1, g1, sh2, sc2, g2 = [m[:, None, :] for m in ada_mod(c, ada_w[layer], ada_b[layer])]
        csh1, csc1, cg1, csh2, csc2, cg2 = ada_mod(c_ctx, ada_w[layer], ada_b[layer])
        hx = rms_norm(x, norm_mix[layer]) * (1.0 + sc1) + sh1
        hc = rms_norm(ctx, norm_mix[layer]) * (1.0 + csc1) + csh1
        if layer % 2 == 0:
            yx, yc = hybrid_attention(hx, hc, attn_w_in[i], attn_w_out[i], attn_rpb[i], attn_sink[i],
                                      pos_r, pos_c, need_ctx)
        else:
            yx, yc = s5_glu_mixer(hx, hc, ssm_a_re[i], ssm_a_im[i], ssm_log_dt[i], ssm_b_re[i], ssm_b_im[i],
                                  ssm_c_re[i], ssm_c_im[i], ssm_d[i], ssm_w_glu[i], ssm_b_glu[i], need_ctx)
        x = x + g1 * yx
        x = x + g2 * swiglu(rms_norm(x, norm_ffn[layer]) * (1.0 + sc2) + sh2,
                            ffn_w1[layer], ffn_w3[layer], ffn_w2[layer])
        if need_ctx:
            ctx = ctx + cg1 * yc
            ctx = ctx + cg2 * swiglu(rms_norm(ctx, norm_ffn[layer]) * (1.0 + csc2) + csh2,
                                     ffn_w1[layer], ffn_w3[layer], ffn_w2[layer])
    return rms_norm(x, norm_final)
```

```python
import contextlib
import numpy as np
import concourse.bass as bass
import concourse.mybir as mybir

F32 = mybir.dt.float32
BF16 = mybir.dt.bfloat16
AF = mybir.ActivationFunctionType
ALU = mybir.AluOpType
AX = mybir.AxisListType

ENGS = ("pe", "act", "dve", "pool", "sp")
SEM_LIMIT = 30000
NDMA_SLOTS = 8


class Buf:
    __slots__ = ("name", "w", "readers", "excl")

    def __init__(self, name):
        self.name = name
        self.excl = False
        self.w = None
        self.readers = {}


class Prog:
    def __init__(self, strict_same_engine=True):
        self.nc = bass.Bass("TRN2", target_bir_lowering=False)
        self.es = contextlib.ExitStack()
        self.q = {e: [] for e in ENGS}
        self.strict = strict_same_engine
        self.sems = {}
        self.epoch = {e: 0 for e in ENGS}
        self.cnt = {e: 0 for e in ENGS}
        self.seen = {e: {} for e in ENGS}
        self.dma_slot = {e: 0 for e in ENGS}
        self.dma_val = {}
        self.n_instr = 0
        self._uid = 0

    def sem(self, key):
        if key not in self.sems:
            self.sems[key] = getattr(self, "sem_es", self.es).enter_context(self.nc.semaphore("s_" + "_".join(str(x) for x in key)))
        return self.sems[key]

    def sbuf(self, name, shape, dtype):
        self._uid += 1
        return self.es.enter_context(self.nc.sbuf_tensor("%s_u%d" % (name, self._uid), list(shape), dtype))

    def psum(self, name, shape, dtype):
        return self.es.enter_context(self.nc.psum_tensor(name, list(shape), dtype))

    def dram(self, name, shape, dtype, kind="Internal"):
        return self.nc.dram_tensor(name, list(shape), dtype, kind=kind)

    def buf(self, name=None):
        self._uid += 1
        return Buf(name or "b%d" % self._uid)

    def _deps(self, eng, reads, writes):
        deps = {}

        def add(d):
            if d is None:
                return
            k, v = d[0], d[1]
            if deps.get(k, 0) < v:
                deps[k] = v
        for b in reads:
            add(b.w)
        for b in writes:
            add(b.w)
            for r in b.readers.values():
                add(r)
        out = []
        for k, v in deps.items():
            if k[0] == "e" and k[1] == eng and (eng == "pe" or not self.strict):
                continue
            if self.seen[eng].get(k, 0) >= v:
                continue
            self.seen[eng][k] = v
            out.append((k, v))
        return out

    def op(self, eng, fn, reads=(), writes=()):
        xs = [b for b in reads if b.excl]
        if xs:
            writes = list(writes) + [b for b in xs if b not in writes]
        waits = self._deps(eng, reads, writes)
        if self.cnt[eng] >= SEM_LIMIT:
            self.epoch[eng] += 1
            self.cnt[eng] = 0
        key = ("e", eng, self.epoch[eng])
        self.sem(key)
        self.cnt[eng] += 1
        val = self.cnt[eng]
        self.q[eng].append((waits, fn, key, 1))
        for b in reads:
            b.readers[eng] = (key, val)
        for b in writes:
            b.w = (key, val, eng)
            b.readers = {}
        self.n_instr += 1

    def dma(self, eng, out, in_, reads=(), writes=(), **kw):
        slot = self.dma_slot[eng]
        self.dma_slot[eng] = (slot + 1) % NDMA_SLOTS
        key = ("d", eng, slot)
        self.sem(key)
        waits = self._deps(eng, reads, writes)
        prev = self.dma_val.get(key, 0)
        if prev > 0 and self.seen[eng].get(key, 0) < prev:
            self.seen[eng][key] = prev
            waits.append((key, prev))
        val = prev + 16
        self.dma_val[key] = val

        def fn(e, out=out, in_=in_, kw=kw):
            return e.dma_start(out=out, in_=in_, **kw)
        self.q[eng].append((waits, fn, key, 16))
        for b in reads:
            b.readers["dma_" + eng + str(slot)] = (key, val)
        for b in writes:
            b.w = (key, val, "dma")
            b.readers = {}
        self.n_instr += 1

    def flush(self):
        for key, val in self.dma_val.items():
            e = key[1]
            if self.seen[e].get(key, 0) < val:
                self.seen[e][key] = val
                self.q[e].append(([(key, val)], None, None, 0))
        nc = self.nc
        engmap = {"pe": "tensor", "act": "scalar", "dve": "vector", "pool": "gpsimd", "sp": "sync"}
        if any(self.q[e] for e in ENGS):
            with nc.Block() as block:
                for e in ENGS:
                    items = self.q[e]
                    if not items:
                        continue

                    def body(h, items=items):
                        for waits, fn, key, inc in items:
                            for k, v in waits:
                                h.wait_ge(self.sems[k], v)
                            if fn is not None:
                                ins = fn(h)
                                ins.then_inc(self.sems[key], inc)
                    getattr(block, engmap[e])(body)
        self.q = {e: [] for e in ENGS}

    @contextlib.contextmanager
    def scope(self):
        outer = self.es
        self.es = contextlib.ExitStack()
        self.sem_es = getattr(self, "sem_es", outer)
        try:
            yield
            self.flush()
        finally:
            self.es.close()
            self.es = outer

    def finish_all(self):
        self.flush()
        self.sem_es.close() if hasattr(self, "sem_es") else None
        self.es.close()
        return self.nc

    def finish(self, final_bufs=()):
        for e in ENGS:
            waits = self._deps(e, list(final_bufs), [])
            if waits:
                self.q[e].append((waits, None, None, 0))
        for key, val in self.dma_val.items():
            e = key[1]
            if self.seen[e].get(key, 0) < val:
                self.seen[e][key] = val
                self.q[e].append(([(key, val)], None, None, 0))
        nc = self.nc
        engmap = {"pe": "tensor", "act": "scalar", "dve": "vector", "pool": "gpsimd", "sp": "sync"}
        with nc.Block() as block:
            for e in ENGS:
                items = self.q[e]
                if not items:
                    continue

                def body(h, items=items):
                    for waits, fn, key, inc in items:
                        for k, v in waits:
                            h.wait_ge(self.sems[k], v)
                        if fn is not None:
                            ins = fn(h)
                            ins.then_inc(self.sems[key], inc)
                getattr(block, engmap[e])(body)
        self.es.close()
        return nc


D = 2048
KC = D // 128
DFF = 5632
NFF = DFF // 128
EPS = 1e-6
WUNIT = 11264


def load_fm(P, dst, dst_buf, vec_ap, eng="sp"):
    P.dma(eng, dst, vec_ap.rearrange("(k p) -> p k", p=128), writes=[dst_buf],
          allow_slow_non_contiguous=True)


class Ctx:
    def __init__(self, P, ident_ap):
        self.P = P
        nc = P.nc
        self.ident = P.sbuf("ident_sb", [128, 128], F32)
        self.ident_b = P.buf("ident")
        P.dma("sp", self.ident[:], ident_ap, writes=[self.ident_b])
        self.ps = P.psum("ps", [128, 8 * 512], F32)
        self.psb = [P.buf("psb%d" % i) for i in range(8)]
        for b in self.psb:
            b.excl = True
        self.wu = None
        self.wu_gen = 0
        self.rr_n = 4

    def bank(self, i):
        return self.ps[:, i * 512:(i + 1) * 512]

    def rr4(self):
        i = getattr(self, "_rr", 0) % self.rr_n
        self._rr = (i + 1) % self.rr_n
        return i

    def alloc_wu(self, n=3):
        P = self.P
        self.wu_gen += 1
        self.wu = [P.sbuf("wu%d_%d" % (self.wu_gen, i), [128, WUNIT], BF16) for i in range(n)]
        self.wub = [P.buf("wu%d" % i) for i in range(n)]
        self.wu_next = 0

    def next_wu(self):
        i = self.wu_next
        self.wu_next = (i + 1) % len(self.wu)
        return self.wu[i], self.wub[i]


def emit_modprep(P, C, name, gw_ap, sc_ap, sh_ap, gate_ap, ncls):
    gw = P.sbuf(name + "_gw", [128, KC], F32)
    gwb = P.buf()
    load_fm(P, gw[:], gwb, gw_ap)
    A, B, G = [], [], []
    for c in range(ncls):
        a = P.sbuf("%s_A%d" % (name, c), [128, KC], F32)
        ab = P.buf()
        load_fm(P, a[:], ab, sc_ap[c])
        P.op("dve", lambda e, a=a: e.tensor_scalar(out=a[:], in0=a[:], scalar1=1.0, scalar2=None, op0=ALU.add),
             reads=[ab], writes=[ab])
        P.op("dve", lambda e, a=a: e.tensor_tensor(out=a[:], in0=a[:], in1=gw[:], op=ALU.mult),
             reads=[ab, gwb], writes=[ab])
        b = P.sbuf("%s_B%d" % (name, c), [128, KC], F32)
        bb = P.buf()
        load_fm(P, b[:], bb, sh_ap[c])
        g = None
        gb = None
        if gate_ap is not None:
            g = P.sbuf("%s_G%d" % (name, c), [128, D], F32)
            gb = P.buf()
            P.dma("sp", g[:], gate_ap[c].partition_broadcast(128), writes=[gb])
        A.append((a, ab))
        B.append((b, bb))
        G.append((g, gb))
    return A, B, G


def emit_norm_T(P, C, xt, xtb, hT, hTb, col0, A, B, scr):
    ssq, rstd, xn, sb = scr
    P.op("act", lambda e: e.activation(out=xn[:], in_=xt, func=AF.Square, accum_out=ssq[:]),
         reads=[xtb], writes=[sb])
    P.op("dve", lambda e: e.tensor_scalar(out=rstd[:], in0=ssq[:], scalar1=1.0 / D, scalar2=EPS,
                                          op0=ALU.mult, op1=ALU.add), reads=[sb], writes=[sb])
    P.op("act", lambda e: e.activation(out=rstd[:], in_=rstd[:], func=AF.Sqrt), reads=[sb], writes=[sb])
    P.op("dve", lambda e: e.reciprocal(out=rstd[:], in_=rstd[:]), reads=[sb], writes=[sb])
    P.op("act", lambda e: e.activation(out=xn[:], in_=xt, func=AF.Copy, scale=rstd[:]),
         reads=[xtb, sb], writes=[sb])
    a, ab = A
    b, bb = B
    for q in range(KC // 4):
        bank = 6 + (q % 2)
        for kk in range(4):
            k = q * 4 + kk
            P.op("pe", lambda e, k=k, kk=kk, bank=bank: e.transpose(
                out=C.ps[:, bank * 512 + kk * 128: bank * 512 + (kk + 1) * 128],
                in_=xn[:, k * 128:(k + 1) * 128], identity=C.ident[:]),
                reads=[sb, C.ident_b], writes=[C.psb[bank]])
        for kk in range(4):
            k = q * 4 + kk
            P.op("dve", lambda e, k=k, kk=kk, bank=bank: e.tensor_scalar(
                out=hT[:, k, col0:col0 + 128],
                in0=C.ps[:, bank * 512 + kk * 128: bank * 512 + (kk + 1) * 128],
                scalar1=a[:, k:k + 1], scalar2=b[:, k:k + 1], op0=ALU.mult, op1=ALU.add),
                reads=[C.psb[bank], ab, bb], writes=[hTb])


def emit_ffn(P, C, name, xin, xout, T, cls_of_tile, A, B, G, w1, w3, w2):
    ntile = T // 128
    xt = [P.sbuf("%s_xt%d" % (name, i), [128, D], F32) for i in range(4)]
    xtb = [P.buf() for i in range(4)]
    hT = P.sbuf(name + "_hT", [128, KC, 512], BF16)
    hTb = P.buf()
    gT = P.sbuf(name + "_gT", [128, NFF, 512], BF16)
    gTb = [P.buf() for j in range(NFF)]
    ssq = P.sbuf(name + "_ssq", [128, 1], F32)
    rstd = P.sbuf(name + "_rstd", [128, 1], F32)
    xn = P.sbuf(name + "_xn", [128, D], F32)
    scr = (ssq, rstd, xn, P.buf())
    su = [P.sbuf("%s_su%d" % (name, i), [128, 512], F32) for i in range(2)]
    sub = [P.buf() for i in range(2)]
    ot = [P.sbuf("%s_ot%d" % (name, i), [128, 256], F32) for i in range(2)]
    otb = [P.buf() for i in range(2)]
    it = 0
    dn = 0
    for t0 in range(0, ntile, 4):
        nt = min(4, ntile - t0)
        ntok = nt * 128
        for i in range(nt):
            P.dma("sp", xt[i][:], xin[(t0 + i) * 128:(t0 + i + 1) * 128, :], writes=[xtb[i]])
            c = cls_of_tile[t0 + i]
            emit_norm_T(P, C, xt[i][:], xtb[i], hT, hTb, i * 128, A[c], B[c], scr)
        for jg in range(NFF // 4):
            w1u, w1b = C.next_wu()
            w3u, w3b = C.next_wu()
            w1v = w1u[:, 0:KC * 512].rearrange("p (k c) -> p k c", k=KC)
            w3v = w3u[:, 0:KC * 512].rearrange("p (k c) -> p k c", k=KC)
            P.dma("pool", w1v, w1[:, jg * 512:(jg + 1) * 512].rearrange("(k p) c -> p k c", p=128), writes=[w1b])
            P.dma("pool", w3v, w3[:, jg * 512:(jg + 1) * 512].rearrange("(k p) c -> p k c", p=128), writes=[w3b])
            for jj in range(4):
                j = jg * 4 + jj
                bu = it % 2
                bv = 2 + it % 2
                for k in range(KC):
                    P.op("pe", lambda e, k=k, jj=jj, bu=bu, w1v=w1v, ntok=ntok: e.matmul(
                        C.ps[:, bu * 512: bu * 512 + ntok], lhsT=w1v[:, k, jj * 128:(jj + 1) * 128],
                        rhs=hT[:, k, 0:ntok], start=(k == 0), stop=(k == KC - 1)),
                        reads=[w1b, hTb], writes=[C.psb[bu]])
                for k in range(KC):
                    P.op("pe", lambda e, k=k, jj=jj, bv=bv, w3v=w3v, ntok=ntok: e.matmul(
                        C.ps[:, bv * 512: bv * 512 + ntok], lhsT=w3v[:, k, jj * 128:(jj + 1) * 128],
                        rhs=hT[:, k, 0:ntok], start=(k == 0), stop=(k == KC - 1)),
                        reads=[w3b, hTb], writes=[C.psb[bv]])
                s = su[it % 2]
                P.op("act", lambda e, s=s, bu=bu, ntok=ntok: e.activation(out=s[:, 0:ntok], in_=C.ps[:, bu * 512: bu * 512 + ntok],
                                                               func=AF.Silu),
                     reads=[C.psb[bu]], writes=[sub[it % 2]])
                P.op("dve", lambda e, s=s, bv=bv, j=j, ntok=ntok: e.tensor_tensor(
                    out=gT[:, j, 0:ntok], in0=s[:, 0:ntok], in1=C.ps[:, bv * 512: bv * 512 + ntok], op=ALU.mult),
                    reads=[sub[it % 2], C.psb[bv]], writes=[gTb[j]])
                it += 1
        for cb in range(D // 256):
            w2u, w2b = C.next_wu()
            w2v = w2u[:, 0:NFF * 256].rearrange("p (j c) -> p j c", j=NFF)
            P.dma("pool", w2v, w2[:, cb * 256:(cb + 1) * 256].rearrange("(j p) c -> p j c", p=128), writes=[w2b])
            for i in range(nt):
                bank = 4 + dn % 2
                for j in range(NFF):
                    P.op("pe", lambda e, j=j, i=i, bank=bank, w2v=w2v: e.matmul(
                        C.ps[:, bank * 512: bank * 512 + 256], lhsT=gT[:, j, i * 128:(i + 1) * 128],
                        rhs=w2v[:, j, :], start=(j == 0), stop=(j == NFF - 1)),
                        reads=[w2b, gTb[j]], writes=[C.psb[bank]])
                c = cls_of_tile[t0 + i]
                g, gb = G[c]
                o = ot[dn % 2]
                ob = otb[dn % 2]
                P.op("dve", lambda e, o=o, bank=bank, g=g, cb=cb: e.tensor_tensor(
                    out=o[:], in0=C.ps[:, bank * 512: bank * 512 + 256], in1=g[:, cb * 256:(cb + 1) * 256], op=ALU.mult),
                    reads=[C.psb[bank], gb], writes=[ob])
                P.op("pool", lambda e, o=o, i=i, cb=cb: e.tensor_tensor(
                    out=xt[i][:, cb * 256:(cb + 1) * 256], in0=o[:], in1=xt[i][:, cb * 256:(cb + 1) * 256], op=ALU.add),
                    reads=[ob, xtb[i]], writes=[xtb[i]])
                dn += 1
        for i in range(nt):
            P.dma("sp", xout[(t0 + i) * 128:(t0 + i + 1) * 128, :], xt[i][:], reads=[xtb[i]], writes=[P.outb])


NKT = 14
NQT = 10
SCALE = 128 ** -0.5


def q_hcol(qi):
    return (qi + 2) * 128 if qi < 8 else 1536 + (qi - 8) * 128


def emit_norm_all(P, C, srcs, hT, hTb, A, B):
    xt = [P.sbuf("nx%d" % i, [128, D], F32) for i in range(2)]
    xtb = [P.buf() for i in range(2)]
    ssq = P.sbuf("n_ssq", [128, 1], F32)
    rstd = P.sbuf("n_rstd", [128, 1], F32)
    xn = P.sbuf("n_xn", [128, D], F32)
    scr = (ssq, rstd, xn, P.buf())
    for n, (ap, c, col0) in enumerate(srcs):
        P.dma("sp", xt[n % 2][:], ap, writes=[xtb[n % 2]])
        emit_norm_T(P, C, xt[n % 2][:], xtb[n % 2], hT, hTb, col0, A[c], B[c], scr)


def emit_proj_fm(P, C, dst, dstb, wv, wb, wc0, hT, hTb, ranges, evac="act"):
    for (d0, h0, n) in ranges:
        bank = C.rr4()
        for k in range(KC):
            P.op("pe", lambda e, k=k, bank=bank, h0=h0, n=n: e.matmul(
                C.ps[:, bank * 512: bank * 512 + n], lhsT=wv[:, k, wc0:wc0 + 128],
                rhs=hT[:, k, h0:h0 + n], start=(k == 0), stop=(k == KC - 1)),
                reads=[wb, hTb], writes=[C.psb[bank]])
        P.op(evac, lambda e, bank=bank, d0=d0, n=n: (e.activation(out=dst[:, d0:d0 + n], in_=C.ps[:, bank * 512: bank * 512 + n], func=AF.Copy)
                                                     if evac == "act" else
                                                     e.tensor_copy(out=dst[:, d0:d0 + n], in_=C.ps[:, bank * 512: bank * 512 + n])),
             reads=[C.psb[bank]], writes=[dstb])


def emit_proj_v(P, C, Vh, Vhb, wv, wb, wc0, hT, hTb):
    for g0 in range(0, NKT, 4):
        n = min(4, NKT - g0)
        bank = C.rr4()
        for j in range(n):
            kt = g0 + j
            for k in range(KC):
                P.op("pe", lambda e, k=k, kt=kt, j=j, bank=bank: e.matmul(
                    C.ps[:, bank * 512 + j * 128: bank * 512 + (j + 1) * 128],
                    lhsT=hT[:, k, kt * 128:(kt + 1) * 128], rhs=wv[:, k, wc0:wc0 + 128],
                    start=(k == 0), stop=(k == KC - 1)),
                    reads=[wb, hTb], writes=[C.psb[bank]])
        P.op("dve", lambda e, g0=g0, n=n, bank=bank: e.tensor_copy(
            out=Vh[:, g0:g0 + n, :], in_=C.ps[:, bank * 512: bank * 512 + n * 128].rearrange("p (j d) -> p j d", j=n)),
            reads=[C.psb[bank]], writes=[Vhb])


def emit_attn_core(P, C, S, hidx, qT, qTb, kT, kTb, Vh, Vhb, slots_of, tab, tabb, esink_col, oT_d, oTd_b):
    for qi in range(NQT):
        kts, toff, ntab = slots_of(qi)
        ns = len(kts)
        it = S["it"]
        S["it"] += 1
        b0 = 4 + 2 * (it % 2)
        ob = it % 2
        pe_sb = S["pe_sb"][it % 2]
        pe_b = S["pe_b"][it % 2]
        pm = S["pm"][it % 2]
        pm_b = S["pm_b"][it % 2]
        for si, kt in enumerate(kts):
            bank = b0 + si // 4
            off = bank * 512 + (si % 4) * 128
            P.op("pe", lambda e, kt=kt, off=off, qi=qi: e.matmul(
                C.ps[:, off:off + 128], lhsT=kT[:, kt * 128:(kt + 1) * 128], rhs=qT[:, qi * 128:(qi + 1) * 128],
                start=True, stop=True), reads=[kTb, qTb], writes=[C.psb[bank]])
        for half in range((ns + 3) // 4):
            n = min(4, ns - half * 4)
            bank = b0 + half
            P.op("act", lambda e, half=half, n=n, bank=bank, pe_sb=pe_sb: e.activation(
                out=pe_sb[:, half * 512: half * 512 + n * 128], in_=C.ps[:, bank * 512: bank * 512 + n * 128],
                func=AF.Exp, scale=SCALE), reads=[C.psb[bank]], writes=[pe_b])
        if toff is not None:
            P.op("dve", lambda e, toff=toff, ntab=ntab, pe_sb=pe_sb, pm=pm: e.tensor_tensor(
                out=pm[:, 0:ntab * 128], in0=pe_sb[:, 0:ntab * 128], in1=tab[:, toff:toff + ntab * 128], op=ALU.mult),
                reads=[pe_b, tabb], writes=[pm_b])
        else:
            ntab = 0
        otb = S["otb"][ob]
        denb = S["denb"][ob]
        for si, kt in enumerate(kts):
            src, srcb = (pm, pm_b) if si < ntab else (pe_sb, pe_b)
            P.op("pe", lambda e, kt=kt, si=si, src=src, ob=ob, ns=ns: e.matmul(
                S["ot"][:, ob * 512: ob * 512 + 128], lhsT=Vh[:, kt, :], rhs=src[:, si * 128:(si + 1) * 128],
                start=(si == 0), stop=(si == ns - 1)), reads=[Vhb, srcb], writes=[otb])
        for si, kt in enumerate(kts):
            src, srcb = (pm, pm_b) if si < ntab else (pe_sb, pe_b)
            P.op("pe", lambda e, si=si, src=src, ob=ob, ns=ns: e.matmul(
                S["ot"][:, ob * 512 + 128: ob * 512 + 256], lhsT=S["ones"][:], rhs=src[:, si * 128:(si + 1) * 128],
                start=(si == 0), stop=(si == ns - 1)), reads=[srcb, S["ones_b"]], writes=[denb])
        rec = S["rec"][it % 2]
        recb = S["rec_b"][it % 2]
        if esink_col is not None:
            P.op("dve", lambda e, rec=rec, ob=ob: e.tensor_scalar(
                out=rec[:], in0=S["ot"][:, ob * 512 + 128: ob * 512 + 256], scalar1=esink_col, scalar2=None, op0=ALU.add),
                reads=[denb, S["esink_b"]], writes=[recb])
            P.op("dve", lambda e, rec=rec: e.reciprocal(out=rec[:], in_=rec[:]), reads=[recb], writes=[recb])
        else:
            P.op("dve", lambda e, rec=rec, ob=ob: e.reciprocal(out=rec[:], in_=S["ot"][:, ob * 512 + 128: ob * 512 + 256]),
                 reads=[denb], writes=[recb])
        oh = S["oh"]
        P.op("dve", lambda e, rec=rec, ob=ob, qi=qi: e.tensor_tensor(
            out=oh[:, qi * 128:(qi + 1) * 128], in0=S["ot"][:, ob * 512: ob * 512 + 128], in1=rec[:], op=ALU.mult),
            reads=[otb, recb], writes=[S["oh_b"]])
    P.dma("sp", oT_d[hidx * 128:(hidx + 1) * 128, :], S["oh"][:], reads=[S["oh_b"]], writes=[oTd_b])


def emit_attn(P, C, I):
    hT = P.sbuf("a_hT", [128, KC, NKT * 128], BF16)
    hTb = P.buf()
    A, B, _ = emit_modprep(P, C, "am", I["gw"], I["sc"], I["sh"], None, 2)
    with P.scope():
        srcs = [(I["xh"][t * 128:(t + 1) * 128, :], 0, t * 128) for t in range(12)]
        srcs += [(I["ctx"][t * 128:(t + 1) * 128, :], 1, 1536 + t * 128) for t in range(2)]
        emit_norm_all(P, C, srcs, hT, hTb, A, B)
    oTd_b = P.buf("oTd")
    with P.scope():
        S = {"it": 0}
        S["ot"] = C.ps[:, 2 * 512: 4 * 512]
        S["otb"] = [C.psb[2], C.psb[3]]
        S["denb"] = [C.psb[2], C.psb[3]]
        C.rr_n = 2
        S["pe_sb"] = [P.sbuf("a_pe%d" % i, [128, 1024], BF16) for i in range(2)]
        S["pe_b"] = [P.buf() for i in range(2)]
        S["pm"] = [P.sbuf("a_pm%d" % i, [128, 640], BF16) for i in range(2)]
        S["pm_b"] = [P.buf() for i in range(2)]
        S["rec"] = [P.sbuf("a_rec%d" % i, [128, 128], F32) for i in range(2)]
        S["rec_b"] = [P.buf() for i in range(2)]
        S["oh"] = P.sbuf("a_oh", [128, NQT * 128], BF16)
        S["oh_b"] = P.buf()
        S["ones"] = P.sbuf("a_ones", [128, 128], BF16)
        S["ones_b"] = P.buf()
        P.op("dve", lambda e: e.memset(S["ones"][:], 1.0), writes=[S["ones_b"]])
        esink = P.sbuf("a_esink", [128, 8], F32)
        S["esink_b"] = P.buf()
        P.dma("sp", esink[:], I["sink"].partition_broadcast(128), writes=[S["esink_b"]])
        P.op("act", lambda e: e.activation(out=esink[:], in_=esink[:], func=AF.Exp), reads=[S["esink_b"]], writes=[S["esink_b"]])
        qT = P.sbuf("a_qT", [128, NQT * 128], BF16)
        qTb = P.buf()
        kT = P.sbuf("a_kT", [128, NKT * 128], BF16)
        kTb = P.buf()
        Vh = P.sbuf("a_Vh", [128, NKT, 128], BF16)
        Vhb = P.buf()
        nabf = P.sbuf("a_nabf", [128, 25 * 128], F32)
        nabfb = P.buf()
        Eh = P.sbuf("a_Eh", [128, 25 * 128], BF16)
        Ehb = P.buf()
        swm = P.sbuf("a_swm", [128, 9 * 128], BF16)
        swmb = P.buf()
        P.dma("pool", swm[:], I["swm"], writes=[swmb])
        q_ranges = [(0, 256, 512), (512, 768, 512), (1024, 1536, 256)]
        k_ranges = [(0, 0, 512), (512, 512, 512), (1024, 1024, 512), (1536, 1536, 256)]
        w_in = I["w_in"]

        def load_unit(c0):
            u, ub = C.next_wu()
            v = u[:, 0:KC * 512].rearrange("p (k c) -> p k c", k=KC)
            P.dma("pool", v, w_in[:, c0:c0 + 512].rearrange("(k p) c -> p k c", p=128), writes=[ub])
            return v, ub

        def cls_of(qi):
            return {0: 0, 1: 1, 6: 3, 7: 4}.get(qi, 2)

        def na_slots(qi):
            if qi < 8:
                return [qi + s for s in range(5)] + [12, 13], cls_of(qi) * 640, 5
            return [12, 13], None, 0

        def sw_slots(qi):
            if qi < 8:
                c = 0 if qi == 0 else (2 if qi == 7 else 1)
                return [qi + 1 + s for s in range(3)] + [12, 13], c * 384, 3
            return [12, 13], None, 0

        for hg in range(2):
            wq, wqb = load_unit(hg * 512)
            wk, wkb = load_unit(1024 + hg * 512)
            wvv, wvb = load_unit(2048 + hg * 512)
            for hh in range(4):
                h = hg * 4 + hh
                P.dma("sp", nabf[:], I["nab"][h], writes=[nabfb])
                P.op("act", lambda e: e.activation(out=Eh[:], in_=nabf[:], func=AF.Exp), reads=[nabfb], writes=[Ehb])
                emit_proj_fm(P, C, qT, qTb, wq, wqb, hh * 128, hT, hTb, q_ranges)
                emit_proj_fm(P, C, kT, kTb, wk, wkb, hh * 128, hT, hTb, k_ranges)
                emit_proj_v(P, C, Vh, Vhb, wvv, wvb, hh * 128, hT, hTb)
                emit_attn_core(P, C, S, h, qT, qTb, kT, kTb, Vh, Vhb, na_slots, Eh, Ehb, None, I["oT_d"], oTd_b)
        cosT = P.sbuf("a_cos", [128, 1536], F32)
        sinT = P.sbuf("a_sin", [128, 1536], F32)
        ropeb = P.buf()
        P.dma("sp", cosT[:], I["ropec"], writes=[ropeb])
        P.dma("sp", sinT[:], I["ropes"], writes=[ropeb])
        rperm = P.sbuf("a_rperm", [128, 128], F32)
        rpb_ = P.buf()
        P.dma("sp", rperm[:], I["rperm"], writes=[rpb_])
        qf = P.sbuf("a_qf", [128, 1792], F32)
        qfb = P.buf()
        t1 = P.sbuf("a_t1", [128, 512], F32)
        t1b = P.buf()
        t2 = P.sbuf("a_t2", [128, 512], F32)
        t2b = P.buf()

        def rope(dst, dstb, ncols_rot, tcol0, ncols_all):
            for c0 in range(0, ncols_rot, 512):
                n = min(512, ncols_rot - c0)
                bank = C.rr4()
                P.op("pe", lambda e, c0=c0, n=n, bank=bank: e.matmul(
                    C.ps[:, bank * 512: bank * 512 + n], lhsT=rperm[:], rhs=qf[:, c0:c0 + n], start=True, stop=True),
                    reads=[rpb_, qfb], writes=[C.psb[bank]])
                P.op("dve", lambda e, c0=c0, n=n: e.tensor_tensor(
                    out=t1[:, 0:n], in0=qf[:, c0:c0 + n], in1=cosT[:, tcol0 + c0: tcol0 + c0 + n], op=ALU.mult),
                    reads=[qfb, ropeb], writes=[t1b])
                P.op("dve", lambda e, c0=c0, n=n, bank=bank: e.tensor_tensor(
                    out=t2[:, 0:n], in0=C.ps[:, bank * 512: bank * 512 + n], in1=sinT[:, tcol0 + c0: tcol0 + c0 + n], op=ALU.mult),
                    reads=[C.psb[bank], ropeb], writes=[t2b])
                P.op("pool", lambda e, c0=c0, n=n: e.tensor_tensor(
                    out=dst[:, c0:c0 + n], in0=t1[:, 0:n], in1=t2[:, 0:n], op=ALU.add),
                    reads=[t1b, t2b], writes=[dstb])
            if ncols_all > ncols_rot:
                P.op("act", lambda e: e.activation(out=dst[:, ncols_rot:ncols_all], in_=qf[:, ncols_rot:ncols_all], func=AF.Copy),
                     reads=[qfb], writes=[dstb])

        wkv, wkvb = load_unit(4096)
        for g in range(2):
            wq, wqb = load_unit(3072 + g * 512)
            emit_proj_fm(P, C, qf, qfb, wkv, wkvb, g * 128, hT, hTb, k_ranges)
            rope(kT, kTb, 1536, 0, 1792)
            emit_proj_v(P, C, Vh, Vhb, wkv, wkvb, 256 + g * 128, hT, hTb)
            for hh in range(4):
                hq = g * 4 + hh
                emit_proj_fm(P, C, qf, qfb, wq, wqb, hh * 128, hT, hTb, q_ranges)
                rope(qT, qTb, 1024, 256, 1280)
                emit_attn_core(P, C, S, 8 + hq, qT, qTb, kT, kTb, Vh, Vhb, sw_slots, swm, swmb,
                               esink[:, hq:hq + 1], I["oT_d"], oTd_b)
    C.rr_n = 4
    return oTd_b


def emit_outproj(P, C, oT_d, oTd_b, xsrc_of, x1out, outb, w_out, G, ntile, cls_of_tile, name="op"):
    ncol = ntile * 128
    oT = P.sbuf(name + "_oT", [128, KC, ncol], BF16)
    oTb = P.buf()
    P.dma("sp", oT[:], oT_d.rearrange("(k p) t -> p k t", p=128), reads=[oTd_b], writes=[oTb])
    xs = [P.sbuf("%s_xs%d" % (name, i), [128, 512], F32) for i in range(2)]
    xsb = [P.buf() for i in range(2)]
    tm = [P.sbuf("%s_tm%d" % (name, i), [128, 512], F32) for i in range(2)]
    tmb = [P.buf() for i in range(2)]
    it = 0
    for cb in range(D // 512):
        u, ub = C.next_wu()
        wv = u[:, 0:KC * 512].rearrange("p (k c) -> p k c", k=KC)
        P.dma("pool", wv, w_out[:, cb * 512:(cb + 1) * 512].rearrange("(k p) c -> p k c", p=128), writes=[ub])
        for t in range(ntile):
            bank = C.rr4()
            for k in range(KC):
                P.op("pe", lambda e, k=k, t=t, bank=bank, wv=wv: e.matmul(
                    C.ps[:, bank * 512:(bank + 1) * 512], lhsT=oT[:, k, t * 128:(t + 1) * 128], rhs=wv[:, k, :],
                    start=(k == 0), stop=(k == KC - 1)), reads=[oTb, ub], writes=[C.psb[bank]])
            g, gb = G[cls_of_tile[t]]
            x_, xb_ = xs[it % 2], xsb[it % 2]
            t_, tb_ = tm[it % 2], tmb[it % 2]
            P.dma("sp", x_[:], xsrc_of(t)[:, cb * 512:(cb + 1) * 512], writes=[xb_])
            P.op("dve", lambda e, t_=t_, bank=bank, g=g, cb=cb: e.tensor_tensor(
                out=t_[:], in0=C.ps[:, bank * 512:(bank + 1) * 512], in1=g[:, cb * 512:(cb + 1) * 512], op=ALU.mult),
                reads=[C.psb[bank], gb], writes=[tb_])
            P.op("pool", lambda e, t_=t_, x_=x_: e.tensor_tensor(out=x_[:], in0=t_[:], in1=x_[:], op=ALU.add),
                 reads=[tb_, xb_], writes=[xb_])
            P.dma("sp", x1out[t * 128:(t + 1) * 128, cb * 512:(cb + 1) * 512], x_[:], reads=[xb_], writes=[outb])
            it += 1


GRID_W = 64


def host_consts():
    ident = np.eye(128, dtype=np.float32)
    rperm = np.zeros((128, 128), np.float32)
    for dp in range(128):
        partner = dp + 32 if (dp % 64) < 32 else dp - 32
        rperm[partner, dp] = 1.0
    return ident, rperm


def host_na_index(core):
    idx = np.full((128, 5, 5, 128), 465, np.int64)
    k = np.arange(128)
    q = np.arange(128)
    for ci, i in enumerate([0, 1, 3, 6, 7]):
        m = 8 * core + i
        for s in range(5):
            e = i + s
            Pk = 8 * core - 2 + e
            if core == 0 and e == 0:
                Pk = 3
            if core == 7 and e == 11:
                Pk = 60
            if Pk < 0 or Pk > 63:
                continue
            kr = (2 * Pk + k // 64)[:, None]
            kc = (k % 64)[:, None]
            qr = (2 * m + q // 64)[None, :]
            qc = (q % 64)[None, :]
            kr0 = np.clip(qr - 4, 0, 120)
            ws = np.clip(qc - 8, 0, 48)
            valid = (kr >= kr0) & (kr < kr0 + 8) & (kc >= ws) & (kc < ws + 16)
            ridx = kr - qr + 7
            cidx = np.clip(kc - qc + 15, 0, 30)
            flat = ridx * 31 + cidx
            idx[:, ci, s, :] = np.where(valid, flat, 465)
    return idx


def host_sw_mask(core):
    out = np.zeros((128, 3, 3, 128), np.float32)
    k = np.arange(128)[:, None]
    q = np.arange(128)[None, :]
    for ci, i in enumerate([0, 3, 7]):
        m = 8 * core + i
        for s in range(3):
            kp = (m - 1 + s) * 128 + k
            qp = m * 128 + q
            valid = (np.abs(kp - qp) <= 128) & (kp >= 0) & (kp < 8192)
            out[:, ci, s, :] = valid
    return out


def host_rope(core):
    tok = (8 * core - 2) * 128 + np.arange(1536)
    pos_r = (tok // GRID_W).astype(np.float32)
    pos_c = (tok % GRID_W).astype(np.float32)
    inv = (np.float32(10000.0) ** (-np.arange(32, dtype=np.float32) / np.float32(32))).astype(np.float32)
    cosT = np.zeros((128, 1536), np.float32)
    sinT = np.zeros((128, 1536), np.float32)
    for d in range(128):
        pos = pos_r if d < 64 else pos_c
        ang = (pos * inv[d % 32]).astype(np.float32)
        cosT[d] = np.cos(ang)
        sn = np.sin(ang)
        sinT[d] = -sn if (d % 64) < 32 else sn
    return cosT, sinT


def host_xh(x2d, core):
    out = np.zeros((1536, x2d.shape[1]), np.float32)
    for e in range(12):
        Pk = 8 * core - 2 + e
        if core == 0 and e == 0:
            Pk = 3
        if core == 7 and e == 11:
            Pk = 60
        if 0 <= Pk <= 63:
            out[e * 128:(e + 1) * 128] = x2d[Pk * 128:(Pk + 1) * 128]
    return out


GS = 128
NB = 32
NCH = 128
NCC = 32
I32 = mybir.dt.int32
PI = float(np.pi)


def s5_tt(P, eng, out, a, b, op, bufs):
    P.op(eng, lambda e: e.tensor_tensor(out=out, in0=a, in1=b, op=op), reads=bufs, writes=bufs[:1])


def s5_ts(P, eng, out, a, s1, op0, bufs, s2=None, op1=None):
    if op1 is None:
        P.op(eng, lambda e: e.tensor_scalar(out=out, in0=a, scalar1=s1, scalar2=None, op0=op0), reads=bufs, writes=bufs[:1])
    else:
        P.op(eng, lambda e: e.tensor_scalar(out=out, in0=a, scalar1=s1, scalar2=s2, op0=op0, op1=op1), reads=bufs, writes=bufs[:1])


def s5_cmul(P, eng, outr, outi, xr, xi, yr, yi, t1, t2, bufs, neg_i=False):
    s5_tt(P, eng, t1, xr, yr, ALU.mult, bufs)
    s5_tt(P, eng, t2, xi, yi, ALU.mult, bufs)
    s5_tt(P, eng, outr, t1, t2, ALU.subtract, bufs)
    s5_tt(P, eng, t1, xr, yi, ALU.mult, bufs)
    s5_tt(P, eng, t2, xi, yr, ALU.mult, bufs)
    if neg_i:
        s5_tt(P, eng, t1, t1, t2, ALU.add, bufs)
        s5_ts(P, eng, outi, t1, -1.0, ALU.mult, bufs)
    else:
        s5_tt(P, eng, outi, t1, t2, ALU.add, bufs)


def s5_horner(P, out, x, coefs, tmp, bufs):
    n = len(coefs) - 1
    s5_ts(P, "dve", out, x, float(coefs[n]), ALU.mult, bufs, s2=float(coefs[n - 1]), op1=ALU.add)
    for k in range(n - 2, -1, -1):
        s5_tt(P, "dve", tmp, out, x, ALU.mult, bufs)
        s5_ts(P, "dve", out, tmp, float(coefs[k]), ALU.add, bufs)


def s5_exp_small(P, out, x, tmp, bufs, sign=1.0):
    import math
    co = [sign ** k / math.factorial(k) for k in range(9)]
    s5_horner(P, out, x, co, tmp, bufs)


def s5_exp(P, out, x, y, tmp, bufs):
    import math
    s5_ts(P, "dve", y, x, 0.125, ALU.mult, bufs)
    co = [1.0 / math.factorial(k) for k in range(15)]
    s5_horner(P, out, y, co, tmp, bufs)
    for _ in range(3):
        s5_tt(P, "dve", out, out, out, ALU.mult, bufs)


def s5_sincos(P, sn, cs, th, ni, q, r, m, tmp, bufs):
    import math
    C1 = 6.28125
    C2 = 2 * math.pi - C1
    s5_ts(P, "dve", q, th, 1.0 / (2 * PI), ALU.mult, bufs)
    P.op("dve", lambda e: e.tensor_copy(out=ni, in_=q), reads=bufs, writes=bufs[:1])
    P.op("dve", lambda e: e.tensor_copy(out=q, in_=ni), reads=bufs, writes=bufs[:1])
    P.op("dve", lambda e: e.scalar_tensor_tensor(out=r, in0=q, scalar=-C1, in1=th, op0=ALU.mult, op1=ALU.add), reads=bufs, writes=bufs[:1])
    P.op("dve", lambda e: e.scalar_tensor_tensor(out=r, in0=q, scalar=-C2, in1=r, op0=ALU.mult, op1=ALU.add), reads=bufs, writes=bufs[:1])
    for thr, op, add in ((PI, ALU.is_gt, -1.0), (-PI, ALU.is_lt, 1.0)):
        s5_ts(P, "dve", m, r, thr, op, bufs)
        P.op("dve", lambda e, add=add: e.scalar_tensor_tensor(out=r, in0=m, scalar=add * C1, in1=r, op0=ALU.mult, op1=ALU.add),
             reads=bufs, writes=bufs[:1])
        P.op("dve", lambda e, add=add: e.scalar_tensor_tensor(out=r, in0=m, scalar=add * C2, in1=r, op0=ALU.mult, op1=ALU.add),
             reads=bufs, writes=bufs[:1])
    s5_ts(P, "dve", r, r, 0.25, ALU.mult, bufs)
    s5_tt(P, "dve", q, r, r, ALU.mult, bufs)
    sco = [(-1.0) ** k / math.factorial(2 * k + 1) for k in range(7)]
    cco = [(-1.0) ** k / math.factorial(2 * k) for k in range(8)]
    s5_horner(P, sn, q, sco, tmp, bufs)
    s5_tt(P, "dve", sn, sn, r, ALU.mult, bufs)
    s5_horner(P, cs, q, cco, tmp, bufs)
    for _ in range(2):
        s5_tt(P, "dve", tmp, sn, cs, ALU.mult, bufs)
        s5_tt(P, "dve", m, sn, sn, ALU.mult, bufs)
        s5_ts(P, "dve", sn, tmp, 2.0, ALU.mult, bufs)
        s5_ts(P, "dve", cs, m, -2.0, ALU.mult, bufs, s2=1.0, op1=ALU.add)


def emit_s5_params(P, C, I):
    pb = P.buf("s5p")
    B1 = [pb]

    def T(name, shape=(128, GS), dt=F32):
        return P.sbuf("s5_" + name, list(shape), dt)
    PERS = {}
    for nm, shp in (("bbr", (128, GS, 16)), ("bbi", (128, GS, 16)), ("ctr", (128, GS, 16)), ("cti", (128, GS, 16)),
                    ("pwAr", (128, GS, 9)), ("pwAi", (128, GS, 9)), ("pwBr", (128, GS, 9)), ("pwBi", (128, GS, 9)),
                    ("sBr", (128, GS)), ("sBi", (128, GS)), ("sCr", (128, GS)), ("sCi", (128, GS)),
                    ("L8r", (128, GS)), ("L8i", (128, GS)), ("A2", (128, 2, GS)), ("B2", (128, 2, GS))):
        PERS[nm] = T(nm, shp)
    with P.scope():
        return _emit_s5_params_inner(P, C, I, PERS, pb, B1, T)


def _emit_s5_params_inner(P, C, I, PERS, pb, B1, T0):
    def T(name, shape=(128, GS), dt=F32):
        if name in PERS:
            return PERS[name]
        return T0(name, shape, dt)
    are, aim, dt_ = T("are"), T("aim"), T("dt")
    an = T("anat")
    for src_, dst_ in ((I["a_re"], are), (I["a_im"], aim)):
        P.dma("sp", an[:].rearrange("g (d p) -> g d p", d=2), src_.rearrange("d g p -> g d p"), reads=B1, writes=B1)
        bank = C.rr4()
        P.op("pe", lambda e, bank=bank: e.transpose(out=C.ps[:, bank * 512: bank * 512 + 128], in_=an[:], identity=C.ident[:]),
             reads=B1 + [C.ident_b], writes=[C.psb[bank]])
        P.op("act", lambda e, dst_=dst_, bank=bank: e.activation(out=dst_[:], in_=C.ps[:, bank * 512: bank * 512 + 128], func=AF.Copy),
             reads=[C.psb[bank]], writes=B1)
    for d in range(2):
        sl = slice(d * 64, (d + 1) * 64)
        P.dma("sp", dt_[sl, :], I["log_dt"][d].partition_broadcast(64), writes=B1)
    q, r, m, tq = T("q"), T("r"), T("m"), T("tq")
    ldt = T("ldt")
    P.op("dve", lambda e: e.tensor_copy(out=ldt[:], in_=dt_[:]), reads=B1, writes=B1)
    s5_exp(P, dt_[:], ldt[:], q[:], tq[:], B1)
    ardt, th = T("ardt"), T("th")
    s5_tt(P, "dve", ardt[:], are[:], dt_[:], ALU.mult, B1)
    s5_tt(P, "dve", th[:], aim[:], dt_[:], ALU.mult, B1)
    mag, magi = T("mag"), T("magi")
    s5_exp_small(P, mag[:], ardt[:], tq[:], B1)
    s5_exp_small(P, magi[:], ardt[:], tq[:], B1, sign=-1.0)
    ni = T("ni", dt=I32)
    sn, cs = T("sn"), T("cs")
    s5_sincos(P, sn[:], cs[:], th[:], ni[:], q[:], r[:], m[:], tq[:], B1)
    lr, li, vr, vi = T("lr"), T("li"), T("vr"), T("vi")
    s5_tt(P, "dve", lr[:], mag[:], cs[:], ALU.mult, B1)
    s5_tt(P, "dve", li[:], mag[:], sn[:], ALU.mult, B1)
    s5_tt(P, "dve", vr[:], magi[:], cs[:], ALU.mult, B1)
    s5_tt(P, "dve", vi[:], magi[:], sn[:], ALU.mult, B1)
    s5_ts(P, "dve", vi[:], vi[:], -1.0, ALU.mult, B1)
    nr, den, cr_, ci_ = T("nr"), T("den"), T("cfr"), T("cfi")
    s5_ts(P, "dve", nr[:], lr[:], -1.0, ALU.add, B1)
    s5_tt(P, "dve", den[:], are[:], are[:], ALU.mult, B1)
    s5_tt(P, "dve", q[:], aim[:], aim[:], ALU.mult, B1)
    s5_tt(P, "dve", den[:], den[:], q[:], ALU.add, B1)
    P.op("dve", lambda e: e.reciprocal(out=den[:], in_=den[:]), reads=B1, writes=B1)
    s5_tt(P, "dve", cr_[:], nr[:], are[:], ALU.mult, B1)
    s5_tt(P, "dve", q[:], li[:], aim[:], ALU.mult, B1)
    s5_tt(P, "dve", cr_[:], cr_[:], q[:], ALU.add, B1)
    s5_tt(P, "dve", cr_[:], cr_[:], den[:], ALU.mult, B1)
    s5_tt(P, "dve", ci_[:], li[:], are[:], ALU.mult, B1)
    s5_tt(P, "dve", q[:], nr[:], aim[:], ALU.mult, B1)
    s5_tt(P, "dve", ci_[:], ci_[:], q[:], ALU.subtract, B1)
    s5_tt(P, "dve", ci_[:], ci_[:], den[:], ALU.mult, B1)
    br, bi = T("br", (128, GS, 16)), T("bi", (128, GS, 16))
    for d in range(2):
        sl = slice(d * 64, (d + 1) * 64)
        for gq in range(8):
            gs_ = slice(gq * 16, (gq + 1) * 16)
            P.dma("sp", br[sl, gs_, :], I["b_re"][d, gs_].rearrange("g p h -> p g h"), writes=B1)
            P.dma("sp", bi[sl, gs_, :], I["b_im"][d, gs_].rearrange("g p h -> p g h"), writes=B1)
    bbr, bbi = T("bbr", (128, GS, 16)), T("bbi", (128, GS, 16))
    t1, t2 = T("t1", (128, GS, 16)), T("t2", (128, GS, 16))
    cfr_b = cr_[:].unsqueeze(2).broadcast_to([128, GS, 16])
    cfi_b = ci_[:].unsqueeze(2).broadcast_to([128, GS, 16])
    s5_cmul(P, "dve", bbr[:], bbi[:], cfr_b, cfi_b, br[:], bi[:], t1[:], t2[:], B1)
    ctr, cti = T("ctr", (128, GS, 16)), T("cti", (128, GS, 16))
    cn = [T("cn%d" % i, (128, 128)) for i in range(2)]
    cnb = [P.buf() for i in range(2)]
    n = 0
    for src, dst in ((I["c_re"], ctr), (I["c_im"], cti)):
        for o in range(GS // 8):
            c_, cb_ = cn[n % 2], cnb[n % 2]
            P.dma("sp", c_[:].rearrange("q (d p) -> q d p", d=2),
                  src[:, o * 8:(o + 1) * 8].rearrange("d g h p -> (g h) d p"), writes=[cb_])
            bank = C.rr4()
            P.op("pe", lambda e, c_=c_, bank=bank: e.transpose(out=C.ps[:, bank * 512: bank * 512 + 128], in_=c_[:], identity=C.ident[:]),
                 reads=[cb_, C.ident_b], writes=[C.psb[bank]])
            P.op("act", lambda e, dst=dst, o=o, bank=bank: e.activation(
                out=dst[:, o * 8:(o + 1) * 8, :].rearrange("p g h -> p (g h)"), in_=C.ps[:, bank * 512: bank * 512 + 128], func=AF.Copy),
                reads=[C.psb[bank]], writes=B1)
            n += 1
    bAr, bAi, bBr, bBi = T("bAr"), T("bAi"), T("bBr"), T("bBi")
    lo, hi = slice(0, 64), slice(64, 128)
    for dst, a, b in ((bAr, vr, lr), (bAi, vi, li), (bBr, lr, vr), (bBi, li, vi)):
        P.op("dve", lambda e, dst=dst, a=a: e.tensor_copy(out=dst[lo, :], in_=a[lo, :]), reads=B1, writes=B1)
        P.op("dve", lambda e, dst=dst, b=b: e.tensor_copy(out=dst[hi, :], in_=b[hi, :]), reads=B1, writes=B1)
    pw = {}
    for nm, (xr, xi) in (("A", (bAr, bAi)), ("B", (bBr, bBi))):
        pr, pi_ = PERS["pw%sr" % nm], PERS["pw%si" % nm]
        P.op("dve", lambda e, pr=pr: e.memset(pr[:, :, 0], 1.0), reads=B1, writes=B1)
        P.op("dve", lambda e, pi_=pi_: e.memset(pi_[:, :, 0], 0.0), reads=B1, writes=B1)
        for k in range(1, 9):
            s5_cmul(P, "dve", pr[:, :, k], pi_[:, :, k], pr[:, :, k - 1], pi_[:, :, k - 1], xr[:], xi[:], q[:], m[:], B1)
        pw[nm] = (pr, pi_)
    sBr, sBi, sCr, sCi, L8r, L8i = T("sBr"), T("sBi"), T("sCr"), T("sCi"), T("L8r"), T("L8i")
    pAr, pAi = pw["A"]
    pBr, pBi = pw["B"]
    cp = lambda dst, sl, src: P.op("dve", lambda e: e.tensor_copy(out=dst[sl, :], in_=src), reads=B1, writes=B1)
    cp(sBr, lo, pBr[lo, :, 7]); cp(sBi, lo, pBi[lo, :, 7])
    P.op("dve", lambda e: e.memset(sBr[hi, :], 1.0), reads=B1, writes=B1)
    P.op("dve", lambda e: e.memset(sBi[hi, :], 0.0), reads=B1, writes=B1)
    cp(sCr, lo, pBr[lo, :, 1]); cp(sCi, lo, pBi[lo, :, 1])
    cp(sCr, hi, pAr[hi, :, 8]); cp(sCi, hi, pAi[hi, :, 8])
    cp(L8r, lo, pBr[lo, :, 8]); cp(L8i, lo, pBi[lo, :, 8])
    cp(L8r, hi, pAr[hi, :, 8]); cp(L8i, hi, pAi[hi, :, 8])
    A2, B2 = T("A2", (128, 2, GS)), T("B2", (128, 2, GS))
    for j in range(2):
        P.op("dve", lambda e, j=j: e.tensor_copy(out=A2[:, j, :], in_=L8r[:]), reads=B1, writes=B1)
    P.op("dve", lambda e: e.tensor_copy(out=B2[:, 1, :], in_=L8i[:]), reads=B1, writes=B1)
    s5_ts(P, "dve", B2[:, 0, :], L8i[:], -1.0, ALU.mult, B1)
    return dict(pb=pb, bbr=bbr, bbi=bbi, ctr=ctr, cti=cti, pw=pw, sBr=sBr, sBi=sBi, sCr=sCr, sCi=sCi,
                L8r=L8r, L8i=L8i, A2=A2, B2=B2)


def s5_recur(P, eng, psl, W, Wb, nsteps, ascending, A2v, B2v, pb, init, tA, tB, tb_):
    order = range(nsteps) if ascending else range(nsteps - 1, -1, -1)
    prev = init
    for c in order:
        cur = W[psl, c]
        if prev is not None:
            P.op(eng, lambda e, prev=prev: e.tensor_tensor(out=tA[psl], in0=prev[:, 0:2, :], in1=A2v[psl], op=ALU.mult),
                 reads=[tb_, Wb, pb], writes=[tb_])
            P.op(eng, lambda e, prev=prev: e.tensor_tensor(out=tB[psl], in0=prev[:, 1:3, :], in1=B2v[psl], op=ALU.mult),
                 reads=[tb_, Wb, pb], writes=[tb_])
            P.op(eng, lambda e: e.tensor_tensor(out=tA[psl], in0=tA[psl], in1=tB[psl], op=ALU.add), reads=[tb_], writes=[tb_])
            P.op(eng, lambda e, cur=cur: e.tensor_tensor(out=cur[:, 0:2, :], in0=cur[:, 0:2, :], in1=tA[psl], op=ALU.add),
                 reads=[tb_, Wb], writes=[Wb])
        P.op(eng, lambda e, cur=cur: e.tensor_copy(out=cur[:, 2, :], in_=cur[:, 0, :]), reads=[Wb], writes=[Wb])
        prev = cur


def emit_s5a(P, C, I, hTd, hTdb):
    Q = emit_s5_params(P, C, I)
    pb = Q["pb"]
    lo, hi = slice(0, 64), slice(64, 128)
    outb = P.buf("s5a_out")
    P.dma("sp", I["L8d"][:, 0], Q["A2"][:], reads=[pb], writes=[outb])
    P.dma("sp", I["L8d"][:, 1], Q["B2"][:], reads=[pb], writes=[outb])
    sel = P.sbuf("s5_sel", [128, 64, 128], BF16)
    selb = P.buf()
    P.dma("pool", sel[:], I["sel"], writes=[selb])
    mf = P.sbuf("s5_mf", [128, 128], F32)
    mb_ = P.sbuf("s5_mb", [128, 128], F32)
    dv = P.sbuf("s5_dv", [128, GS], F32)
    cb_ = P.buf()
    P.dma("sp", mf[:], I["maskf"], writes=[cb_])
    P.dma("sp", mb_[:], I["maskb"], writes=[cb_])
    for j in range(8):
        P.dma("sp", dv[j * 16:(j + 1) * 16, :], I["ssm_d"].rearrange("(g h) -> h g", h=16), writes=[cb_], allow_slow_non_contiguous=True)
    identb = P.sbuf("s5_identb", [128, 128], BF16)
    identbb = P.buf()
    P.op("act", lambda e: e.activation(out=identb[:], in_=C.ident[:], func=AF.Copy), reads=[C.ident_b], writes=[identbb])
    MS = []
    for i in range(2):
        d_ = {nm: P.sbuf("s5_%s%d" % (nm, i), [128, 8, 128], BF16) for nm in ("BLr", "BLi", "CLr", "nCLi", "BcTr", "BcTi")}
        d_["Cc"] = P.sbuf("s5_Cc%d" % i, [128, 2, 8, 128], BF16)
        d_["b"] = P.buf()
        MS.append(d_)
    f32t = {nm: P.sbuf("s5_f_" + nm, [128, 8, 128], F32) for nm in ("Xr", "Xi", "t1", "t2")}
    fb = P.buf()
    W = P.sbuf("s5_W", [128, NCH, 3, NB], F32)
    Wb = [P.buf(), P.buf()]
    Wc = P.sbuf("s5_Wc", [128, NCC, 3, NB], F32)
    Wcb = [P.buf(), P.buf()]
    tA = [P.sbuf("s5_tA%d" % i, [128, 2, NB], F32) for i in range(2)]
    tB = [P.sbuf("s5_tB%d" % i, [128, 2, NB], F32) for i in range(2)]
    tb_ = [P.buf() for i in range(2)]
    Eo = P.sbuf("s5_Eo", [128, 2, GS], F32)
    Ec = P.sbuf("s5_Ec", [128, 2, GS], F32)
    Eb = [P.buf(), P.buf()]
    Tg = [P.sbuf("s5_Tg%d" % i, [128, 128], BF16) for i in range(2)]
    Tgb = [P.buf() for i in range(2)]
    tT = [P.sbuf("s5_tT%d" % i, [128, 128], F32) for i in range(2)]
    tTb = P.buf()
    Bc = [P.sbuf("s5_Bc%d" % i, [128, 2, 128], BF16) for i in range(2)]
    Bcb = [P.buf() for i in range(2)]
    Ug = [P.sbuf("s5_Ug%d" % i, [128, NCH + NCC], BF16) for i in range(2)]
    Ugb = [P.buf() for i in range(2)]
    Yi = [P.sbuf("s5_Yi%d" % i, [128, 128], F32) for i in range(2)]
    Yib = [P.buf() for i in range(2)]
    pAr, pAi = Q["pw"]["A"]
    pBr, pBi = Q["pw"]["B"]
    C.rr_n = 4
    hk = [P.sbuf("s5_hk%d" % i, [128, NCH * 8 + NCC * 8], BF16) for i in range(2)]
    hkb = [P.buf() for i in range(2)]
    import os as _os
    for o in range(int(_os.environ.get("S5_MAXOCT", GS // 8))):
        M = MS[o % 2]
        Mb = M["b"]
        g0 = o * 8
        P.dma("sp", hk[o % 2][:], hTd[o], reads=[hTdb], writes=[hkb[o % 2]])
        eng = "dve"
        bufs = [fb, pb, Mb]
        bc4 = lambda ap: ap.unsqueeze(2).broadcast_to([128, 8, 8, 16])
        pw4 = lambda ap: ap.unsqueeze(3).broadcast_to([128, 8, 8, 16])
        v4 = lambda t: t[:].rearrange("p g (j h) -> p g j h", j=8)
        sc3 = lambda ap: ap.unsqueeze(2).broadcast_to([128, 8, 128])
        s5_cmul(P, eng, v4(f32t["Xr"]), v4(f32t["Xi"]), bc4(Q["bbr"][:, g0:g0 + 8, :]), bc4(Q["bbi"][:, g0:g0 + 8, :]),
                pw4(pAr[:, g0:g0 + 8, 0:8]), pw4(pAi[:, g0:g0 + 8, 0:8]), v4(f32t["t1"]), v4(f32t["t2"]), bufs)
        P.op("act", lambda e, M=M: e.activation(out=M["BLr"][:], in_=f32t["Xr"][:], func=AF.Copy), reads=[fb], writes=[Mb])
        P.op("act", lambda e, M=M: e.activation(out=M["BLi"][:], in_=f32t["Xi"][:], func=AF.Copy), reads=[fb], writes=[Mb])
        s5_tt(P, eng, f32t["t1"][:], f32t["Xr"][:], sc3(Q["sBr"][:, g0:g0 + 8]), ALU.mult, bufs)
        s5_tt(P, eng, f32t["t2"][:], f32t["Xi"][:], sc3(Q["sBi"][:, g0:g0 + 8]), ALU.mult, bufs)
        P.op(eng, lambda e, M=M: e.tensor_tensor(out=M["BcTr"][:], in0=f32t["t1"][:], in1=f32t["t2"][:], op=ALU.subtract),
             reads=[fb], writes=[Mb])
        s5_tt(P, eng, f32t["t1"][:], f32t["Xr"][:], sc3(Q["sBi"][:, g0:g0 + 8]), ALU.mult, bufs)
        s5_tt(P, eng, f32t["t2"][:], f32t["Xi"][:], sc3(Q["sBr"][:, g0:g0 + 8]), ALU.mult, bufs)
        P.op(eng, lambda e, M=M: e.tensor_tensor(out=M["BcTi"][:], in0=f32t["t1"][:], in1=f32t["t2"][:], op=ALU.add),
             reads=[fb], writes=[Mb])
        s5_cmul(P, eng, v4(f32t["Xr"]), v4(f32t["Xi"]), bc4(Q["ctr"][:, g0:g0 + 8, :]), bc4(Q["cti"][:, g0:g0 + 8, :]),
                pw4(pBr[:, g0:g0 + 8, 0:8]), pw4(pBi[:, g0:g0 + 8, 0:8]), v4(f32t["t1"]), v4(f32t["t2"]), bufs)
        P.op("act", lambda e, M=M: e.activation(out=M["CLr"][:], in_=f32t["Xr"][:], func=AF.Copy), reads=[fb], writes=[Mb])
        P.op("act", lambda e, M=M: e.activation(out=M["nCLi"][:], in_=f32t["Xi"][:], func=AF.Copy, scale=-1.0), reads=[fb], writes=[Mb])
        s5_tt(P, eng, f32t["t1"][:], f32t["Xr"][:], sc3(Q["sCr"][:, g0:g0 + 8]), ALU.mult, bufs)
        s5_tt(P, eng, f32t["t2"][:], f32t["Xi"][:], sc3(Q["sCi"][:, g0:g0 + 8]), ALU.mult, bufs)
        P.op(eng, lambda e, M=M: e.tensor_tensor(out=M["Cc"][:, 0], in0=f32t["t1"][:], in1=f32t["t2"][:], op=ALU.subtract),
             reads=[fb], writes=[Mb])
        s5_tt(P, eng, f32t["t1"][:], f32t["Xr"][:], sc3(Q["sCi"][:, g0:g0 + 8]), ALU.mult, bufs)
        s5_tt(P, eng, f32t["t2"][:], f32t["Xi"][:], sc3(Q["sCr"][:, g0:g0 + 8]), ALU.mult, bufs)
        s5_tt(P, eng, f32t["t1"][:], f32t["t1"][:], f32t["t2"][:], ALU.add, bufs)
        P.op(eng, lambda e, M=M: e.tensor_scalar(out=M["Cc"][:, 1], in0=f32t["t1"][:], scalar1=-1.0, scalar2=None, op0=ALU.mult),
             reads=[fb], writes=[Mb])
        P.dma("sp", I["Ccd"][o], M["Cc"][:].rearrange("p a g m -> p (a g m)"), reads=[Mb], writes=[outb])
        SKIP = _os.environ.get("S5_SKIP", "")
        ONLY = _os.environ.get("S5_ONLY", "TBUYV")
        for gg in range(8 if "G" not in SKIP else 0):
            g = g0 + gg
            gl = g % NB
            par = g % 2
            if "T" in ONLY:
                tbanks = (4, 5)
                for half, sl in enumerate((lo, hi)):
                    bank = tbanks[half]
                    P.op("pe", lambda e, M=M, gg=gg, sl=sl, bank=bank: e.matmul(
                        C.ps[:, bank * 512: bank * 512 + 128], lhsT=M["BLr"][sl, gg, :], rhs=M["CLr"][sl, gg, :], start=True, stop=False),
                        reads=[Mb], writes=[C.psb[bank]])
                    P.op("pe", lambda e, M=M, gg=gg, sl=sl, bank=bank: e.matmul(
                        C.ps[:, bank * 512: bank * 512 + 128], lhsT=M["BLi"][sl, gg, :], rhs=M["nCLi"][sl, gg, :], start=False, stop=True),
                        reads=[Mb], writes=[C.psb[bank]])
                P.op("dve", lambda e: e.tensor_tensor(out=tT[0][:], in0=C.ps[:, 4 * 512: 4 * 512 + 128], in1=mf[:], op=ALU.mult),
                     reads=[C.psb[4], cb_], writes=[tTb])
                P.op("dve", lambda e: e.tensor_tensor(out=tT[1][:], in0=C.ps[:, 5 * 512: 5 * 512 + 128], in1=mb_[:], op=ALU.mult),
                     reads=[C.psb[5], cb_], writes=[tTb])
                P.op("dve", lambda e: e.tensor_tensor(out=tT[0][:], in0=tT[0][:], in1=tT[1][:], op=ALU.add), reads=[tTb], writes=[tTb])
                P.op("dve", lambda e, g=g, par=par: e.scalar_tensor_tensor(out=Tg[par][:], in0=C.ident[:], scalar=dv[:, g:g + 1], in1=tT[0][:],
                                                                      op0=ALU.mult, op1=ALU.add),
                     reads=[tTb, cb_, C.ident_b], writes=[Tgb[par]])
            if "B" in ONLY:
                bank = C.rr4()
                for a_, nm in enumerate(("BcTr", "BcTi")):
                    P.op("pe", lambda e, M=M, nm=nm, gg=gg, a_=a_, bank=bank: e.transpose(
                        out=C.ps[:, bank * 512 + a_ * 64: bank * 512 + a_ * 64 + 64].bitcast(BF16), in_=M[nm][:, gg, :], identity=identb[:]),
                        reads=[Mb, identbb], writes=[C.psb[bank]])
                P.op("act", lambda e, par=par, bank=bank: e.activation(
                    out=Bc[par][:].rearrange("p a m -> p (a m)"), in_=C.ps[:, bank * 512: bank * 512 + 128].bitcast(BF16), func=AF.Copy),
                    reads=[C.psb[bank]], writes=[Bcb[par]])
            if "U" in ONLY:
                bank = C.rr4()
                hv = hk[o % 2][:].rearrange("p (c j) -> p j c", j=8)
                hTb = hkb[o % 2]
                for j in range(8):
                    P.op("pe", lambda e, j=j, gg=gg, bank=bank, hv=hv: e.matmul(
                        C.ps[:, bank * 512: bank * 512 + NCH + NCC], lhsT=sel[:, gg * 8 + j, :], rhs=hv[:, j, :], start=(j == 0), stop=(j == 7)),
                        reads=[selb, hTb], writes=[C.psb[bank]])
                P.op("act", lambda e, par=par, bank=bank: e.activation(out=Ug[par][:], in_=C.ps[:, bank * 512: bank * 512 + NCH + NCC], func=AF.Copy),
                     reads=[C.psb[bank]], writes=[Ugb[par]])
            if "Y" in ONLY:
                bank = C.rr4()
                P.op("pe", lambda e, par=par, bank=bank: e.matmul(C.ps[:, bank * 512: bank * 512 + NCH], lhsT=Tg[par][:], rhs=Ug[par][:, 0:NCH],
                                                                start=True, stop=True), reads=[Tgb[par], Ugb[par]], writes=[C.psb[bank]])
                P.op("act", lambda e, par=par, bank=bank: e.activation(out=Yi[par][:], in_=C.ps[:, bank * 512: bank * 512 + NCH], func=AF.Copy),
                     reads=[C.psb[bank]], writes=[Yib[par]])
                P.dma("sp", I["Yd"][g], Yi[par][:], reads=[Yib[par]], writes=[outb])
            if "V" in ONLY:
                for a_ in range(2):
                    bank = C.rr4()
                    P.op("pe", lambda e, par=par, a_=a_, bank=bank: e.matmul(C.ps[:, bank * 512: bank * 512 + NCH + NCC], lhsT=Bc[par][:, a_, :],
                                                                           rhs=Ug[par][:], start=True, stop=True),
                         reads=[Bcb[par], Ugb[par]], writes=[C.psb[bank]])
                    P.op("dve", lambda e, a_=a_, gl=gl, bank=bank: e.tensor_copy(out=W[:, :, a_, gl], in_=C.ps[:, bank * 512: bank * 512 + NCH]),
                         reads=[C.psb[bank]], writes=Wb)
                    P.op("act", lambda e, a_=a_, gl=gl, bank=bank: e.activation(out=Wc[:, :, a_, gl], in_=C.ps[:, bank * 512 + NCH: bank * 512 + NCH + NCC],
                                                                              func=AF.Copy),
                         reads=[C.psb[bank]], writes=Wcb)
            if gl == NB - 1:
                bi_ = g // NB
                gb0 = bi_ * NB
                P.dma("sp", I["Vd"][bi_], W[:].rearrange("p c s g -> p (c s g)"), reads=Wb, writes=[outb])
                A2v = Q["A2"][:, :, gb0:gb0 + NB]
                B2v = Q["B2"][:, :, gb0:gb0 + NB]
                s5_recur(P, "dve", lo, W, Wb[0], NCH, True, A2v, B2v, pb, None, tA[0], tB[0], tb_[0])
                s5_recur(P, "pool", hi, W, Wb[1], NCH, False, A2v, B2v, pb, None, tA[1], tB[1], tb_[1])
                s5_recur(P, "dve", lo, Wc, Wcb[0], NCC, True, A2v, B2v, pb, None, tA[0], tB[0], tb_[0])
                s5_recur(P, "pool", hi, Wc, Wcb[1], NCC, False, A2v, B2v, pb, None, tA[1], tB[1], tb_[1])
                P.op("dve", lambda e, gb0=gb0: e.tensor_copy(out=Eo[lo, :, gb0:gb0 + NB], in_=W[lo, NCH - 1, 0:2, :]), reads=[Wb[0]], writes=[Eb[0]])
                P.op("pool", lambda e, gb0=gb0: e.tensor_copy(out=Eo[hi, :, gb0:gb0 + NB], in_=W[hi, 0, 0:2, :]), reads=[Wb[1]], writes=[Eb[1]])
                P.op("dve", lambda e, gb0=gb0: e.tensor_copy(out=Ec[lo, :, gb0:gb0 + NB], in_=Wc[lo, NCC - 1, 0:2, :]), reads=[Wcb[0]], writes=[Eb[0]])
                P.op("pool", lambda e, gb0=gb0: e.tensor_copy(out=Ec[hi, :, gb0:gb0 + NB], in_=Wc[hi, 0, 0:2, :]), reads=[Wcb[1]], writes=[Eb[1]])
    P.dma("sp", I["Eown"], Eo[:].rearrange("p a g -> p (a g)"), reads=Eb, writes=[outb])
    P.dma("sp", I["Ectx"], Ec[:].rearrange("p a g -> p (a g)"), reads=Eb, writes=[outb])
    return outb


def host_s5_consts():
    sel = np.zeros((128, 64, 128), np.float32)
    for gm in range(8):
        for j in range(8):
            for h in range(16):
                sel[16 * gm + h, gm * 8 + j, j * 16 + h] = 1.0
    selT = np.ascontiguousarray(sel.transpose(2, 1, 0))
    jj = np.arange(128) // 16
    maskf = (jj[None, :] >= jj[:, None]).astype(np.float32)
    maskb = (jj[:, None] >= jj[None, :]).astype(np.float32)
    return sel, selT, maskf, maskb


def emit_s5b(P, C, I, gyT, gyTb):
    lo, hi = slice(0, 64), slice(64, 128)
    pb = P.buf("s5b_p")
    B1 = [pb]

    def T(name, shape=(128, GS), dt=F32):
        return P.sbuf("s5b_" + name, list(shape), dt)
    A2, B2 = T("A2", (128, 2, GS)), T("B2", (128, 2, GS))
    P.dma("sp", A2[:], I["L8d"][:, 0], writes=B1)
    P.dma("sp", B2[:], I["L8d"][:, 1], writes=B1)
    Lr, Li, t1, t2, nr_, ni_ = T("Lr"), T("Li"), T("t1"), T("t2"), T("nr"), T("ni")
    P.op("dve", lambda e: e.tensor_copy(out=Lr[:], in_=A2[:, 0, :]), reads=B1, writes=B1)
    P.op("dve", lambda e: e.tensor_copy(out=Li[:], in_=B2[:, 1, :]), reads=B1, writes=B1)
    for _ in range(7):
        s5_cmul(P, "dve", nr_[:], ni_[:], Lr[:], Li[:], Lr[:], Li[:], t1[:], t2[:], B1)
        P.op("dve", lambda e: e.tensor_copy(out=Lr[:], in_=nr_[:]), reads=B1, writes=B1)
        P.op("dve", lambda e: e.tensor_copy(out=Li[:], in_=ni_[:]), reads=B1, writes=B1)
    init3 = T("init3", (128, 3, GS))
    El = T("El", (128, 2, GS))
    P.dma("sp", init3[:, 0:2, :], I["Elist"][0].rearrange("p (a g) -> p a g", a=2), writes=B1)
    for s_ in range(1, 9):
        P.dma("sp", El[:], I["Elist"][s_].rearrange("p (a g) -> p a g", a=2), reads=B1, writes=B1)
        s5_cmul(P, "dve", nr_[:], ni_[:], init3[:, 0, :], init3[:, 1, :], Lr[:], Li[:], t1[:], t2[:], B1)
        s5_tt(P, "dve", init3[:, 0, :], nr_[:], El[:, 0, :], ALU.add, B1)
        s5_tt(P, "dve", init3[:, 1, :], ni_[:], El[:, 1, :], ALU.add, B1)
    P.op("dve", lambda e: e.tensor_copy(out=init3[:, 2, :], in_=init3[:, 0, :]), reads=B1, writes=B1)
    selT = P.sbuf("s5b_selT", [128, 64, 128], BF16)
    selTb = P.buf()
    P.dma("pool", selT[:], I["selT"], writes=[selTb])
    W = P.sbuf("s5b_W", [128, NCH, 3, NB], F32)
    Wb = [P.buf(), P.buf()]
    Inb = P.sbuf("s5b_Inb", [128, 2, NB, NCH], BF16)
    Inbb = P.buf()
    tA = [P.sbuf("s5b_tA%d" % i, [128, 2, NB], F32) for i in range(2)]
    tB = [P.sbuf("s5b_tB%d" % i, [128, 2, NB], F32) for i in range(2)]
    tb_ = [P.buf() for i in range(2)]
    Cc = [P.sbuf("s5b_Cc%d" % i, [128, 2, 8, 128], BF16) for i in range(2)]
    Ccb = [P.buf() for i in range(2)]
    Yi = [P.sbuf("s5b_Yi%d" % i, [128, 128], F32) for i in range(2)]
    Yib = [P.buf() for i in range(2)]
    Yo = [P.sbuf("s5b_Yo%d" % i, [128, 8, 128], BF16) for i in range(2)]
    Yob = [P.buf() for i in range(2)]
    g1_ = P.sbuf("s5b_g1", [128, 512], F32)
    g2_ = P.sbuf("s5b_g2", [128, 512], F32)
    gb_ = P.buf()
    C.rr_n = 4
    for bi_ in range(GS // NB):
        gb0 = bi_ * NB
        P.dma("sp", W[:].rearrange("p c s g -> p (c s g)"), I["Vd"][bi_], writes=Wb)
        A2v = A2[:, :, gb0:gb0 + NB]
        B2v = B2[:, :, gb0:gb0 + NB]
        iv = init3[:, :, gb0:gb0 + NB]
        s5_recur(P, "dve", lo, W, Wb[0], NCH, True, A2v, B2v, pb, iv[lo], tA[0], tB[0], tb_[0])
        s5_recur(P, "pool", hi, W, Wb[1], NCH, False, A2v, B2v, pb, iv[hi], tA[1], tB[1], tb_[1])
        for ri in range(2):
            P.op("dve", lambda e, ri=ri: e.tensor_copy(out=Inb[lo, ri, :, 1:NCH], in_=W[lo, 0:NCH - 1, ri, :].rearrange("p c g -> p g c")),
                 reads=[Wb[0]], writes=[Inbb])
            P.op("dve", lambda e, ri=ri, iv=iv: e.tensor_copy(out=Inb[lo, ri, :, 0], in_=iv[lo, ri, :]), reads=B1, writes=[Inbb])
            P.op("pool", lambda e, ri=ri: e.tensor_copy(out=Inb[hi, ri, :, 0:NCH - 1], in_=W[hi, 1:NCH, ri, :].rearrange("p c g -> p g c")),
                 reads=[Wb[1]], writes=[Inbb])
            P.op("pool", lambda e, ri=ri, iv=iv: e.tensor_copy(out=Inb[hi, ri, :, NCH - 1], in_=iv[hi, ri, :]), reads=B1, writes=[Inbb])
        for oo in range(NB // 8):
            o = bi_ * (NB // 8) + oo
            cc, ccb = Cc[o % 2], Ccb[o % 2]
            yo, yob = Yo[o % 2], Yob[o % 2]
            P.dma("sp", cc[:].rearrange("p a g m -> p (a g m)"), I["Ccd"][o], writes=[ccb])
            for gg in range(8):
                g = o * 8 + gg
                gl = g % NB
                yi, yib = Yi[g % 2], Yib[g % 2]
                P.dma("sp", yi[:], I["Yd"][g], writes=[yib])
                bank = C.rr4()
                for a_ in range(2):
                    P.op("pe", lambda e, a_=a_, gg=gg, gl=gl, cc=cc, bank=bank: e.matmul(
                        C.ps[:, bank * 512: bank * 512 + NCH], lhsT=cc[:, a_, gg, :], rhs=Inb[:, a_, gl, :], start=(a_ == 0), stop=(a_ == 1)),
                        reads=[ccb, Inbb], writes=[C.psb[bank]])
                P.op("dve", lambda e, gg=gg, yo=yo, yi=yi, bank=bank: e.tensor_tensor(
                    out=yo[:, gg, :], in0=C.ps[:, bank * 512: bank * 512 + NCH], in1=yi[:], op=ALU.add),
                    reads=[C.psb[bank], yib], writes=[yob])
            for half in range(2):
                bank = 4 + half
                first = True
                for gg in range(8):
                    for i in range(8):
                        outv = C.ps[:, bank * 512:(bank + 1) * 512].rearrange("p (c i) -> p i c", i=8)[:, i, :]
                        P.op("pe", lambda e, gg=gg, i=i, half=half, outv=outv, yo=yo, first=first: e.matmul(
                            outv, lhsT=selT[:, gg * 8 + i, :], rhs=yo[:, gg, half * 64:(half + 1) * 64],
                            start=first, stop=(gg == 7 and i == 7), skip_group_check=True),
                            reads=[selTb, yob], writes=[C.psb[bank]])
                        first = False
                xp = C.ps[:, bank * 512:(bank + 1) * 512]
                P.op("act", lambda e, xp=xp: e.activation(out=g1_[:], in_=xp, func=AF.Square), reads=[C.psb[bank]], writes=[gb_])
                P.op("dve", lambda e: e.tensor_scalar(out=g1_[:], in0=g1_[:], scalar1=0.044715, scalar2=1.0, op0=ALU.mult, op1=ALU.add),
                     reads=[gb_], writes=[gb_])
                P.op("dve", lambda e, xp=xp: e.tensor_tensor(out=g2_[:], in0=g1_[:], in1=xp, op=ALU.mult), reads=[gb_, C.psb[bank]], writes=[gb_])
                P.op("act", lambda e: e.activation(out=g2_[:], in_=g2_[:], func=AF.Sigmoid, scale=1.5957691216057308), reads=[gb_], writes=[gb_])
                P.op("dve", lambda e, xp=xp, o=o, half=half: e.tensor_tensor(out=gyT[:, o, half * 512:(half + 1) * 512], in0=g2_[:], in1=xp, op=ALU.mult),
                     reads=[gb_, C.psb[bank]], writes=[gyTb])


def emit_glu(P, C, gyT, gyTb, xin, xout, outb, w_glu, b_glu, gate_ap, ntile=8):
    bt = P.sbuf("gl_bt", [128, 2 * D], F32)
    gt = P.sbuf("gl_gt", [128, D], F32)
    tb = P.buf()
    P.dma("sp", bt[:], b_glu.partition_broadcast(128), writes=[tb])
    P.dma("sp", gt[:], gate_ap.partition_broadcast(128), writes=[tb])
    xs = [P.sbuf("gl_xs%d" % i, [128, 512], F32) for i in range(2)]
    xsb = [P.buf() for i in range(2)]
    sv = [P.sbuf("gl_sv%d" % i, [128, 512], F32) for i in range(2)]
    sg = [P.sbuf("gl_sg%d" % i, [128, 512], F32) for i in range(2)]
    svb = [P.buf() for i in range(2)]
    C.rr_n = 4
    it = 0
    for cb in range(D // 512):
        uv, uvb = C.next_wu()
        ug, ugb = C.next_wu()
        wv = uv[:, 0:KC * 512].rearrange("p (k c) -> p k c", k=KC)
        wg = ug[:, 0:KC * 512].rearrange("p (k c) -> p k c", k=KC)
        P.dma("pool", wv, w_glu[:, cb * 512:(cb + 1) * 512].rearrange("(k p) c -> p k c", p=128), writes=[uvb])
        P.dma("pool", wg, w_glu[:, D + cb * 512: D + (cb + 1) * 512].rearrange("(k p) c -> p k c", p=128), writes=[ugb])
        for t in range(ntile):
            bv = C.rr4()
            bg = C.rr4()
            for k in range(KC):
                P.op("pe", lambda e, k=k, t=t, bv=bv, wv=wv: e.matmul(C.ps[:, bv * 512:(bv + 1) * 512], lhsT=gyT[:, k, t * 128:(t + 1) * 128],
                                                                    rhs=wv[:, k, :], start=(k == 0), stop=(k == KC - 1)),
                     reads=[gyTb, uvb], writes=[C.psb[bv]])
            for k in range(KC):
                P.op("pe", lambda e, k=k, t=t, bg=bg, wg=wg: e.matmul(C.ps[:, bg * 512:(bg + 1) * 512], lhsT=gyT[:, k, t * 128:(t + 1) * 128],
                                                                    rhs=wg[:, k, :], start=(k == 0), stop=(k == KC - 1)),
                     reads=[gyTb, ugb], writes=[C.psb[bg]])
            x_, xb_ = xs[it % 2], xsb[it % 2]
            v_, g_, vb_ = sv[it % 2], sg[it % 2], svb[it % 2]
            P.dma("sp", x_[:], xin[t * 128:(t + 1) * 128, cb * 512:(cb + 1) * 512], writes=[xb_])
            P.op("dve", lambda e, g_=g_, bg=bg, cb=cb: e.tensor_tensor(out=g_[:], in0=C.ps[:, bg * 512:(bg + 1) * 512],
                                                                    in1=bt[:, D + cb * 512: D + (cb + 1) * 512], op=ALU.add),
                 reads=[C.psb[bg], tb], writes=[vb_])
            P.op("act", lambda e, g_=g_: e.activation(out=g_[:], in_=g_[:], func=AF.Sigmoid), reads=[vb_], writes=[vb_])
            P.op("dve", lambda e, v_=v_, bv=bv, cb=cb: e.tensor_tensor(out=v_[:], in0=C.ps[:, bv * 512:(bv + 1) * 512],
                                                                    in1=bt[:, cb * 512:(cb + 1) * 512], op=ALU.add),
                 reads=[C.psb[bv], tb], writes=[vb_])
            P.op("pool", lambda e, v_=v_, g_=g_: e.tensor_tensor(out=v_[:], in0=v_[:], in1=g_[:], op=ALU.mult), reads=[vb_], writes=[vb_])
            P.op("pool", lambda e, v_=v_, cb=cb: e.tensor_tensor(out=v_[:], in0=v_[:], in1=gt[:, cb * 512:(cb + 1) * 512], op=ALU.mult),
                 reads=[vb_, tb], writes=[vb_])
            P.op("pool", lambda e, v_=v_, x_=x_: e.tensor_tensor(out=x_[:], in0=v_[:], in1=x_[:], op=ALU.add), reads=[vb_, xb_], writes=[xb_])
            P.dma("sp", xout[t * 128:(t + 1) * 128, cb * 512:(cb + 1) * 512], x_[:], reads=[xb_], writes=[outb])
            it += 1


NMOD = 6 * D


def emit_mod(P, C, c_ap, cctx_ap, ada_w, ada_b, mod_d, modb):
    cf = P.sbuf("md_cf", [128, 2, KC], F32)
    cfb = P.buf()
    load_fm(P, cf[:, 0, :], cfb, c_ap)
    load_fm(P, cf[:, 1, :], cfb, cctx_ap)
    P.op("act", lambda e: e.activation(out=cf[:], in_=cf[:], func=AF.Silu), reads=[cfb], writes=[cfb])
    cT = P.sbuf("md_cT", [128, KC, 2], BF16)
    cTb = P.buf()
    P.op("dve", lambda e: e.tensor_copy(out=cT[:], in_=cf[:].rearrange("p v k -> p k v")), reads=[cfb], writes=[cTb])
    bt = [P.sbuf("md_bt%d" % i, [2, 512], F32) for i in range(2)]
    btb = [P.buf() for i in range(2)]
    ob = [P.sbuf("md_o%d" % i, [2, 512], F32) for i in range(2)]
    obb = [P.buf() for i in range(2)]
    C.rr_n = 4
    it = 0
    for L in range(2):
        for nb in range(NMOD // 512):
            u, ub = C.next_wu()
            wv = u[:, 0:KC * 512].rearrange("p (k c) -> p k c", k=KC)
            P.dma("pool", wv, ada_w[L][:, nb * 512:(nb + 1) * 512].rearrange("(k p) c -> p k c", p=128), writes=[ub])
            bank = C.rr4()
            for k in range(KC):
                P.op("pe", lambda e, k=k, bank=bank, wv=wv: e.matmul(C.ps[0:2, bank * 512:(bank + 1) * 512], lhsT=cT[:, k, :], rhs=wv[:, k, :],
                                                                   start=(k == 0), stop=(k == KC - 1)), reads=[cTb, ub], writes=[C.psb[bank]])
            o_, ob_ = ob[it % 2], obb[it % 2]
            b_, bb_ = bt[it % 2], btb[it % 2]
            P.dma("sp", b_[:], ada_b[L, nb * 512:(nb + 1) * 512].partition_broadcast(2), writes=[bb_])
            P.op("dve", lambda e, o_=o_, bank=bank, b_=b_: e.tensor_tensor(out=o_[:], in0=C.ps[0:2, bank * 512:(bank + 1) * 512],
                                                                         in1=b_[:], op=ALU.add),
                 reads=[C.psb[bank], bb_], writes=[ob_])
            P.dma("sp", mod_d[L, :, nb * 512:(nb + 1) * 512], o_[:], reads=[ob_], writes=[modb])
            it += 1


def mod_slices(mod_d, L, which):
    return [mod_d[L, v, which * D:(which + 1) * D] for v in range(2)]


def emit_final_norm(P, C, xin, xinb, xout, outb, gw_ap, ntile=8):
    gt = P.sbuf("fn_g", [128, D], F32)
    gtb = P.buf()
    P.dma("sp", gt[:], gw_ap.partition_broadcast(128), writes=[gtb])
    xt = [P.sbuf("fn_x%d" % i, [128, D], F32) for i in range(2)]
    xtb = [P.buf() for i in range(2)]
    xn = [P.sbuf("fn_n%d" % i, [128, D], F32) for i in range(2)]
    xnb = [P.buf() for i in range(2)]
    ssq = [P.sbuf("fn_s%d" % i, [128, 1], F32) for i in range(2)]
    for t in range(ntile):
        x_, xb_, n_, nb_, s_ = xt[t % 2], xtb[t % 2], xn[t % 2], xnb[t % 2], ssq[t % 2]
        P.dma("sp", x_[:], xin[t * 128:(t + 1) * 128, :], reads=[xinb], writes=[xb_])
        P.op("act", lambda e, x_=x_, n_=n_, s_=s_: e.activation(out=n_[:], in_=x_[:], func=AF.Square, accum_out=s_[:]), reads=[xb_], writes=[nb_])
        P.op("dve", lambda e, s_=s_: e.tensor_scalar(out=s_[:], in0=s_[:], scalar1=1.0 / D, scalar2=EPS, op0=ALU.mult, op1=ALU.add),
             reads=[nb_], writes=[nb_])
        P.op("act", lambda e, s_=s_: e.activation(out=s_[:], in_=s_[:], func=AF.Sqrt), reads=[nb_], writes=[nb_])
        P.op("dve", lambda e, s_=s_: e.reciprocal(out=s_[:], in_=s_[:]), reads=[nb_], writes=[nb_])
        P.op("act", lambda e, x_=x_, n_=n_, s_=s_: e.activation(out=n_[:], in_=x_[:], func=AF.Copy, scale=s_[:]), reads=[xb_, nb_], writes=[nb_])
        P.op("dve", lambda e, n_=n_: e.tensor_tensor(out=n_[:], in0=n_[:], in1=gt[:], op=ALU.mult), reads=[nb_, gtb], writes=[nb_])
        P.dma("sp", xout[t * 128:(t + 1) * 128, :], n_[:], reads=[nb_], writes=[outb])


S5_KEYS = ("a_re", "a_im", "log_dt", "b_re", "b_im", "c_re", "c_im")


def build_launch1():
    P = Prog()
    nc = P.nc
    P.outb = P.buf("out")

    def inp(name, shape):
        return nc.dram_tensor(name, list(shape), F32, kind="ExternalInput").ap()

    def outp(name, shape, dt=F32, kind="ExternalOutput"):
        return nc.dram_tensor(name, list(shape), dt, kind=kind).ap()
    I = dict(ident=inp("ident", [128, 128]), rperm=inp("rperm", [128, 128]), xh=inp("xh", [1536, D]), ctx=inp("ctx", [256, D]),
             c=inp("c", [D]), c_ctx=inp("c_ctx", [D]), ada_w=inp("ada_w", [2, D, NMOD]), ada_b=inp("ada_b", [2, NMOD]),
             norm_mix=inp("norm_mix", [2, D]), norm_ffn=inp("norm_ffn", [2, D]),
             w_in=inp("w_in", [D, 4608]), w_out=inp("w_out", [D, D]), nab=inp("nab", [8, 128, 3200]),
             swm=inp("swm", [128, 1152]), ropec=inp("ropec", [128, 1536]), ropes=inp("ropes", [128, 1536]), sink=inp("sink", [8]),
             w1=inp("w1", [D, DFF]), w3=inp("w3", [D, DFF]), w2=inp("w2", [DFF, D]),
             a_re=inp("a_re", [2, 128, 64]), a_im=inp("a_im", [2, 128, 64]), log_dt=inp("log_dt", [2, 128]),
             b_re=inp("b_re", [2, 128, 64, 16]), b_im=inp("b_im", [2, 128, 64, 16]),
             c_re=inp("c_re", [2, 128, 16, 64]), c_im=inp("c_im", [2, 128, 16, 64]),
             ssm_d=inp("ssm_d", [D]), sel=inp("sel", [128, 64, 128]), maskf=inp("maskf", [128, 128]), maskb=inp("maskb", [128, 128]))
    I["Vd"] = outp("Vd", [4, 128, NCH * 3 * NB])
    I["Yd"] = outp("Yd", [GS, 128, 128])
    I["Ccd"] = outp("Ccd", [16, 128, 2048], BF16)
    I["Eown"] = outp("Eown", [128, 2 * GS])
    I["Ectx"] = outp("Ectx", [128, 2 * GS])
    I["L8d"] = outp("L8d", [128, 2, 2, GS])
    mod_d = outp("mod_d", [2, 2, NMOD])
    x2 = outp("x2", [1280, D])
    I["oT_d"] = outp("oT_d", [2048, 1280], BF16, kind="Internal")
    import os as _os
    x1 = outp("x1", [1280, D], kind="ExternalOutput" if _os.environ.get("DBG_X1") else "Internal")
    hTd = outp("hTd", [KC, 128, 1280], BF16, kind="Internal")
    modb = P.buf("modd")
    C = Ctx(P, I["ident"])
    with P.scope():
        C.alloc_wu(3)
        emit_mod(P, C, I["c"], I["c_ctx"], I["ada_w"], I["ada_b"], mod_d, modb)
    I["gw"] = I["norm_mix"][0]
    I["sc"] = mod_slices(mod_d, 0, 1)
    I["sh"] = mod_slices(mod_d, 0, 0)
    with P.scope():
        C.alloc_wu(3)
        oTd_b = emit_attn(P, C, I)
    x1b = P.buf("x1")
    with P.scope():
        C.alloc_wu(3)
        _, _, G = emit_modprep(P, C, "om", I["gw"], I["sc"], I["sh"], mod_slices(mod_d, 0, 2), 2)
        xsrc = lambda t: (I["xh"][256 + t * 128: 256 + (t + 1) * 128, :] if t < 8 else I["ctx"][(t - 8) * 128:(t - 7) * 128, :])
        emit_outproj(P, C, I["oT_d"], oTd_b, xsrc, x1, x1b, I["w_out"], G, 10, [0] * 8 + [1] * 2)
    with P.scope():
        C.alloc_wu(3)
        A, B, G = emit_modprep(P, C, "f0", I["norm_ffn"][0], mod_slices(mod_d, 0, 4), mod_slices(mod_d, 0, 3), mod_slices(mod_d, 0, 5), 2)
        emit_ffn(P, C, "f0", x1, x2, 1280, [0] * 8 + [1] * 2, A, B, G, I["w1"], I["w3"], I["w2"])
    hTdb = P.buf("hTd")
    with P.scope():
        hT = P.sbuf("l1_hT", [128, KC, 1280], BF16)
        hTb = P.buf()
        A, B, _ = emit_modprep(P, C, "s5m", I["norm_mix"][1], mod_slices(mod_d, 1, 1), mod_slices(mod_d, 1, 0), None, 2)
        srcs = [(x2[t * 128:(t + 1) * 128, :], 0 if t < 8 else 1, t * 128) for t in range(10)]
        emit_norm_all(P, C, srcs, hT, hTb, A, B)
        P.dma("sp", hTd.rearrange("k p t -> p k t"), hT[:], reads=[hTb], writes=[hTdb])
    with P.scope():
        emit_s5a(P, C, I, hTd, hTdb)
    print("launch1 n_instr", P.n_instr)
    return P.finish_all()


def build_launch2():
    P = Prog()
    nc = P.nc
    P.outb = P.buf("out")

    def inp(name, shape, dt=F32):
        return nc.dram_tensor(name, list(shape), dt, kind="ExternalInput").ap()

    def outp(name, shape, dt=F32, kind="ExternalOutput"):
        return nc.dram_tensor(name, list(shape), dt, kind=kind).ap()
    I = dict(ident=inp("ident", [128, 128]), selT=inp("selT", [128, 64, 128]),
             Vd=inp("Vd", [4, 128, NCH * 3 * NB]), Yd=inp("Yd", [GS, 128, 128]), Ccd=inp("Ccd", [16, 128, 2048], BF16),
             L8d=inp("L8d", [128, 2, 2, GS]), Elist=inp("Elist", [9, 128, 2 * GS]), mod_d=inp("mod_d", [2, 2, NMOD]),
             x2=inp("x2", [1024, D]), w_glu=inp("w_glu", [D, 2 * D]), b_glu=inp("b_glu", [2 * D]),
             norm_ffn=inp("norm_ffn", [D]), norm_final=inp("norm_final", [D]),
             w1=inp("w1", [D, DFF]), w3=inp("w3", [D, DFF]), w2=inp("w2", [DFF, D]))
    out = outp("out", [1024, D])
    x3 = outp("x3", [1024, D], kind="Internal")
    x4 = outp("x4", [1024, D], kind="Internal")
    mod_d = I["mod_d"]
    C = Ctx(P, I["ident"])
    x3b = P.buf("x3")
    with P.scope():
        gyT = P.sbuf("gyT", [128, KC, 1024], BF16)
        gyTb = P.buf()
        with P.scope():
            emit_s5b(P, C, I, gyT, gyTb)
        with P.scope():
            C.alloc_wu(2)
            emit_glu(P, C, gyT, gyTb, I["x2"], x3, x3b, I["w_glu"], I["b_glu"], mod_d[1, 0, 2 * D:3 * D])
    with P.scope():
        C.alloc_wu(3)
        A, B, G = emit_modprep(P, C, "f1", I["norm_ffn"], [mod_d[1, 0, 4 * D:5 * D]], [mod_d[1, 0, 3 * D:4 * D]], [mod_d[1, 0, 5 * D:6 * D]], 1)
        P.outb = P.buf("x4")
        emit_ffn(P, C, "f1", x3, x4, 1024, [0] * 8, A, B, G, I["w1"], I["w3"], I["w2"])
    with P.scope():
        ob = P.buf("final")
        emit_final_norm(P, C, x4, P.outb, out, ob, I["norm_final"])
    print("launch2 n_instr", P.n_instr)
    return P.finish_all()


def kernel(x, c, ctx, c_ctx, ada_w, ada_b, norm_mix, norm_ffn, ffn_w1, ffn_w3, ffn_w2,
           attn_w_in, attn_w_out, attn_rpb, attn_sink,
           ssm_a_re, ssm_a_im, ssm_log_dt, ssm_b_re, ssm_b_im, ssm_c_re, ssm_c_im,
           ssm_d, ssm_w_glu, ssm_b_glu, norm_final):
    from concourse.bass_utils import run_bass_kernel_spmd
    f32 = lambda a: np.ascontiguousarray(np.asarray(a, dtype=np.float32))
    x = f32(x)
    ctx = f32(ctx)
    ncore = 8
    ident, rperm = host_consts()
    sel, selT, maskf, maskb = host_s5_consts()
    rpb_ext = np.concatenate([f32(attn_rpb)[0].reshape(8, 465), np.full((8, 1), -30000.0, np.float32)], axis=1)
    shared1 = dict(ident=ident, rperm=rperm, ctx=ctx[0], c=f32(c)[0], c_ctx=f32(c_ctx), ada_w=f32(ada_w), ada_b=f32(ada_b),
                   norm_mix=f32(norm_mix), norm_ffn=f32(norm_ffn), w_in=f32(attn_w_in)[0], w_out=f32(attn_w_out)[0],
                   sink=f32(attn_sink)[0], w1=f32(ffn_w1)[0], w3=f32(ffn_w3)[0], w2=f32(ffn_w2)[0],
                   a_re=f32(ssm_a_re)[0], a_im=f32(ssm_a_im)[0], log_dt=f32(ssm_log_dt)[0], b_re=f32(ssm_b_re)[0], b_im=f32(ssm_b_im)[0],
                   c_re=f32(ssm_c_re)[0], c_im=f32(ssm_c_im)[0], ssm_d=f32(ssm_d)[0], sel=sel, maskf=maskf, maskb=maskb)
    in1 = []
    for k in range(ncore):
        d = dict(shared1)
        d["xh"] = host_xh(x[0], k)
        d["nab"] = np.ascontiguousarray(rpb_ext[:, host_na_index(k)].reshape(8, 128, 3200))
        d["swm"] = np.ascontiguousarray(host_sw_mask(k).reshape(128, 1152))
        d["ropec"], d["ropes"] = host_rope(k)
        in1.append(d)
    nc1 = build_launch1()
    r1 = run_bass_kernel_spmd(nc1, in1, core_ids=list(range(ncore))).results
    Eown = [np.asarray(r1[k]["Eown"], dtype=np.float32) for k in range(ncore)]
    Ectx = np.asarray(r1[0]["Ectx"], dtype=np.float32)
    zero = np.zeros_like(Ectx)
    shared2 = dict(ident=ident, selT=selT, w_glu=f32(ssm_w_glu)[0], b_glu=f32(ssm_b_glu)[0], norm_ffn=f32(norm_ffn)[1],
                   norm_final=f32(norm_final), w1=f32(ffn_w1)[1], w3=f32(ffn_w3)[1], w2=f32(ffn_w2)[1])
    in2 = []
    for k in range(ncore):
        lf = [Ectx] + [Eown[m] for m in range(k)]
        lb = [Ectx] + [Eown[m] for m in range(ncore - 1, k, -1)]
        lf = [zero] * (9 - len(lf)) + lf
        lb = [zero] * (9 - len(lb)) + lb
        El = np.stack([np.concatenate([lf[s][0:64], lb[s][64:128]], axis=0) for s in range(9)])
        d = dict(shared2)
        d.update(Vd=r1[k]["Vd"], Yd=r1[k]["Yd"], Ccd=r1[k]["Ccd"], L8d=r1[k]["L8d"], Elist=np.ascontiguousarray(El),
                 mod_d=r1[k]["mod_d"], x2=np.ascontiguousarray(np.asarray(r1[k]["x2"])[0:1024]))
        in2.append(d)
    nc2 = build_launch2()
    r2 = run_bass_kernel_spmd(nc2, in2, core_ids=list(range(ncore))).results
    out = np.concatenate([np.asarray(r2[k]["out"], dtype=np.float32) for k in range(ncore)], axis=0)
    return out[None]
```

```python
import contextlib
import numpy as np
import concourse.bass as bass
import concourse.mybir as mybir

F32 = mybir.dt.float32
BF16 = mybir.dt.bfloat16
AF = mybir.ActivationFunctionType
ALU = mybir.AluOpType
AX = mybir.AxisListType

ENGS = ("pe", "act", "dve", "pool", "sp")
SEM_LIMIT = 30000
NDMA_SLOTS = 8


class Buf:
    __slots__ = ("name", "w", "readers", "excl")

    def __init__(self, name):
        self.name = name
        self.excl = False
        self.w = None
        self.readers = {}


class Prog:
    def __init__(self, strict_same_engine=True):
        self.nc = bass.Bass("TRN2", target_bir_lowering=False)
        self.es = contextlib.ExitStack()
        self.q = {e: [] for e in ENGS}
        self.strict = strict_same_engine
        self.sems = {}
        self.epoch = {e: 0 for e in ENGS}
        self.cnt = {e: 0 for e in ENGS}
        self.seen = {e: {} for e in ENGS}
        self.dma_slot = {e: 0 for e in ENGS}
        self.dma_val = {}
        self.n_instr = 0
        self._uid = 0

    def sem(self, key):
        if key not in self.sems:
            self.sems[key] = getattr(self, "sem_es", self.es).enter_context(self.nc.semaphore("s_" + "_".join(str(x) for x in key)))
        return self.sems[key]

    def sbuf(self, name, shape, dtype):
        self._uid += 1
        return self.es.enter_context(self.nc.sbuf_tensor("%s_u%d" % (name, self._uid), list(shape), dtype))

    def psum(self, name, shape, dtype):
        return self.es.enter_context(self.nc.psum_tensor(name, list(shape), dtype))

    def dram(self, name, shape, dtype, kind="Internal"):
        return self.nc.dram_tensor(name, list(shape), dtype, kind=kind)

    def buf(self, name=None):
        self._uid += 1
        return Buf(name or "b%d" % self._uid)

    def _deps(self, eng, reads, writes, relax=False):
        deps = {}

        def add(d):
            if d is None:
                return
            k, v = d[0], d[1]
            if deps.get(k, 0) < v:
                deps[k] = v
        for b in reads:
            add(b.w)
        for b in writes:
            add(b.w)
            for r in b.readers.values():
                add(r)
        out = []
        for k, v in deps.items():
            if k[0] == "e" and k[1] == eng and (eng == "pe" or relax or not self.strict):
                continue
            if self.seen[eng].get(k, 0) >= v:
                continue
            self.seen[eng][k] = v
            out.append((k, v))
        return out

    def op(self, eng, fn, reads=(), writes=(), relax=False):
        xs = [b for b in reads if b.excl]
        if xs:
            writes = list(writes) + [b for b in xs if b not in writes]
        waits = self._deps(eng, reads, writes, relax)
        if self.cnt[eng] >= SEM_LIMIT:
            self.epoch[eng] += 1
            self.cnt[eng] = 0
        key = ("e", eng, self.epoch[eng])
        self.sem(key)
        self.cnt[eng] += 1
        val = self.cnt[eng]
        self.q[eng].append((waits, fn, key, 1))
        for b in reads:
            b.readers[eng] = (key, val)
        for b in writes:
            b.w = (key, val, eng)
            b.readers = {}
        self.n_instr += 1

    def dma(self, eng, out, in_, reads=(), writes=(), **kw):
        slot = self.dma_slot[eng]
        self.dma_slot[eng] = (slot + 1) % NDMA_SLOTS
        key = ("d", eng, slot)
        self.sem(key)
        waits = self._deps(eng, reads, writes)
        prev = self.dma_val.get(key, 0)
        if prev > 0 and self.seen[eng].get(key, 0) < prev:
            self.seen[eng][key] = prev
            waits.append((key, prev))
        val = prev + 16
        self.dma_val[key] = val

        def fn(e, out=out, in_=in_, kw=kw):
            return e.dma_start(out=out, in_=in_, **kw)
        self.q[eng].append((waits, fn, key, 16))
        for b in reads:
            b.readers["dma_" + eng + str(slot)] = (key, val)
        for b in writes:
            b.w = (key, val, "dma")
            b.readers = {}
        self.n_instr += 1

    def flush(self):
        for key, val in self.dma_val.items():
            e = key[1]
            if self.seen[e].get(key, 0) < val:
                self.seen[e][key] = val
                self.q[e].append(([(key, val)], None, None, 0))
        nc = self.nc
        engmap = {"pe": "tensor", "act": "scalar", "dve": "vector", "pool": "gpsimd", "sp": "sync"}
        if any(self.q[e] for e in ENGS):
            with nc.Block(no_gpsimd_drain=True) as block:
                for e in ENGS:
                    items = self.q[e]
                    if not items:
                        continue

                    def body(h, items=items):
                        for waits, fn, key, inc in items:
                            for k, v in waits:
                                h.wait_ge(self.sems[k], v)
                            if fn is not None:
                                ins = fn(h)
                                ins.then_inc(self.sems[key], inc)
                    getattr(block, engmap[e])(body)
        self.q = {e: [] for e in ENGS}

    @contextlib.contextmanager
    def scope(self):
        outer = self.es
        self.es = contextlib.ExitStack()
        self.sem_es = getattr(self, "sem_es", outer)
        try:
            yield
            self.flush()
        finally:
            self.es.close()
            self.es = outer

    def finish_all(self):
        self.flush()
        self.sem_es.close() if hasattr(self, "sem_es") else None
        self.es.close()
        return self.nc

    def finish(self, final_bufs=()):
        for e in ENGS:
            waits = self._deps(e, list(final_bufs), [])
            if waits:
                self.q[e].append((waits, None, None, 0))
        for key, val in self.dma_val.items():
            e = key[1]
            if self.seen[e].get(key, 0) < val:
                self.seen[e][key] = val
                self.q[e].append(([(key, val)], None, None, 0))
        nc = self.nc
        engmap = {"pe": "tensor", "act": "scalar", "dve": "vector", "pool": "gpsimd", "sp": "sync"}
        with nc.Block() as block:
            for e in ENGS:
                items = self.q[e]
                if not items:
                    continue

                def body(h, items=items):
                    for waits, fn, key, inc in items:
                        for k, v in waits:
                            h.wait_ge(self.sems[k], v)
                        if fn is not None:
                            ins = fn(h)
                            ins.then_inc(self.sems[key], inc)
                getattr(block, engmap[e])(body)
        self.es.close()
        return nc


D = 2048
KC = D // 128
DFF = 5632
NFF = DFF // 128
EPS = 1e-6
WUNIT = 11264


def load_fm(P, dst, dst_buf, vec_ap, eng="sp"):
    P.dma(eng, dst, vec_ap.rearrange("(k p) -> p k", p=128), writes=[dst_buf],
          allow_slow_non_contiguous=True)


class Ctx:
    def __init__(self, P, ident_ap):
        self.P = P
        nc = P.nc
        self.ident = P.sbuf("ident_sb", [128, 128], F32)
        self.ident_b = P.buf("ident")
        P.dma("sp", self.ident[:], ident_ap, writes=[self.ident_b])
        self.ps = P.psum("ps", [128, 8 * 512], F32)
        self.psb = [P.buf("psb%d" % i) for i in range(8)]
        for b in self.psb:
            b.excl = True
        self.wu = None
        self.wu_gen = 0
        self.rr_n = 4

    def bank(self, i):
        return self.ps[:, i * 512:(i + 1) * 512]

    def rr4(self):
        i = getattr(self, "_rr", 0) % self.rr_n
        self._rr = (i + 1) % self.rr_n
        return i

    def alloc_wu(self, n=3):
        P = self.P
        self.wu_gen += 1
        self.wu = [P.sbuf("wu%d_%d" % (self.wu_gen, i), [128, WUNIT], BF16) for i in range(n)]
        self.wub = [P.buf("wu%d" % i) for i in range(n)]
        self.wu_next = 0

    def next_wu(self):
        i = self.wu_next
        self.wu_next = (i + 1) % len(self.wu)
        return self.wu[i], self.wub[i]


def emit_modprep(P, C, name, gw_ap, sc_ap, sh_ap, gate_ap, ncls):
    gw = P.sbuf(name + "_gw", [128, KC], F32)
    gwb = P.buf()
    load_fm(P, gw[:], gwb, gw_ap)
    A, B, G = [], [], []
    for c in range(ncls):
        a = P.sbuf("%s_A%d" % (name, c), [128, KC], F32)
        ab = P.buf()
        load_fm(P, a[:], ab, sc_ap[c])
        P.op("dve", lambda e, a=a: e.tensor_scalar(out=a[:], in0=a[:], scalar1=1.0, scalar2=None, op0=ALU.add),
             reads=[ab], writes=[ab])
        P.op("dve", lambda e, a=a: e.tensor_tensor(out=a[:], in0=a[:], in1=gw[:], op=ALU.mult),
             reads=[ab, gwb], writes=[ab])
        b = P.sbuf("%s_B%d" % (name, c), [128, KC], F32)
        bb = P.buf()
        load_fm(P, b[:], bb, sh_ap[c])
        g = None
        gb = None
        if gate_ap is not None:
            g = P.sbuf("%s_G%d" % (name, c), [128, D], F32)
            gb = P.buf()
            P.dma("sp", g[:], gate_ap[c].partition_broadcast(128), writes=[gb])
        A.append((a, ab))
        B.append((b, bb))
        G.append((g, gb))
    return A, B, G


def emit_norm_T(P, C, xt, xtb, hT, hTb, col0, A, B, scr):
    ssq, rstd, xn, sb = scr
    P.op("act", lambda e: e.activation(out=xn[:], in_=xt, func=AF.Square, accum_out=ssq[:]),
         reads=[xtb], writes=[sb])
    P.op("dve", lambda e: e.tensor_scalar(out=rstd[:], in0=ssq[:], scalar1=1.0 / D, scalar2=EPS,
                                          op0=ALU.mult, op1=ALU.add), reads=[sb], writes=[sb])
    P.op("act", lambda e: e.activation(out=rstd[:], in_=rstd[:], func=AF.Sqrt), reads=[sb], writes=[sb])
    P.op("dve", lambda e: e.reciprocal(out=rstd[:], in_=rstd[:]), reads=[sb], writes=[sb])
    P.op("act", lambda e: e.activation(out=xn[:], in_=xt, func=AF.Copy, scale=rstd[:]),
         reads=[xtb, sb], writes=[sb])
    a, ab = A
    b, bb = B
    for q in range(KC // 4):
        bank = 6 + (q % 2)
        for kk in range(4):
            k = q * 4 + kk
            P.op("pe", lambda e, k=k, kk=kk, bank=bank: e.transpose(
                out=C.ps[:, bank * 512 + kk * 128: bank * 512 + (kk + 1) * 128],
                in_=xn[:, k * 128:(k + 1) * 128], identity=C.ident[:]),
                reads=[sb, C.ident_b], writes=[C.psb[bank]])
        for kk in range(4):
            k = q * 4 + kk
            P.op("dve", lambda e, k=k, kk=kk, bank=bank: e.tensor_scalar(
                out=hT[:, k, col0:col0 + 128],
                in0=C.ps[:, bank * 512 + kk * 128: bank * 512 + (kk + 1) * 128],
                scalar1=a[:, k:k + 1], scalar2=b[:, k:k + 1], op0=ALU.mult, op1=ALU.add),
                reads=[C.psb[bank], ab, bb], writes=[hTb])


def emit_ffn(P, C, name, xin, xout, T, cls_of_tile, A, B, G, w1, w3, w2):
    ntile = T // 128
    xt = [P.sbuf("%s_xt%d" % (name, i), [128, D], F32) for i in range(4)]
    xtb = [P.buf() for i in range(4)]
    hT = P.sbuf(name + "_hT", [128, KC, 512], BF16)
    hTb = P.buf()
    gT = P.sbuf(name + "_gT", [128, NFF, 512], BF16)
    gTb = [P.buf() for j in range(NFF)]
    ssq = P.sbuf(name + "_ssq", [128, 1], F32)
    rstd = P.sbuf(name + "_rstd", [128, 1], F32)
    xn = P.sbuf(name + "_xn", [128, D], F32)
    scr = (ssq, rstd, xn, P.buf())
    su = [P.sbuf("%s_su%d" % (name, i), [128, 512], F32) for i in range(2)]
    sub = [P.buf() for i in range(2)]
    ot = [P.sbuf("%s_ot%d" % (name, i), [128, 256], F32) for i in range(2)]
    otb = [P.buf() for i in range(2)]
    it = 0
    dn = 0
    for t0 in range(0, ntile, 4):
        nt = min(4, ntile - t0)
        ntok = nt * 128
        for i in range(nt):
            P.dma("sp", xt[i][:], xin[(t0 + i) * 128:(t0 + i + 1) * 128, :], writes=[xtb[i]])
            c = cls_of_tile[t0 + i]
            emit_norm_T(P, C, xt[i][:], xtb[i], hT, hTb, i * 128, A[c], B[c], scr)
        for jg in range(NFF // 4):
            w1u, w1b = C.next_wu()
            w3u, w3b = C.next_wu()
            w1v = w1u[:, 0:KC * 512].rearrange("p (k c) -> p k c", k=KC)
            w3v = w3u[:, 0:KC * 512].rearrange("p (k c) -> p k c", k=KC)
            P.dma("pool", w1v, w1[:, jg * 512:(jg + 1) * 512].rearrange("(k p) c -> p k c", p=128), writes=[w1b])
            P.dma("pool", w3v, w3[:, jg * 512:(jg + 1) * 512].rearrange("(k p) c -> p k c", p=128), writes=[w3b])
            for jj in range(4):
                j = jg * 4 + jj
                bu = it % 2
                bv = 2 + it % 2
                for k in range(KC):
                    P.op("pe", lambda e, k=k, jj=jj, bu=bu, w1v=w1v, ntok=ntok: e.matmul(
                        C.ps[:, bu * 512: bu * 512 + ntok], lhsT=w1v[:, k, jj * 128:(jj + 1) * 128],
                        rhs=hT[:, k, 0:ntok], start=(k == 0), stop=(k == KC - 1)),
                        reads=[w1b, hTb], writes=[C.psb[bu]])
                for k in range(KC):
                    P.op("pe", lambda e, k=k, jj=jj, bv=bv, w3v=w3v, ntok=ntok: e.matmul(
                        C.ps[:, bv * 512: bv * 512 + ntok], lhsT=w3v[:, k, jj * 128:(jj + 1) * 128],
                        rhs=hT[:, k, 0:ntok], start=(k == 0), stop=(k == KC - 1)),
                        reads=[w3b, hTb], writes=[C.psb[bv]])
                s = su[it % 2]
                P.op("act", lambda e, s=s, bu=bu, ntok=ntok: e.activation(out=s[:, 0:ntok], in_=C.ps[:, bu * 512: bu * 512 + ntok],
                                                               func=AF.Silu),
                     reads=[C.psb[bu]], writes=[sub[it % 2]])
                P.op("dve", lambda e, s=s, bv=bv, j=j, ntok=ntok: e.tensor_tensor(
                    out=gT[:, j, 0:ntok], in0=s[:, 0:ntok], in1=C.ps[:, bv * 512: bv * 512 + ntok], op=ALU.mult),
                    reads=[sub[it % 2], C.psb[bv]], writes=[gTb[j]])
                it += 1
        for cb in range(D // 256):
            w2u, w2b = C.next_wu()
            w2v = w2u[:, 0:NFF * 256].rearrange("p (j c) -> p j c", j=NFF)
            P.dma("pool", w2v, w2[:, cb * 256:(cb + 1) * 256].rearrange("(j p) c -> p j c", p=128), writes=[w2b])
            for i in range(nt):
                bank = 4 + dn % 2
                for j in range(NFF):
                    P.op("pe", lambda e, j=j, i=i, bank=bank, w2v=w2v: e.matmul(
                        C.ps[:, bank * 512: bank * 512 + 256], lhsT=gT[:, j, i * 128:(i + 1) * 128],
                        rhs=w2v[:, j, :], start=(j == 0), stop=(j == NFF - 1)),
                        reads=[w2b, gTb[j]], writes=[C.psb[bank]])
                c = cls_of_tile[t0 + i]
                g, gb = G[c]
                o = ot[dn % 2]
                ob = otb[dn % 2]
                P.op("dve", lambda e, o=o, bank=bank, g=g, cb=cb: e.tensor_tensor(
                    out=o[:], in0=C.ps[:, bank * 512: bank * 512 + 256], in1=g[:, cb * 256:(cb + 1) * 256], op=ALU.mult),
                    reads=[C.psb[bank], gb], writes=[ob])
                P.op("pool", lambda e, o=o, i=i, cb=cb: e.tensor_tensor(
                    out=xt[i][:, cb * 256:(cb + 1) * 256], in0=o[:], in1=xt[i][:, cb * 256:(cb + 1) * 256], op=ALU.add),
                    reads=[ob, xtb[i]], writes=[xtb[i]])
                dn += 1
        for i in range(nt):
            P.dma("sp", xout[(t0 + i) * 128:(t0 + i + 1) * 128, :], xt[i][:], reads=[xtb[i]], writes=[P.outb])


NKT = 14
NQT = 10
SCALE = 128 ** -0.5


def q_hcol(qi):
    return (qi + 2) * 128 if qi < 8 else 1536 + (qi - 8) * 128


def emit_norm_all(P, C, srcs, hT, hTb, A, B):
    xt = [P.sbuf("nx%d" % i, [128, D], F32) for i in range(2)]
    xtb = [P.buf() for i in range(2)]
    ssq = P.sbuf("n_ssq", [128, 1], F32)
    rstd = P.sbuf("n_rstd", [128, 1], F32)
    xn = P.sbuf("n_xn", [128, D], F32)
    scr = (ssq, rstd, xn, P.buf())
    for n, (ap, c, col0) in enumerate(srcs):
        P.dma("sp", xt[n % 2][:], ap, writes=[xtb[n % 2]])
        emit_norm_T(P, C, xt[n % 2][:], xtb[n % 2], hT, hTb, col0, A[c], B[c], scr)


def emit_proj_fm(P, C, dst, dstb, wv, wb, wc0, hT, hTb, ranges, evac="act"):
    for (d0, h0, n) in ranges:
        bank = C.rr4()
        for k in range(KC):
            P.op("pe", lambda e, k=k, bank=bank, h0=h0, n=n: e.matmul(
                C.ps[:, bank * 512: bank * 512 + n], lhsT=wv[:, k, wc0:wc0 + 128],
                rhs=hT[:, k, h0:h0 + n], start=(k == 0), stop=(k == KC - 1)),
                reads=[wb, hTb], writes=[C.psb[bank]])
        P.op(evac, lambda e, bank=bank, d0=d0, n=n: (e.activation(out=dst[:, d0:d0 + n], in_=C.ps[:, bank * 512: bank * 512 + n], func=AF.Copy)
                                                     if evac == "act" else
                                                     e.tensor_copy(out=dst[:, d0:d0 + n], in_=C.ps[:, bank * 512: bank * 512 + n])),
             reads=[C.psb[bank]], writes=[dstb])


def emit_proj_v(P, C, Vh, Vhb, wv, wb, wc0, hT, hTb):
    for g0 in range(0, NKT, 4):
        n = min(4, NKT - g0)
        bank = C.rr4()
        for j in range(n):
            kt = g0 + j
            for k in range(KC):
                P.op("pe", lambda e, k=k, kt=kt, j=j, bank=bank: e.matmul(
                    C.ps[:, bank * 512 + j * 128: bank * 512 + (j + 1) * 128],
                    lhsT=hT[:, k, kt * 128:(kt + 1) * 128], rhs=wv[:, k, wc0:wc0 + 128],
                    start=(k == 0), stop=(k == KC - 1)),
                    reads=[wb, hTb], writes=[C.psb[bank]])
        P.op("dve", lambda e, g0=g0, n=n, bank=bank: e.tensor_copy(
            out=Vh[:, g0:g0 + n, :], in_=C.ps[:, bank * 512: bank * 512 + n * 128].rearrange("p (j d) -> p j d", j=n)),
            reads=[C.psb[bank]], writes=[Vhb])


def emit_attn_core(P, C, S, hidx, qT, qTb, kT, kTb, Vh, Vhb, slots_of, tab, tabb, esink_col, oT_d, oTd_b):
    for qi in range(NQT):
        kts, toff, ntab = slots_of(qi)
        ns = len(kts)
        it = S["it"]
        S["it"] += 1
        b0 = 4 + 2 * (it % 2)
        ob = it % 2
        pe_sb = S["pe_sb"][it % 2]
        pe_b = S["pe_b"][it % 2]
        pm = S["pm"][it % 2]
        pm_b = S["pm_b"][it % 2]
        for si, kt in enumerate(kts):
            bank = b0 + si // 4
            off = bank * 512 + (si % 4) * 128
            P.op("pe", lambda e, kt=kt, off=off, qi=qi: e.matmul(
                C.ps[:, off:off + 128], lhsT=kT[:, kt * 128:(kt + 1) * 128], rhs=qT[:, qi * 128:(qi + 1) * 128],
                start=True, stop=True), reads=[kTb, qTb], writes=[C.psb[bank]])
        for half in range((ns + 3) // 4):
            n = min(4, ns - half * 4)
            bank = b0 + half
            P.op("act", lambda e, half=half, n=n, bank=bank, pe_sb=pe_sb: e.activation(
                out=pe_sb[:, half * 512: half * 512 + n * 128], in_=C.ps[:, bank * 512: bank * 512 + n * 128],
                func=AF.Exp, scale=SCALE), reads=[C.psb[bank]], writes=[pe_b])
        if toff is not None:
            P.op("dve", lambda e, toff=toff, ntab=ntab, pe_sb=pe_sb, pm=pm: e.tensor_tensor(
                out=pm[:, 0:ntab * 128], in0=pe_sb[:, 0:ntab * 128], in1=tab[:, toff:toff + ntab * 128], op=ALU.mult),
                reads=[pe_b, tabb], writes=[pm_b])
        else:
            ntab = 0
        otb = S["otb"][ob]
        denb = S["denb"][ob]
        for si, kt in enumerate(kts):
            src, srcb = (pm, pm_b) if si < ntab else (pe_sb, pe_b)
            P.op("pe", lambda e, kt=kt, si=si, src=src, ob=ob, ns=ns: e.matmul(
                S["ot"][:, ob * 512: ob * 512 + 128], lhsT=Vh[:, kt, :], rhs=src[:, si * 128:(si + 1) * 128],
                start=(si == 0), stop=(si == ns - 1)), reads=[Vhb, srcb], writes=[otb])
        for si, kt in enumerate(kts):
            src, srcb = (pm, pm_b) if si < ntab else (pe_sb, pe_b)
            P.op("pe", lambda e, si=si, src=src, ob=ob, ns=ns: e.matmul(
                S["ot"][:, ob * 512 + 128: ob * 512 + 256], lhsT=S["ones"][:], rhs=src[:, si * 128:(si + 1) * 128],
                start=(si == 0), stop=(si == ns - 1)), reads=[srcb, S["ones_b"]], writes=[denb])
        rec = S["rec"][it % 2]
        recb = S["rec_b"][it % 2]
        if esink_col is not None:
            P.op("dve", lambda e, rec=rec, ob=ob: e.tensor_scalar(
                out=rec[:], in0=S["ot"][:, ob * 512 + 128: ob * 512 + 256], scalar1=esink_col, scalar2=None, op0=ALU.add),
                reads=[denb, S["esink_b"]], writes=[recb])
            P.op("dve", lambda e, rec=rec: e.reciprocal(out=rec[:], in_=rec[:]), reads=[recb], writes=[recb])
        else:
            P.op("dve", lambda e, rec=rec, ob=ob: e.reciprocal(out=rec[:], in_=S["ot"][:, ob * 512 + 128: ob * 512 + 256]),
                 reads=[denb], writes=[recb])
        oh = S["oh"]
        P.op("dve", lambda e, rec=rec, ob=ob, qi=qi: e.tensor_tensor(
            out=oh[:, qi * 128:(qi + 1) * 128], in0=S["ot"][:, ob * 512: ob * 512 + 128], in1=rec[:], op=ALU.mult),
            reads=[otb, recb], writes=[S["oh_b"]])
    P.dma("sp", oT_d[hidx * 128:(hidx + 1) * 128, :], S["oh"][:], reads=[S["oh_b"]], writes=[oTd_b])


def emit_attn(P, C, I):
    hT = P.sbuf("a_hT", [128, KC, NKT * 128], BF16)
    hTb = P.buf()
    A, B, _ = emit_modprep(P, C, "am", I["gw"], I["sc"], I["sh"], None, 2)
    with P.scope():
        srcs = [(I["xh"][t * 128:(t + 1) * 128, :], 0, t * 128) for t in range(12)]
        srcs += [(I["ctx"][t * 128:(t + 1) * 128, :], 1, 1536 + t * 128) for t in range(2)]
        emit_norm_all(P, C, srcs, hT, hTb, A, B)
    oTd_b = P.buf("oTd")
    with P.scope():
        S = {"it": 0}
        S["ot"] = C.ps[:, 2 * 512: 4 * 512]
        S["otb"] = [C.psb[2], C.psb[3]]
        S["denb"] = [C.psb[2], C.psb[3]]
        C.rr_n = 2
        S["pe_sb"] = [P.sbuf("a_pe%d" % i, [128, 1024], BF16) for i in range(2)]
        S["pe_b"] = [P.buf() for i in range(2)]
        S["pm"] = [P.sbuf("a_pm%d" % i, [128, 640], BF16) for i in range(2)]
        S["pm_b"] = [P.buf() for i in range(2)]
        S["rec"] = [P.sbuf("a_rec%d" % i, [128, 128], F32) for i in range(2)]
        S["rec_b"] = [P.buf() for i in range(2)]
        S["oh"] = P.sbuf("a_oh", [128, NQT * 128], BF16)
        S["oh_b"] = P.buf()
        S["ones"] = P.sbuf("a_ones", [128, 128], BF16)
        S["ones_b"] = P.buf()
        P.op("dve", lambda e: e.memset(S["ones"][:], 1.0), writes=[S["ones_b"]])
        esink = P.sbuf("a_esink", [128, 8], F32)
        S["esink_b"] = P.buf()
        P.dma("sp", esink[:], I["sink"].partition_broadcast(128), writes=[S["esink_b"]])
        P.op("act", lambda e: e.activation(out=esink[:], in_=esink[:], func=AF.Exp), reads=[S["esink_b"]], writes=[S["esink_b"]])
        qT = P.sbuf("a_qT", [128, NQT * 128], BF16)
        qTb = P.buf()
        kT = P.sbuf("a_kT", [128, NKT * 128], BF16)
        kTb = P.buf()
        Vh = P.sbuf("a_Vh", [128, NKT, 128], BF16)
        Vhb = P.buf()
        nabf = P.sbuf("a_nabf", [128, 25 * 128], F32)
        nabfb = P.buf()
        Eh = P.sbuf("a_Eh", [128, 25 * 128], BF16)
        Ehb = P.buf()
        swm = P.sbuf("a_swm", [128, 9 * 128], BF16)
        swmb = P.buf()
        P.dma("pool", swm[:], I["swm"], writes=[swmb])
        q_ranges = [(0, 256, 512), (512, 768, 512), (1024, 1536, 256)]
        k_ranges = [(0, 0, 512), (512, 512, 512), (1024, 1024, 512), (1536, 1536, 256)]
        w_in = I["w_in"]

        def load_unit(c0):
            u, ub = C.next_wu()
            v = u[:, 0:KC * 512].rearrange("p (k c) -> p k c", k=KC)
            P.dma("pool", v, w_in[:, c0:c0 + 512].rearrange("(k p) c -> p k c", p=128), writes=[ub])
            return v, ub

        def cls_of(qi):
            return {0: 0, 1: 1, 6: 3, 7: 4}.get(qi, 2)

        def na_slots(qi):
            if qi < 8:
                return [qi + s for s in range(5)] + [12, 13], cls_of(qi) * 640, 5
            return [12, 13], None, 0

        def sw_slots(qi):
            if qi < 8:
                c = 0 if qi == 0 else (2 if qi == 7 else 1)
                return [qi + 1 + s for s in range(3)] + [12, 13], c * 384, 3
            return [12, 13], None, 0

        for hg in range(2):
            wq, wqb = load_unit(hg * 512)
            wk, wkb = load_unit(1024 + hg * 512)
            wvv, wvb = load_unit(2048 + hg * 512)
            for hh in range(4):
                h = hg * 4 + hh
                P.dma("sp", nabf[:], I["nab"][h], writes=[nabfb])
                P.op("act", lambda e: e.activation(out=Eh[:], in_=nabf[:], func=AF.Exp), reads=[nabfb], writes=[Ehb])
                emit_proj_fm(P, C, qT, qTb, wq, wqb, hh * 128, hT, hTb, q_ranges)
                emit_proj_fm(P, C, kT, kTb, wk, wkb, hh * 128, hT, hTb, k_ranges)
                emit_proj_v(P, C, Vh, Vhb, wvv, wvb, hh * 128, hT, hTb)
                emit_attn_core(P, C, S, h, qT, qTb, kT, kTb, Vh, Vhb, na_slots, Eh, Ehb, None, I["oT_d"], oTd_b)
        cosT = P.sbuf("a_cos", [128, 1536], F32)
        sinT = P.sbuf("a_sin", [128, 1536], F32)
        ropeb = P.buf()
        P.dma("sp", cosT[:], I["ropec"], writes=[ropeb])
        P.dma("sp", sinT[:], I["ropes"], writes=[ropeb])
        rperm = P.sbuf("a_rperm", [128, 128], F32)
        rpb_ = P.buf()
        P.dma("sp", rperm[:], I["rperm"], writes=[rpb_])
        qf = P.sbuf("a_qf", [128, 1792], F32)
        qfb = P.buf()
        t1 = P.sbuf("a_t1", [128, 512], F32)
        t1b = P.buf()
        t2 = P.sbuf("a_t2", [128, 512], F32)
        t2b = P.buf()

        def rope(dst, dstb, ncols_rot, tcol0, ncols_all):
            for c0 in range(0, ncols_rot, 512):
                n = min(512, ncols_rot - c0)
                bank = C.rr4()
                P.op("pe", lambda e, c0=c0, n=n, bank=bank: e.matmul(
                    C.ps[:, bank * 512: bank * 512 + n], lhsT=rperm[:], rhs=qf[:, c0:c0 + n], start=True, stop=True),
                    reads=[rpb_, qfb], writes=[C.psb[bank]])
                P.op("dve", lambda e, c0=c0, n=n: e.tensor_tensor(
                    out=t1[:, 0:n], in0=qf[:, c0:c0 + n], in1=cosT[:, tcol0 + c0: tcol0 + c0 + n], op=ALU.mult),
                    reads=[qfb, ropeb], writes=[t1b])
                P.op("dve", lambda e, c0=c0, n=n, bank=bank: e.tensor_tensor(
                    out=t2[:, 0:n], in0=C.ps[:, bank * 512: bank * 512 + n], in1=sinT[:, tcol0 + c0: tcol0 + c0 + n], op=ALU.mult),
                    reads=[C.psb[bank], ropeb], writes=[t2b])
                P.op("pool", lambda e, c0=c0, n=n: e.tensor_tensor(
                    out=dst[:, c0:c0 + n], in0=t1[:, 0:n], in1=t2[:, 0:n], op=ALU.add),
                    reads=[t1b, t2b], writes=[dstb])
            if ncols_all > ncols_rot:
                P.op("act", lambda e: e.activation(out=dst[:, ncols_rot:ncols_all], in_=qf[:, ncols_rot:ncols_all], func=AF.Copy),
                     reads=[qfb], writes=[dstb])

        wkv, wkvb = load_unit(4096)
        for g in range(2):
            wq, wqb = load_unit(3072 + g * 512)
            emit_proj_fm(P, C, qf, qfb, wkv, wkvb, g * 128, hT, hTb, k_ranges)
            rope(kT, kTb, 1536, 0, 1792)
            emit_proj_v(P, C, Vh, Vhb, wkv, wkvb, 256 + g * 128, hT, hTb)
            for hh in range(4):
                hq = g * 4 + hh
                emit_proj_fm(P, C, qf, qfb, wq, wqb, hh * 128, hT, hTb, q_ranges)
                rope(qT, qTb, 1024, 256, 1280)
                emit_attn_core(P, C, S, 8 + hq, qT, qTb, kT, kTb, Vh, Vhb, sw_slots, swm, swmb,
                               esink[:, hq:hq + 1], I["oT_d"], oTd_b)
    C.rr_n = 4
    return oTd_b


def emit_outproj(P, C, oT_d, oTd_b, xsrc_of, x1out, outb, w_out, G, ntile, cls_of_tile, name="op"):
    ncol = ntile * 128
    oT = P.sbuf(name + "_oT", [128, KC, ncol], BF16)
    oTb = P.buf()
    P.dma("sp", oT[:], oT_d.rearrange("(k p) t -> p k t", p=128), reads=[oTd_b], writes=[oTb])
    xs = [P.sbuf("%s_xs%d" % (name, i), [128, 512], F32) for i in range(2)]
    xsb = [P.buf() for i in range(2)]
    tm = [P.sbuf("%s_tm%d" % (name, i), [128, 512], F32) for i in range(2)]
    tmb = [P.buf() for i in range(2)]
    it = 0
    for cb in range(D // 512):
        u, ub = C.next_wu()
        wv = u[:, 0:KC * 512].rearrange("p (k c) -> p k c", k=KC)
        P.dma("pool", wv, w_out[:, cb * 512:(cb + 1) * 512].rearrange("(k p) c -> p k c", p=128), writes=[ub])
        for t in range(ntile):
            bank = C.rr4()
            for k in range(KC):
                P.op("pe", lambda e, k=k, t=t, bank=bank, wv=wv: e.matmul(
                    C.ps[:, bank * 512:(bank + 1) * 512], lhsT=oT[:, k, t * 128:(t + 1) * 128], rhs=wv[:, k, :],
                    start=(k == 0), stop=(k == KC - 1)), reads=[oTb, ub], writes=[C.psb[bank]])
            g, gb = G[cls_of_tile[t]]
            x_, xb_ = xs[it % 2], xsb[it % 2]
            t_, tb_ = tm[it % 2], tmb[it % 2]
            P.dma("sp", x_[:], xsrc_of(t)[:, cb * 512:(cb + 1) * 512], writes=[xb_])
            P.op("dve", lambda e, t_=t_, bank=bank, g=g, cb=cb: e.tensor_tensor(
                out=t_[:], in0=C.ps[:, bank * 512:(bank + 1) * 512], in1=g[:, cb * 512:(cb + 1) * 512], op=ALU.mult),
                reads=[C.psb[bank], gb], writes=[tb_])
            P.op("pool", lambda e, t_=t_, x_=x_: e.tensor_tensor(out=x_[:], in0=t_[:], in1=x_[:], op=ALU.add),
                 reads=[tb_, xb_], writes=[xb_])
            P.dma("sp", x1out[t * 128:(t + 1) * 128, cb * 512:(cb + 1) * 512], x_[:], reads=[xb_], writes=[outb])
            it += 1


GRID_W = 64


def host_consts():
    ident = np.eye(128, dtype=np.float32)
    rperm = np.zeros((128, 128), np.float32)
    for dp in range(128):
        partner = dp + 32 if (dp % 64) < 32 else dp - 32
        rperm[partner, dp] = 1.0
    return ident, rperm


def host_na_index(core):
    idx = np.full((128, 5, 5, 128), 465, np.int64)
    k = np.arange(128)
    q = np.arange(128)
    for ci, i in enumerate([0, 1, 3, 6, 7]):
        m = 8 * core + i
        for s in range(5):
            e = i + s
            Pk = 8 * core - 2 + e
            if core == 0 and e == 0:
                Pk = 3
            if core == 7 and e == 11:
                Pk = 60
            if Pk < 0 or Pk > 63:
                continue
            kr = (2 * Pk + k // 64)[:, None]
            kc = (k % 64)[:, None]
            qr = (2 * m + q // 64)[None, :]
            qc = (q % 64)[None, :]
            kr0 = np.clip(qr - 4, 0, 120)
            ws = np.clip(qc - 8, 0, 48)
            valid = (kr >= kr0) & (kr < kr0 + 8) & (kc >= ws) & (kc < ws + 16)
            ridx = kr - qr + 7
            cidx = np.clip(kc - qc + 15, 0, 30)
            flat = ridx * 31 + cidx
            idx[:, ci, s, :] = np.where(valid, flat, 465)
    return idx


def host_sw_mask(core):
    out = np.zeros((128, 3, 3, 128), np.float32)
    k = np.arange(128)[:, None]
    q = np.arange(128)[None, :]
    for ci, i in enumerate([0, 3, 7]):
        m = 8 * core + i
        for s in range(3):
            kp = (m - 1 + s) * 128 + k
            qp = m * 128 + q
            valid = (np.abs(kp - qp) <= 128) & (kp >= 0) & (kp < 8192)
            out[:, ci, s, :] = valid
    return out


def host_rope(core):
    tok = (8 * core - 2) * 128 + np.arange(1536)
    pos_r = (tok // GRID_W).astype(np.float32)
    pos_c = (tok % GRID_W).astype(np.float32)
    inv = (np.float32(10000.0) ** (-np.arange(32, dtype=np.float32) / np.float32(32))).astype(np.float32)
    cosT = np.zeros((128, 1536), np.float32)
    sinT = np.zeros((128, 1536), np.float32)
    for d in range(128):
        pos = pos_r if d < 64 else pos_c
        ang = (pos * inv[d % 32]).astype(np.float32)
        cosT[d] = np.cos(ang)
        sn = np.sin(ang)
        sinT[d] = -sn if (d % 64) < 32 else sn
    return cosT, sinT


def host_xh(x2d, core):
    out = np.zeros((1536, x2d.shape[1]), np.float32)
    for e in range(12):
        Pk = 8 * core - 2 + e
        if core == 0 and e == 0:
            Pk = 3
        if core == 7 and e == 11:
            Pk = 60
        if 0 <= Pk <= 63:
            out[e * 128:(e + 1) * 128] = x2d[Pk * 128:(Pk + 1) * 128]
    return out


GS = 128
NB = 32
NCH = 128
NCC = 32
I32 = mybir.dt.int32
PI = float(np.pi)


def s5_tt(P, eng, out, a, b, op, bufs):
    P.op(eng, lambda e: e.tensor_tensor(out=out, in0=a, in1=b, op=op), reads=bufs, writes=bufs[:1], relax=(eng == "dve"))


def s5_ts(P, eng, out, a, s1, op0, bufs, s2=None, op1=None):
    if op1 is None:
        P.op(eng, lambda e: e.tensor_scalar(out=out, in0=a, scalar1=s1, scalar2=None, op0=op0), reads=bufs, writes=bufs[:1], relax=(eng == "dve"))
    else:
        P.op(eng, lambda e: e.tensor_scalar(out=out, in0=a, scalar1=s1, scalar2=s2, op0=op0, op1=op1), reads=bufs, writes=bufs[:1],
             relax=(eng == "dve"))


def s5_cmul(P, eng, outr, outi, xr, xi, yr, yi, t1, t2, bufs, neg_i=False):
    s5_tt(P, eng, t1, xr, yr, ALU.mult, bufs)
    s5_tt(P, eng, t2, xi, yi, ALU.mult, bufs)
    s5_tt(P, eng, outr, t1, t2, ALU.subtract, bufs)
    s5_tt(P, eng, t1, xr, yi, ALU.mult, bufs)
    s5_tt(P, eng, t2, xi, yr, ALU.mult, bufs)
    if neg_i:
        s5_tt(P, eng, t1, t1, t2, ALU.add, bufs)
        s5_ts(P, eng, outi, t1, -1.0, ALU.mult, bufs)
    else:
        s5_tt(P, eng, outi, t1, t2, ALU.add, bufs)


def s5_horner(P, out, x, coefs, tmp, bufs):
    n = len(coefs) - 1
    s5_ts(P, "dve", out, x, float(coefs[n]), ALU.mult, bufs, s2=float(coefs[n - 1]), op1=ALU.add)
    for k in range(n - 2, -1, -1):
        s5_tt(P, "dve", tmp, out, x, ALU.mult, bufs)
        s5_ts(P, "dve", out, tmp, float(coefs[k]), ALU.add, bufs)


def s5_exp_small(P, out, x, tmp, bufs, sign=1.0):
    import math
    co = [sign ** k / math.factorial(k) for k in range(9)]
    s5_horner(P, out, x, co, tmp, bufs)


def s5_exp(P, out, x, y, tmp, bufs):
    import math
    s5_ts(P, "dve", y, x, 0.125, ALU.mult, bufs)
    co = [1.0 / math.factorial(k) for k in range(15)]
    s5_horner(P, out, y, co, tmp, bufs)
    for _ in range(3):
        s5_tt(P, "dve", out, out, out, ALU.mult, bufs)


def s5_sincos(P, sn, cs, th, ni, q, r, m, tmp, bufs):
    import math
    C1 = 6.28125
    C2 = 2 * math.pi - C1
    s5_ts(P, "dve", q, th, 1.0 / (2 * PI), ALU.mult, bufs)
    P.op("dve", lambda e: e.tensor_copy(out=ni, in_=q), reads=bufs, writes=bufs[:1])
    P.op("dve", lambda e: e.tensor_copy(out=q, in_=ni), reads=bufs, writes=bufs[:1])
    P.op("dve", lambda e: e.scalar_tensor_tensor(out=r, in0=q, scalar=-C1, in1=th, op0=ALU.mult, op1=ALU.add), reads=bufs, writes=bufs[:1])
    P.op("dve", lambda e: e.scalar_tensor_tensor(out=r, in0=q, scalar=-C2, in1=r, op0=ALU.mult, op1=ALU.add), reads=bufs, writes=bufs[:1])
    for thr, op, add in ((PI, ALU.is_gt, -1.0), (-PI, ALU.is_lt, 1.0)):
        s5_ts(P, "dve", m, r, thr, op, bufs)
        P.op("dve", lambda e, add=add: e.scalar_tensor_tensor(out=r, in0=m, scalar=add * C1, in1=r, op0=ALU.mult, op1=ALU.add),
             reads=bufs, writes=bufs[:1])
        P.op("dve", lambda e, add=add: e.scalar_tensor_tensor(out=r, in0=m, scalar=add * C2, in1=r, op0=ALU.mult, op1=ALU.add),
             reads=bufs, writes=bufs[:1])
    s5_ts(P, "dve", r, r, 0.25, ALU.mult, bufs)
    s5_tt(P, "dve", q, r, r, ALU.mult, bufs)
    sco = [(-1.0) ** k / math.factorial(2 * k + 1) for k in range(7)]
    cco = [(-1.0) ** k / math.factorial(2 * k) for k in range(8)]
    s5_horner(P, sn, q, sco, tmp, bufs)
    s5_tt(P, "dve", sn, sn, r, ALU.mult, bufs)
    s5_horner(P, cs, q, cco, tmp, bufs)
    for _ in range(2):
        s5_tt(P, "dve", tmp, sn, cs, ALU.mult, bufs)
        s5_tt(P, "dve", m, sn, sn, ALU.mult, bufs)
        s5_ts(P, "dve", sn, tmp, 2.0, ALU.mult, bufs)
        s5_ts(P, "dve", cs, m, -2.0, ALU.mult, bufs, s2=1.0, op1=ALU.add)


def emit_s5_params(P, C, I):
    pb = P.buf("s5p")
    B1 = [pb]

    def T(name, shape=(128, GS), dt=F32):
        return P.sbuf("s5_" + name, list(shape), dt)
    PERS = {}
    for nm, shp in (("bbr", (128, GS, 16)), ("bbi", (128, GS, 16)), ("ctr", (128, GS, 16)), ("cti", (128, GS, 16)),
                    ("pwAr", (128, GS, 9)), ("pwAi", (128, GS, 9)), ("pwBr", (128, GS, 9)), ("pwBi", (128, GS, 9)),
                    ("sBr", (128, GS)), ("sBi", (128, GS)), ("sCr", (128, GS)), ("sCi", (128, GS)),
                    ("L8r", (128, GS)), ("L8i", (128, GS)), ("A2", (128, 2, GS)), ("B2", (128, 2, GS))):
        PERS[nm] = T(nm, shp)
    with P.scope():
        return _emit_s5_params_inner(P, C, I, PERS, pb, B1, T)


def _emit_s5_params_inner(P, C, I, PERS, pb, B1, T0):
    def T(name, shape=(128, GS), dt=F32):
        if name in PERS:
            return PERS[name]
        return T0(name, shape, dt)
    are, aim, dt_ = T("are"), T("aim"), T("dt")
    an = T("anat")
    for src_, dst_ in ((I["a_re"], are), (I["a_im"], aim)):
        P.dma("sp", an[:].rearrange("g (d p) -> g d p", d=2), src_.rearrange("d g p -> g d p"), reads=B1, writes=B1)
        bank = C.rr4()
        P.op("pe", lambda e, bank=bank: e.transpose(out=C.ps[:, bank * 512: bank * 512 + 128], in_=an[:], identity=C.ident[:]),
             reads=B1 + [C.ident_b], writes=[C.psb[bank]])
        P.op("act", lambda e, dst_=dst_, bank=bank: e.activation(out=dst_[:], in_=C.ps[:, bank * 512: bank * 512 + 128], func=AF.Copy),
             reads=[C.psb[bank]], writes=B1)
    for d in range(2):
        sl = slice(d * 64, (d + 1) * 64)
        P.dma("sp", dt_[sl, :], I["log_dt"][d].partition_broadcast(64), writes=B1)
    q, r, m, tq = T("q"), T("r"), T("m"), T("tq")
    ldt = T("ldt")
    P.op("dve", lambda e: e.tensor_copy(out=ldt[:], in_=dt_[:]), reads=B1, writes=B1)
    s5_exp(P, dt_[:], ldt[:], q[:], tq[:], B1)
    ardt, th = T("ardt"), T("th")
    s5_tt(P, "dve", ardt[:], are[:], dt_[:], ALU.mult, B1)
    s5_tt(P, "dve", th[:], aim[:], dt_[:], ALU.mult, B1)
    mag, magi = T("mag"), T("magi")
    s5_exp_small(P, mag[:], ardt[:], tq[:], B1)
    s5_exp_small(P, magi[:], ardt[:], tq[:], B1, sign=-1.0)
    ni = T("ni", dt=I32)
    sn, cs = T("sn"), T("cs")
    s5_sincos(P, sn[:], cs[:], th[:], ni[:], q[:], r[:], m[:], tq[:], B1)
    lr, li, vr, vi = T("lr"), T("li"), T("vr"), T("vi")
    s5_tt(P, "dve", lr[:], mag[:], cs[:], ALU.mult, B1)
    s5_tt(P, "dve", li[:], mag[:], sn[:], ALU.mult, B1)
    s5_tt(P, "dve", vr[:], magi[:], cs[:], ALU.mult, B1)
    s5_tt(P, "dve", vi[:], magi[:], sn[:], ALU.mult, B1)
    s5_ts(P, "dve", vi[:], vi[:], -1.0, ALU.mult, B1)
    nr, den, cr_, ci_ = T("nr"), T("den"), T("cfr"), T("cfi")
    s5_ts(P, "dve", nr[:], lr[:], -1.0, ALU.add, B1)
    s5_tt(P, "dve", den[:], are[:], are[:], ALU.mult, B1)
    s5_tt(P, "dve", q[:], aim[:], aim[:], ALU.mult, B1)
    s5_tt(P, "dve", den[:], den[:], q[:], ALU.add, B1)
    P.op("dve", lambda e: e.reciprocal(out=den[:], in_=den[:]), reads=B1, writes=B1)
    s5_tt(P, "dve", cr_[:], nr[:], are[:], ALU.mult, B1)
    s5_tt(P, "dve", q[:], li[:], aim[:], ALU.mult, B1)
    s5_tt(P, "dve", cr_[:], cr_[:], q[:], ALU.add, B1)
    s5_tt(P, "dve", cr_[:], cr_[:], den[:], ALU.mult, B1)
    s5_tt(P, "dve", ci_[:], li[:], are[:], ALU.mult, B1)
    s5_tt(P, "dve", q[:], nr[:], aim[:], ALU.mult, B1)
    s5_tt(P, "dve", ci_[:], ci_[:], q[:], ALU.subtract, B1)
    s5_tt(P, "dve", ci_[:], ci_[:], den[:], ALU.mult, B1)
    br, bi = T("br", (128, GS, 16)), T("bi", (128, GS, 16))
    for d in range(2):
        sl = slice(d * 64, (d + 1) * 64)
        for gq in range(8):
            gs_ = slice(gq * 16, (gq + 1) * 16)
            P.dma("sp", br[sl, gs_, :], I["b_re"][d, gs_].rearrange("g p h -> p g h"), writes=B1)
            P.dma("sp", bi[sl, gs_, :], I["b_im"][d, gs_].rearrange("g p h -> p g h"), writes=B1)
    bbr, bbi = T("bbr", (128, GS, 16)), T("bbi", (128, GS, 16))
    t1, t2 = T("t1", (128, GS, 16)), T("t2", (128, GS, 16))
    cfr_b = cr_[:].unsqueeze(2).broadcast_to([128, GS, 16])
    cfi_b = ci_[:].unsqueeze(2).broadcast_to([128, GS, 16])
    s5_cmul(P, "dve", bbr[:], bbi[:], cfr_b, cfi_b, br[:], bi[:], t1[:], t2[:], B1)
    ctr, cti = T("ctr", (128, GS, 16)), T("cti", (128, GS, 16))
    cn = [T("cn%d" % i, (128, 128)) for i in range(2)]
    cnb = [P.buf() for i in range(2)]
    n = 0
    for src, dst in ((I["c_re"], ctr), (I["c_im"], cti)):
        for o in range(GS // 8):
            c_, cb_ = cn[n % 2], cnb[n % 2]
            P.dma("sp", c_[:].rearrange("q (d p) -> q d p", d=2),
                  src[:, o * 8:(o + 1) * 8].rearrange("d g h p -> (g h) d p"), writes=[cb_])
            bank = C.rr4()
            P.op("pe", lambda e, c_=c_, bank=bank: e.transpose(out=C.ps[:, bank * 512: bank * 512 + 128], in_=c_[:], identity=C.ident[:]),
                 reads=[cb_, C.ident_b], writes=[C.psb[bank]])
            P.op("act", lambda e, dst=dst, o=o, bank=bank: e.activation(
                out=dst[:, o * 8:(o + 1) * 8, :].rearrange("p g h -> p (g h)"), in_=C.ps[:, bank * 512: bank * 512 + 128], func=AF.Copy),
                reads=[C.psb[bank]], writes=B1)
            n += 1
    bAr, bAi, bBr, bBi = T("bAr"), T("bAi"), T("bBr"), T("bBi")
    lo, hi = slice(0, 64), slice(64, 128)
    for dst, a, b in ((bAr, vr, lr), (bAi, vi, li), (bBr, lr, vr), (bBi, li, vi)):
        P.op("dve", lambda e, dst=dst, a=a: e.tensor_copy(out=dst[lo, :], in_=a[lo, :]), reads=B1, writes=B1)
        P.op("dve", lambda e, dst=dst, b=b: e.tensor_copy(out=dst[hi, :], in_=b[hi, :]), reads=B1, writes=B1)
    pw = {}
    for nm, (xr, xi) in (("A", (bAr, bAi)), ("B", (bBr, bBi))):
        pr, pi_ = PERS["pw%sr" % nm], PERS["pw%si" % nm]
        P.op("dve", lambda e, pr=pr: e.memset(pr[:, :, 0], 1.0), reads=B1, writes=B1)
        P.op("dve", lambda e, pi_=pi_: e.memset(pi_[:, :, 0], 0.0), reads=B1, writes=B1)
        for k in range(1, 9):
            s5_cmul(P, "dve", pr[:, :, k], pi_[:, :, k], pr[:, :, k - 1], pi_[:, :, k - 1], xr[:], xi[:], q[:], m[:], B1)
        pw[nm] = (pr, pi_)
    sBr, sBi, sCr, sCi, L8r, L8i = T("sBr"), T("sBi"), T("sCr"), T("sCi"), T("L8r"), T("L8i")
    pAr, pAi = pw["A"]
    pBr, pBi = pw["B"]
    cp = lambda dst, sl, src: P.op("dve", lambda e: e.tensor_copy(out=dst[sl, :], in_=src), reads=B1, writes=B1)
    cp(sBr, lo, pBr[lo, :, 7]); cp(sBi, lo, pBi[lo, :, 7])
    P.op("dve", lambda e: e.memset(sBr[hi, :], 1.0), reads=B1, writes=B1)
    P.op("dve", lambda e: e.memset(sBi[hi, :], 0.0), reads=B1, writes=B1)
    cp(sCr, lo, pBr[lo, :, 1]); cp(sCi, lo, pBi[lo, :, 1])
    cp(sCr, hi, pAr[hi, :, 8]); cp(sCi, hi, pAi[hi, :, 8])
    cp(L8r, lo, pBr[lo, :, 8]); cp(L8i, lo, pBi[lo, :, 8])
    cp(L8r, hi, pAr[hi, :, 8]); cp(L8i, hi, pAi[hi, :, 8])
    A2, B2 = T("A2", (128, 2, GS)), T("B2", (128, 2, GS))
    for j in range(2):
        P.op("dve", lambda e, j=j: e.tensor_copy(out=A2[:, j, :], in_=L8r[:]), reads=B1, writes=B1)
    P.op("dve", lambda e: e.tensor_copy(out=B2[:, 1, :], in_=L8i[:]), reads=B1, writes=B1)
    s5_ts(P, "dve", B2[:, 0, :], L8i[:], -1.0, ALU.mult, B1)
    return dict(pb=pb, bbr=bbr, bbi=bbi, ctr=ctr, cti=cti, pw=pw, sBr=sBr, sBi=sBi, sCr=sCr, sCi=sCi,
                L8r=L8r, L8i=L8i, A2=A2, B2=B2)


def s5_recur(P, eng, psl, W, Wb, nsteps, ascending, A2v, B2v, pb, init, tA, tB, tb_):
    order = range(nsteps) if ascending else range(nsteps - 1, -1, -1)
    prev = init
    for c in order:
        cur = W[psl, c]
        if prev is not None:
            P.op(eng, lambda e, prev=prev: e.tensor_tensor(out=tA[psl], in0=prev[:, 0:2, :], in1=A2v[psl], op=ALU.mult),
                 reads=[tb_, Wb, pb], writes=[tb_], relax=True)
            P.op(eng, lambda e, prev=prev: e.tensor_tensor(out=tB[psl], in0=prev[:, 1:3, :], in1=B2v[psl], op=ALU.mult),
                 reads=[tb_, Wb, pb], writes=[tb_], relax=True)
            P.op(eng, lambda e: e.tensor_tensor(out=tA[psl], in0=tA[psl], in1=tB[psl], op=ALU.add), reads=[tb_], writes=[tb_], relax=True)
            P.op(eng, lambda e, cur=cur: e.tensor_tensor(out=cur[:, 0:2, :], in0=cur[:, 0:2, :], in1=tA[psl], op=ALU.add),
                 reads=[tb_, Wb], writes=[Wb], relax=True)
        P.op(eng, lambda e, cur=cur: e.tensor_copy(out=cur[:, 2, :], in_=cur[:, 0, :]), reads=[Wb], writes=[Wb], relax=True)
        prev = cur


def emit_s5a(P, C, I, hTd, hTdb):
    Q = emit_s5_params(P, C, I)
    pb = Q["pb"]
    lo, hi = slice(0, 64), slice(64, 128)
    outb = P.buf("s5a_out")
    P.dma("sp", I["L8d"][:, 0], Q["A2"][:], reads=[pb], writes=[outb])
    P.dma("sp", I["L8d"][:, 1], Q["B2"][:], reads=[pb], writes=[outb])
    sel = P.sbuf("s5_sel", [128, 64, 128], BF16)
    selb = P.buf()
    P.dma("pool", sel[:], I["sel"], writes=[selb])
    mf = P.sbuf("s5_mf", [128, 128], F32)
    mb_ = P.sbuf("s5_mb", [128, 128], F32)
    dv = P.sbuf("s5_dv", [128, GS], F32)
    cb_ = P.buf()
    P.dma("sp", mf[:], I["maskf"], writes=[cb_])
    P.dma("sp", mb_[:], I["maskb"], writes=[cb_])
    for j in range(8):
        P.dma("sp", dv[j * 16:(j + 1) * 16, :], I["ssm_d"].rearrange("(g h) -> h g", h=16), writes=[cb_], allow_slow_non_contiguous=True)
    identb = P.sbuf("s5_identb", [128, 128], BF16)
    identbb = P.buf()
    P.op("act", lambda e: e.activation(out=identb[:], in_=C.ident[:], func=AF.Copy), reads=[C.ident_b], writes=[identbb])
    MS = []
    for i in range(2):
        d_ = {nm: P.sbuf("s5_%s%d" % (nm, i), [128, 8, 128], BF16) for nm in ("BLr", "BLi", "CLr", "nCLi", "BcTr", "BcTi")}
        d_["Cc"] = P.sbuf("s5_Cc%d" % i, [128, 2, 8, 128], BF16)
        d_["b"] = P.buf()
        MS.append(d_)
    f32t = {nm: P.sbuf("s5_f_" + nm, [128, 8, 128], F32) for nm in ("Xr", "Xi", "t1", "t2")}
    fb = P.buf()
    W = P.sbuf("s5_W", [128, NCH, 3, NB], F32)
    Wb = [P.buf(), P.buf()]
    Wc = P.sbuf("s5_Wc", [128, NCC, 3, NB], F32)
    Wcb = [P.buf(), P.buf()]
    tA = [P.sbuf("s5_tA%d" % i, [128, 2, NB], F32) for i in range(2)]
    tB = [P.sbuf("s5_tB%d" % i, [128, 2, NB], F32) for i in range(2)]
    tb_ = [P.buf() for i in range(2)]
    Eo = P.sbuf("s5_Eo", [128, 2, GS], F32)
    Ec = P.sbuf("s5_Ec", [128, 2, GS], F32)
    Eb = [P.buf(), P.buf()]
    Tg = [P.sbuf("s5_Tg%d" % i, [128, 128], BF16) for i in range(2)]
    Tgb = [P.buf() for i in range(2)]
    tT = [P.sbuf("s5_tT%d" % i, [128, 128], F32) for i in range(2)]
    tTb = P.buf()
    Bc = [P.sbuf("s5_Bc%d" % i, [128, 2, 128], BF16) for i in range(2)]
    Bcb = [P.buf() for i in range(2)]
    Ug = [P.sbuf("s5_Ug%d" % i, [128, NCH + NCC], BF16) for i in range(2)]
    Ugb = [P.buf() for i in range(2)]
    Yi = [P.sbuf("s5_Yi%d" % i, [128, 128], F32) for i in range(2)]
    Yib = [P.buf() for i in range(2)]
    pAr, pAi = Q["pw"]["A"]
    pBr, pBi = Q["pw"]["B"]
    C.rr_n = 4
    hk = [P.sbuf("s5_hk%d" % i, [128, NCH * 8 + NCC * 8], BF16) for i in range(2)]
    hkb = [P.buf() for i in range(2)]
    for o in range(GS // 8):
        M = MS[o % 2]
        Mb = M["b"]
        g0 = o * 8
        P.dma("sp", hk[o % 2][:], hTd[o], reads=[hTdb], writes=[hkb[o % 2]])
        eng = "dve"
        bufs = [fb, pb, Mb]
        bc4 = lambda ap: ap.unsqueeze(2).broadcast_to([128, 8, 8, 16])
        pw4 = lambda ap: ap.unsqueeze(3).broadcast_to([128, 8, 8, 16])
        v4 = lambda t: t[:].rearrange("p g (j h) -> p g j h", j=8)
        sc3 = lambda ap: ap.unsqueeze(2).broadcast_to([128, 8, 128])
        s5_cmul(P, eng, v4(f32t["Xr"]), v4(f32t["Xi"]), bc4(Q["bbr"][:, g0:g0 + 8, :]), bc4(Q["bbi"][:, g0:g0 + 8, :]),
                pw4(pAr[:, g0:g0 + 8, 0:8]), pw4(pAi[:, g0:g0 + 8, 0:8]), v4(f32t["t1"]), v4(f32t["t2"]), bufs)
        P.op("act", lambda e, M=M: e.activation(out=M["BLr"][:], in_=f32t["Xr"][:], func=AF.Copy), reads=[fb], writes=[Mb])
        P.op("act", lambda e, M=M: e.activation(out=M["BLi"][:], in_=f32t["Xi"][:], func=AF.Copy), reads=[fb], writes=[Mb])
        s5_tt(P, eng, f32t["t1"][:], f32t["Xr"][:], sc3(Q["sBr"][:, g0:g0 + 8]), ALU.mult, bufs)
        s5_tt(P, eng, f32t["t2"][:], f32t["Xi"][:], sc3(Q["sBi"][:, g0:g0 + 8]), ALU.mult, bufs)
        P.op(eng, lambda e, M=M: e.tensor_tensor(out=M["BcTr"][:], in0=f32t["t1"][:], in1=f32t["t2"][:], op=ALU.subtract),
             reads=[fb], writes=[Mb])
        s5_tt(P, eng, f32t["t1"][:], f32t["Xr"][:], sc3(Q["sBi"][:, g0:g0 + 8]), ALU.mult, bufs)
        s5_tt(P, eng, f32t["t2"][:], f32t["Xi"][:], sc3(Q["sBr"][:, g0:g0 + 8]), ALU.mult, bufs)
        P.op(eng, lambda e, M=M: e.tensor_tensor(out=M["BcTi"][:], in0=f32t["t1"][:], in1=f32t["t2"][:], op=ALU.add),
             reads=[fb], writes=[Mb])
        s5_cmul(P, eng, v4(f32t["Xr"]), v4(f32t["Xi"]), bc4(Q["ctr"][:, g0:g0 + 8, :]), bc4(Q["cti"][:, g0:g0 + 8, :]),
                pw4(pBr[:, g0:g0 + 8, 0:8]), pw4(pBi[:, g0:g0 + 8, 0:8]), v4(f32t["t1"]), v4(f32t["t2"]), bufs)
        P.op("act", lambda e, M=M: e.activation(out=M["CLr"][:], in_=f32t["Xr"][:], func=AF.Copy), reads=[fb], writes=[Mb])
        P.op("act", lambda e, M=M: e.activation(out=M["nCLi"][:], in_=f32t["Xi"][:], func=AF.Copy, scale=-1.0), reads=[fb], writes=[Mb])
        s5_tt(P, eng, f32t["t1"][:], f32t["Xr"][:], sc3(Q["sCr"][:, g0:g0 + 8]), ALU.mult, bufs)
        s5_tt(P, eng, f32t["t2"][:], f32t["Xi"][:], sc3(Q["sCi"][:, g0:g0 + 8]), ALU.mult, bufs)
        P.op(eng, lambda e, M=M: e.tensor_tensor(out=M["Cc"][:, 0], in0=f32t["t1"][:], in1=f32t["t2"][:], op=ALU.subtract),
             reads=[fb], writes=[Mb])
        s5_tt(P, eng, f32t["t1"][:], f32t["Xr"][:], sc3(Q["sCi"][:, g0:g0 + 8]), ALU.mult, bufs)
        s5_tt(P, eng, f32t["t2"][:], f32t["Xi"][:], sc3(Q["sCr"][:, g0:g0 + 8]), ALU.mult, bufs)
        s5_tt(P, eng, f32t["t1"][:], f32t["t1"][:], f32t["t2"][:], ALU.add, bufs)
        P.op(eng, lambda e, M=M: e.tensor_scalar(out=M["Cc"][:, 1], in0=f32t["t1"][:], scalar1=-1.0, scalar2=None, op0=ALU.mult),
             reads=[fb], writes=[Mb])
        P.dma("sp", I["Ccd"][o], M["Cc"][:].rearrange("p a g m -> p (a g m)"), reads=[Mb], writes=[outb])
        ONLY = "TBUYV"
        for gg in range(8):
            g = g0 + gg
            gl = g % NB
            par = g % 2
            if "T" in ONLY:
                tbanks = (4, 5)
                for half, sl in enumerate((lo, hi)):
                    bank = tbanks[half]
                    P.op("pe", lambda e, M=M, gg=gg, sl=sl, bank=bank: e.matmul(
                        C.ps[:, bank * 512: bank * 512 + 128], lhsT=M["BLr"][sl, gg, :], rhs=M["CLr"][sl, gg, :], start=True, stop=False),
                        reads=[Mb], writes=[C.psb[bank]])
                    P.op("pe", lambda e, M=M, gg=gg, sl=sl, bank=bank: e.matmul(
                        C.ps[:, bank * 512: bank * 512 + 128], lhsT=M["BLi"][sl, gg, :], rhs=M["nCLi"][sl, gg, :], start=False, stop=True),
                        reads=[Mb], writes=[C.psb[bank]])
                P.op("dve", lambda e: e.tensor_tensor(out=tT[0][:], in0=C.ps[:, 4 * 512: 4 * 512 + 128], in1=mf[:], op=ALU.mult),
                     reads=[C.psb[4], cb_], writes=[tTb])
                P.op("dve", lambda e: e.tensor_tensor(out=tT[1][:], in0=C.ps[:, 5 * 512: 5 * 512 + 128], in1=mb_[:], op=ALU.mult),
                     reads=[C.psb[5], cb_], writes=[tTb])
                P.op("dve", lambda e: e.tensor_tensor(out=tT[0][:], in0=tT[0][:], in1=tT[1][:], op=ALU.add), reads=[tTb], writes=[tTb])
                P.op("dve", lambda e, g=g, par=par: e.scalar_tensor_tensor(out=Tg[par][:], in0=C.ident[:], scalar=dv[:, g:g + 1], in1=tT[0][:],
                                                                      op0=ALU.mult, op1=ALU.add),
                     reads=[tTb, cb_, C.ident_b], writes=[Tgb[par]])
            if "B" in ONLY:
                bank = C.rr4()
                for a_, nm in enumerate(("BcTr", "BcTi")):
                    P.op("pe", lambda e, M=M, nm=nm, gg=gg, a_=a_, bank=bank: e.transpose(
                        out=C.ps[:, bank * 512 + a_ * 64: bank * 512 + a_ * 64 + 64].bitcast(BF16), in_=M[nm][:, gg, :], identity=identb[:]),
                        reads=[Mb, identbb], writes=[C.psb[bank]])
                P.op("act", lambda e, par=par, bank=bank: e.activation(
                    out=Bc[par][:].rearrange("p a m -> p (a m)"), in_=C.ps[:, bank * 512: bank * 512 + 128].bitcast(BF16), func=AF.Copy),
                    reads=[C.psb[bank]], writes=[Bcb[par]])
            if "U" in ONLY:
                bank = C.rr4()
                hv = hk[o % 2][:].rearrange("p (c j) -> p j c", j=8)
                hTb = hkb[o % 2]
                for j in range(8):
                    P.op("pe", lambda e, j=j, gg=gg, bank=bank, hv=hv: e.matmul(
                        C.ps[:, bank * 512: bank * 512 + NCH + NCC], lhsT=sel[:, gg * 8 + j, :], rhs=hv[:, j, :], start=(j == 0), stop=(j == 7)),
                        reads=[selb, hTb], writes=[C.psb[bank]])
                P.op("act", lambda e, par=par, bank=bank: e.activation(out=Ug[par][:], in_=C.ps[:, bank * 512: bank * 512 + NCH + NCC], func=AF.Copy),
                     reads=[C.psb[bank]], writes=[Ugb[par]])
            if "Y" in ONLY:
                bank = C.rr4()
                P.op("pe", lambda e, par=par, bank=bank: e.matmul(C.ps[:, bank * 512: bank * 512 + NCH], lhsT=Tg[par][:], rhs=Ug[par][:, 0:NCH],
                                                                start=True, stop=True), reads=[Tgb[par], Ugb[par]], writes=[C.psb[bank]])
                P.op("act", lambda e, par=par, bank=bank: e.activation(out=Yi[par][:], in_=C.ps[:, bank * 512: bank * 512 + NCH], func=AF.Copy),
                     reads=[C.psb[bank]], writes=[Yib[par]])
                P.dma("sp", I["Yd"][g], Yi[par][:], reads=[Yib[par]], writes=[outb])
            if "V" in ONLY:
                for a_ in range(2):
                    bank = C.rr4()
                    P.op("pe", lambda e, par=par, a_=a_, bank=bank: e.matmul(C.ps[:, bank * 512: bank * 512 + NCH + NCC], lhsT=Bc[par][:, a_, :],
                                                                           rhs=Ug[par][:], start=True, stop=True),
                         reads=[Bcb[par], Ugb[par]], writes=[C.psb[bank]])
                    P.op("dve", lambda e, a_=a_, gl=gl, bank=bank: e.tensor_copy(out=W[:, :, a_, gl], in_=C.ps[:, bank * 512: bank * 512 + NCH]),
                         reads=[C.psb[bank]], writes=Wb)
                    P.op("act", lambda e, a_=a_, gl=gl, bank=bank: e.activation(out=Wc[:, :, a_, gl], in_=C.ps[:, bank * 512 + NCH: bank * 512 + NCH + NCC],
                                                                              func=AF.Copy),
                         reads=[C.psb[bank]], writes=Wcb)
            if gl == NB - 1:
                bi_ = g // NB
                gb0 = bi_ * NB
                P.dma("sp", I["Vd"][bi_], W[:].rearrange("p c s g -> p (c s g)"), reads=Wb, writes=[outb])
                A2v = Q["A2"][:, :, gb0:gb0 + NB]
                B2v = Q["B2"][:, :, gb0:gb0 + NB]
                s5_recur(P, "dve", lo, W, Wb[0], NCH, True, A2v, B2v, pb, None, tA[0], tB[0], tb_[0])
                s5_recur(P, "pool", hi, W, Wb[1], NCH, False, A2v, B2v, pb, None, tA[1], tB[1], tb_[1])
                s5_recur(P, "dve", lo, Wc, Wcb[0], NCC, True, A2v, B2v, pb, None, tA[0], tB[0], tb_[0])
                s5_recur(P, "pool", hi, Wc, Wcb[1], NCC, False, A2v, B2v, pb, None, tA[1], tB[1], tb_[1])
                P.op("dve", lambda e, gb0=gb0: e.tensor_copy(out=Eo[lo, :, gb0:gb0 + NB], in_=W[lo, NCH - 1, 0:2, :]), reads=[Wb[0]], writes=[Eb[0]])
                P.op("pool", lambda e, gb0=gb0: e.tensor_copy(out=Eo[hi, :, gb0:gb0 + NB], in_=W[hi, 0, 0:2, :]), reads=[Wb[1]], writes=[Eb[1]])
                P.op("dve", lambda e, gb0=gb0: e.tensor_copy(out=Ec[lo, :, gb0:gb0 + NB], in_=Wc[lo, NCC - 1, 0:2, :]), reads=[Wcb[0]], writes=[Eb[0]])
                P.op("pool", lambda e, gb0=gb0: e.tensor_copy(out=Ec[hi, :, gb0:gb0 + NB], in_=Wc[hi, 0, 0:2, :]), reads=[Wcb[1]], writes=[Eb[1]])
    P.dma("sp", I["Eown"], Eo[:].rearrange("p a g -> p (a g)"), reads=Eb, writes=[outb])
    P.dma("sp", I["Ectx"], Ec[:].rearrange("p a g -> p (a g)"), reads=Eb, writes=[outb])
    return outb


def host_s5_consts():
    sel = np.zeros((128, 64, 128), np.float32)
    for gm in range(8):
        for j in range(8):
            for h in range(16):
                sel[16 * gm + h, gm * 8 + j, j * 16 + h] = 1.0
    selT = np.ascontiguousarray(sel.transpose(2, 1, 0))
    jj = np.arange(128) // 16
    maskf = (jj[None, :] >= jj[:, None]).astype(np.float32)
    maskb = (jj[:, None] >= jj[None, :]).astype(np.float32)
    return sel, selT, maskf, maskb


def emit_s5b(P, C, I, gyT, gyTb):
    lo, hi = slice(0, 64), slice(64, 128)
    pb = P.buf("s5b_p")
    B1 = [pb]

    def T(name, shape=(128, GS), dt=F32):
        return P.sbuf("s5b_" + name, list(shape), dt)
    A2, B2 = T("A2", (128, 2, GS)), T("B2", (128, 2, GS))
    P.dma("sp", A2[:], I["L8d"][:, 0], writes=B1)
    P.dma("sp", B2[:], I["L8d"][:, 1], writes=B1)
    Lr, Li, t1, t2, nr_, ni_ = T("Lr"), T("Li"), T("t1"), T("t2"), T("nr"), T("ni")
    P.op("dve", lambda e: e.tensor_copy(out=Lr[:], in_=A2[:, 0, :]), reads=B1, writes=B1)
    P.op("dve", lambda e: e.tensor_copy(out=Li[:], in_=B2[:, 1, :]), reads=B1, writes=B1)
    for _ in range(7):
        s5_cmul(P, "dve", nr_[:], ni_[:], Lr[:], Li[:], Lr[:], Li[:], t1[:], t2[:], B1)
        P.op("dve", lambda e: e.tensor_copy(out=Lr[:], in_=nr_[:]), reads=B1, writes=B1)
        P.op("dve", lambda e: e.tensor_copy(out=Li[:], in_=ni_[:]), reads=B1, writes=B1)
    init3 = T("init3", (128, 3, GS))
    El = T("El", (128, 2, GS))
    P.dma("sp", init3[:, 0:2, :], I["Elist"][0].rearrange("p (a g) -> p a g", a=2), writes=B1)
    for s_ in range(1, 9):
        P.dma("sp", El[:], I["Elist"][s_].rearrange("p (a g) -> p a g", a=2), reads=B1, writes=B1)
        s5_cmul(P, "dve", nr_[:], ni_[:], init3[:, 0, :], init3[:, 1, :], Lr[:], Li[:], t1[:], t2[:], B1)
        s5_tt(P, "dve", init3[:, 0, :], nr_[:], El[:, 0, :], ALU.add, B1)
        s5_tt(P, "dve", init3[:, 1, :], ni_[:], El[:, 1, :], ALU.add, B1)
    P.op("dve", lambda e: e.tensor_copy(out=init3[:, 2, :], in_=init3[:, 0, :]), reads=B1, writes=B1)
    selT = P.sbuf("s5b_selT", [128, 64, 128], BF16)
    selTb = P.buf()
    P.dma("pool", selT[:], I["selT"], writes=[selTb])
    W = P.sbuf("s5b_W", [128, NCH, 3, NB], F32)
    Wb = [P.buf(), P.buf()]
    Inb = P.sbuf("s5b_Inb", [128, 2, NB, NCH], BF16)
    Inbb = P.buf()
    tA = [P.sbuf("s5b_tA%d" % i, [128, 2, NB], F32) for i in range(2)]
    tB = [P.sbuf("s5b_tB%d" % i, [128, 2, NB], F32) for i in range(2)]
    tb_ = [P.buf() for i in range(2)]
    Cc = [P.sbuf("s5b_Cc%d" % i, [128, 2, 8, 128], BF16) for i in range(2)]
    Ccb = [P.buf() for i in range(2)]
    Yi = [P.sbuf("s5b_Yi%d" % i, [128, 128], F32) for i in range(2)]
    Yib = [P.buf() for i in range(2)]
    Yo = [P.sbuf("s5b_Yo%d" % i, [128, 8, 128], BF16) for i in range(2)]
    Yob = [P.buf() for i in range(2)]
    g1_ = P.sbuf("s5b_g1", [128, 512], F32)
    g2_ = P.sbuf("s5b_g2", [128, 512], F32)
    gb_ = P.buf()
    C.rr_n = 4
    for bi_ in range(GS // NB):
        gb0 = bi_ * NB
        P.dma("sp", W[:].rearrange("p c s g -> p (c s g)"), I["Vd"][bi_], writes=Wb)
        A2v = A2[:, :, gb0:gb0 + NB]
        B2v = B2[:, :, gb0:gb0 + NB]
        iv = init3[:, :, gb0:gb0 + NB]
        s5_recur(P, "dve", lo, W, Wb[0], NCH, True, A2v, B2v, pb, iv[lo], tA[0], tB[0], tb_[0])
        s5_recur(P, "pool", hi, W, Wb[1], NCH, False, A2v, B2v, pb, iv[hi], tA[1], tB[1], tb_[1])
        for ri in range(2):
            P.op("dve", lambda e, ri=ri: e.tensor_copy(out=Inb[lo, ri, :, 1:NCH], in_=W[lo, 0:NCH - 1, ri, :].rearrange("p c g -> p g c")),
                 reads=[Wb[0]], writes=[Inbb])
            P.op("dve", lambda e, ri=ri, iv=iv: e.tensor_copy(out=Inb[lo, ri, :, 0], in_=iv[lo, ri, :]), reads=B1, writes=[Inbb])
            P.op("pool", lambda e, ri=ri: e.tensor_copy(out=Inb[hi, ri, :, 0:NCH - 1], in_=W[hi, 1:NCH, ri, :].rearrange("p c g -> p g c")),
                 reads=[Wb[1]], writes=[Inbb])
            P.op("pool", lambda e, ri=ri, iv=iv: e.tensor_copy(out=Inb[hi, ri, :, NCH - 1], in_=iv[hi, ri, :]), reads=B1, writes=[Inbb])
        for oo in range(NB // 8):
            o = bi_ * (NB // 8) + oo
            cc, ccb = Cc[o % 2], Ccb[o % 2]
            yo, yob = Yo[o % 2], Yob[o % 2]
            P.dma("sp", cc[:].rearrange("p a g m -> p (a g m)"), I["Ccd"][o], writes=[ccb])
            for gg in range(8):
                g = o * 8 + gg
                gl = g % NB
                yi, yib = Yi[g % 2], Yib[g % 2]
                P.dma("sp", yi[:], I["Yd"][g], writes=[yib])
                bank = C.rr4()
                for a_ in range(2):
                    P.op("pe", lambda e, a_=a_, gg=gg, gl=gl, cc=cc, bank=bank: e.matmul(
                        C.ps[:, bank * 512: bank * 512 + NCH], lhsT=cc[:, a_, gg, :], rhs=Inb[:, a_, gl, :], start=(a_ == 0), stop=(a_ == 1)),
                        reads=[ccb, Inbb], writes=[C.psb[bank]])
                P.op("dve", lambda e, gg=gg, yo=yo, yi=yi, bank=bank: e.tensor_tensor(
                    out=yo[:, gg, :], in0=C.ps[:, bank * 512: bank * 512 + NCH], in1=yi[:], op=ALU.add),
                    reads=[C.psb[bank], yib], writes=[yob])
            for half in range(2):
                bank = 4 + half
                first = True
                for gg in range(8):
                    for i in range(8):
                        outv = C.ps[:, bank * 512:(bank + 1) * 512].rearrange("p (c i) -> p i c", i=8)[:, i, :]
                        P.op("pe", lambda e, gg=gg, i=i, half=half, outv=outv, yo=yo, first=first: e.matmul(
                            outv, lhsT=selT[:, gg * 8 + i, :], rhs=yo[:, gg, half * 64:(half + 1) * 64],
                            start=first, stop=(gg == 7 and i == 7), skip_group_check=True),
                            reads=[selTb, yob], writes=[C.psb[bank]])
                        first = False
                xp = C.ps[:, bank * 512:(bank + 1) * 512]
                P.op("act", lambda e, xp=xp: e.activation(out=g1_[:], in_=xp, func=AF.Square), reads=[C.psb[bank]], writes=[gb_])
                P.op("dve", lambda e: e.tensor_scalar(out=g1_[:], in0=g1_[:], scalar1=0.044715, scalar2=1.0, op0=ALU.mult, op1=ALU.add),
                     reads=[gb_], writes=[gb_])
                P.op("dve", lambda e, xp=xp: e.tensor_tensor(out=g2_[:], in0=g1_[:], in1=xp, op=ALU.mult), reads=[gb_, C.psb[bank]], writes=[gb_])
                P.op("act", lambda e: e.activation(out=g2_[:], in_=g2_[:], func=AF.Sigmoid, scale=1.5957691216057308), reads=[gb_], writes=[gb_])
                P.op("dve", lambda e, xp=xp, o=o, half=half: e.tensor_tensor(out=gyT[:, o, half * 512:(half + 1) * 512], in0=g2_[:], in1=xp, op=ALU.mult),
                     reads=[gb_, C.psb[bank]], writes=[gyTb])


def emit_glu(P, C, gyT, gyTb, xin, xout, outb, w_glu, b_glu, gate_ap, ntile=8):
    bt = P.sbuf("gl_bt", [128, 2 * D], F32)
    gt = P.sbuf("gl_gt", [128, D], F32)
    tb = P.buf()
    P.dma("sp", bt[:], b_glu.partition_broadcast(128), writes=[tb])
    P.dma("sp", gt[:], gate_ap.partition_broadcast(128), writes=[tb])
    xs = [P.sbuf("gl_xs%d" % i, [128, 512], F32) for i in range(2)]
    xsb = [P.buf() for i in range(2)]
    sv = [P.sbuf("gl_sv%d" % i, [128, 512], F32) for i in range(2)]
    sg = [P.sbuf("gl_sg%d" % i, [128, 512], F32) for i in range(2)]
    svb = [P.buf() for i in range(2)]
    C.rr_n = 4
    it = 0
    for cb in range(D // 512):
        uv, uvb = C.next_wu()
        ug, ugb = C.next_wu()
        wv = uv[:, 0:KC * 512].rearrange("p (k c) -> p k c", k=KC)
        wg = ug[:, 0:KC * 512].rearrange("p (k c) -> p k c", k=KC)
        P.dma("pool", wv, w_glu[:, cb * 512:(cb + 1) * 512].rearrange("(k p) c -> p k c", p=128), writes=[uvb])
        P.dma("pool", wg, w_glu[:, D + cb * 512: D + (cb + 1) * 512].rearrange("(k p) c -> p k c", p=128), writes=[ugb])
        for t in range(ntile):
            bv = C.rr4()
            bg = C.rr4()
            for k in range(KC):
                P.op("pe", lambda e, k=k, t=t, bv=bv, wv=wv: e.matmul(C.ps[:, bv * 512:(bv + 1) * 512], lhsT=gyT[:, k, t * 128:(t + 1) * 128],
                                                                    rhs=wv[:, k, :], start=(k == 0), stop=(k == KC - 1)),
                     reads=[gyTb, uvb], writes=[C.psb[bv]])
            for k in range(KC):
                P.op("pe", lambda e, k=k, t=t, bg=bg, wg=wg: e.matmul(C.ps[:, bg * 512:(bg + 1) * 512], lhsT=gyT[:, k, t * 128:(t + 1) * 128],
                                                                    rhs=wg[:, k, :], start=(k == 0), stop=(k == KC - 1)),
                     reads=[gyTb, ugb], writes=[C.psb[bg]])
            x_, xb_ = xs[it % 2], xsb[it % 2]
            v_, g_, vb_ = sv[it % 2], sg[it % 2], svb[it % 2]
            P.dma("sp", x_[:], xin[t * 128:(t + 1) * 128, cb * 512:(cb + 1) * 512], writes=[xb_])
            P.op("dve", lambda e, g_=g_, bg=bg, cb=cb: e.tensor_tensor(out=g_[:], in0=C.ps[:, bg * 512:(bg + 1) * 512],
                                                                    in1=bt[:, D + cb * 512: D + (cb + 1) * 512], op=ALU.add),
                 reads=[C.psb[bg], tb], writes=[vb_])
            P.op("act", lambda e, g_=g_: e.activation(out=g_[:], in_=g_[:], func=AF.Sigmoid), reads=[vb_], writes=[vb_])
            P.op("dve", lambda e, v_=v_, bv=bv, cb=cb: e.tensor_tensor(out=v_[:], in0=C.ps[:, bv * 512:(bv + 1) * 512],
                                                                    in1=bt[:, cb * 512:(cb + 1) * 512], op=ALU.add),
                 reads=[C.psb[bv], tb], writes=[vb_])
            P.op("pool", lambda e, v_=v_, g_=g_: e.tensor_tensor(out=v_[:], in0=v_[:], in1=g_[:], op=ALU.mult), reads=[vb_], writes=[vb_])
            P.op("pool", lambda e, v_=v_, cb=cb: e.tensor_tensor(out=v_[:], in0=v_[:], in1=gt[:, cb * 512:(cb + 1) * 512], op=ALU.mult),
                 reads=[vb_, tb], writes=[vb_])
            P.op("pool", lambda e, v_=v_, x_=x_: e.tensor_tensor(out=x_[:], in0=v_[:], in1=x_[:], op=ALU.add), reads=[vb_, xb_], writes=[xb_])
            P.dma("sp", xout[t * 128:(t + 1) * 128, cb * 512:(cb + 1) * 512], x_[:], reads=[xb_], writes=[outb])
            it += 1


NMOD = 6 * D


def emit_mod(P, C, c_ap, cctx_ap, ada_w, ada_b, mod_d, modb):
    cf = P.sbuf("md_cf", [128, 2, KC], F32)
    cfb = P.buf()
    load_fm(P, cf[:, 0, :], cfb, c_ap)
    load_fm(P, cf[:, 1, :], cfb, cctx_ap)
    P.op("act", lambda e: e.activation(out=cf[:], in_=cf[:], func=AF.Silu), reads=[cfb], writes=[cfb])
    cT = P.sbuf("md_cT", [128, KC, 2], BF16)
    cTb = P.buf()
    P.op("dve", lambda e: e.tensor_copy(out=cT[:], in_=cf[:].rearrange("p v k -> p k v")), reads=[cfb], writes=[cTb])
    bt = [P.sbuf("md_bt%d" % i, [2, 512], F32) for i in range(2)]
    btb = [P.buf() for i in range(2)]
    ob = [P.sbuf("md_o%d" % i, [2, 512], F32) for i in range(2)]
    obb = [P.buf() for i in range(2)]
    C.rr_n = 4
    it = 0
    for L in range(2):
        for nb in range(NMOD // 512):
            u, ub = C.next_wu()
            wv = u[:, 0:KC * 512].rearrange("p (k c) -> p k c", k=KC)
            P.dma("pool", wv, ada_w[L][:, nb * 512:(nb + 1) * 512].rearrange("(k p) c -> p k c", p=128), writes=[ub])
            bank = C.rr4()
            for k in range(KC):
                P.op("pe", lambda e, k=k, bank=bank, wv=wv: e.matmul(C.ps[0:2, bank * 512:(bank + 1) * 512], lhsT=cT[:, k, :], rhs=wv[:, k, :],
                                                                   start=(k == 0), stop=(k == KC - 1)), reads=[cTb, ub], writes=[C.psb[bank]])
            o_, ob_ = ob[it % 2], obb[it % 2]
            b_, bb_ = bt[it % 2], btb[it % 2]
            P.dma("sp", b_[:], ada_b[L, nb * 512:(nb + 1) * 512].partition_broadcast(2), writes=[bb_])
            P.op("dve", lambda e, o_=o_, bank=bank, b_=b_: e.tensor_tensor(out=o_[:], in0=C.ps[0:2, bank * 512:(bank + 1) * 512],
                                                                         in1=b_[:], op=ALU.add),
                 reads=[C.psb[bank], bb_], writes=[ob_])
            P.dma("sp", mod_d[L, :, nb * 512:(nb + 1) * 512], o_[:], reads=[ob_], writes=[modb])
            it += 1


def mod_slices(mod_d, L, which):
    return [mod_d[L, v, which * D:(which + 1) * D] for v in range(2)]


def emit_final_norm(P, C, xin, xinb, xout, outb, gw_ap, ntile=8):
    gt = P.sbuf("fn_g", [128, D], F32)
    gtb = P.buf()
    P.dma("sp", gt[:], gw_ap.partition_broadcast(128), writes=[gtb])
    xt = [P.sbuf("fn_x%d" % i, [128, D], F32) for i in range(2)]
    xtb = [P.buf() for i in range(2)]
    xn = [P.sbuf("fn_n%d" % i, [128, D], F32) for i in range(2)]
    xnb = [P.buf() for i in range(2)]
    ssq = [P.sbuf("fn_s%d" % i, [128, 1], F32) for i in range(2)]
    for t in range(ntile):
        x_, xb_, n_, nb_, s_ = xt[t % 2], xtb[t % 2], xn[t % 2], xnb[t % 2], ssq[t % 2]
        P.dma("sp", x_[:], xin[t * 128:(t + 1) * 128, :], reads=[xinb], writes=[xb_])
        P.op("act", lambda e, x_=x_, n_=n_, s_=s_: e.activation(out=n_[:], in_=x_[:], func=AF.Square, accum_out=s_[:]), reads=[xb_], writes=[nb_])
        P.op("dve", lambda e, s_=s_: e.tensor_scalar(out=s_[:], in0=s_[:], scalar1=1.0 / D, scalar2=EPS, op0=ALU.mult, op1=ALU.add),
             reads=[nb_], writes=[nb_])
        P.op("act", lambda e, s_=s_: e.activation(out=s_[:], in_=s_[:], func=AF.Sqrt), reads=[nb_], writes=[nb_])
        P.op("dve", lambda e, s_=s_: e.reciprocal(out=s_[:], in_=s_[:]), reads=[nb_], writes=[nb_])
        P.op("act", lambda e, x_=x_, n_=n_, s_=s_: e.activation(out=n_[:], in_=x_[:], func=AF.Copy, scale=s_[:]), reads=[xb_, nb_], writes=[nb_])
        P.op("dve", lambda e, n_=n_: e.tensor_tensor(out=n_[:], in0=n_[:], in1=gt[:], op=ALU.mult), reads=[nb_, gtb], writes=[nb_])
        P.dma("sp", xout[t * 128:(t + 1) * 128, :], n_[:], reads=[nb_], writes=[outb])


S5_KEYS = ("a_re", "a_im", "log_dt", "b_re", "b_im", "c_re", "c_im")


def build_launch1():
    P = Prog()
    nc = P.nc
    P.outb = P.buf("out")

    def inp(name, shape):
        return nc.dram_tensor(name, list(shape), F32, kind="ExternalInput").ap()

    def outp(name, shape, dt=F32, kind="ExternalOutput"):
        return nc.dram_tensor(name, list(shape), dt, kind=kind).ap()
    I = dict(ident=inp("ident", [128, 128]), rperm=inp("rperm", [128, 128]), xh=inp("xh", [1536, D]), ctx=inp("ctx", [256, D]),
             c=inp("c", [D]), c_ctx=inp("c_ctx", [D]), ada_w=inp("ada_w", [2, D, NMOD]), ada_b=inp("ada_b", [2, NMOD]),
             norm_mix=inp("norm_mix", [2, D]), norm_ffn=inp("norm_ffn", [2, D]),
             w_in=inp("w_in", [D, 4608]), w_out=inp("w_out", [D, D]), nab=inp("nab", [8, 128, 3200]),
             swm=inp("swm", [128, 1152]), ropec=inp("ropec", [128, 1536]), ropes=inp("ropes", [128, 1536]), sink=inp("sink", [8]),
             w1=inp("w1", [D, DFF]), w3=inp("w3", [D, DFF]), w2=inp("w2", [DFF, D]),
             a_re=inp("a_re", [2, 128, 64]), a_im=inp("a_im", [2, 128, 64]), log_dt=inp("log_dt", [2, 128]),
             b_re=inp("b_re", [2, 128, 64, 16]), b_im=inp("b_im", [2, 128, 64, 16]),
             c_re=inp("c_re", [2, 128, 16, 64]), c_im=inp("c_im", [2, 128, 16, 64]),
             ssm_d=inp("ssm_d", [D]), sel=inp("sel", [128, 64, 128]), maskf=inp("maskf", [128, 128]), maskb=inp("maskb", [128, 128]))
    I["Vd"] = outp("Vd", [4, 128, NCH * 3 * NB])
    I["Yd"] = outp("Yd", [GS, 128, 128])
    I["Ccd"] = outp("Ccd", [16, 128, 2048], BF16)
    I["Eown"] = outp("Eown", [128, 2 * GS])
    I["Ectx"] = outp("Ectx", [128, 2 * GS])
    I["L8d"] = outp("L8d", [128, 2, 2, GS])
    mod_d = outp("mod_d", [2, 2, NMOD])
    x2 = outp("x2", [1280, D])
    I["oT_d"] = outp("oT_d", [2048, 1280], BF16, kind="Internal")
    x1 = outp("x1", [1280, D], kind="Internal")
    hTd = outp("hTd", [KC, 128, 1280], BF16, kind="Internal")
    modb = P.buf("modd")
    C = Ctx(P, I["ident"])
    with P.scope():
        C.alloc_wu(3)
        emit_mod(P, C, I["c"], I["c_ctx"], I["ada_w"], I["ada_b"], mod_d, modb)
    I["gw"] = I["norm_mix"][0]
    I["sc"] = mod_slices(mod_d, 0, 1)
    I["sh"] = mod_slices(mod_d, 0, 0)
    with P.scope():
        C.alloc_wu(3)
        oTd_b = emit_attn(P, C, I)
    x1b = P.buf("x1")
    with P.scope():
        C.alloc_wu(3)
        _, _, G = emit_modprep(P, C, "om", I["gw"], I["sc"], I["sh"], mod_slices(mod_d, 0, 2), 2)
        xsrc = lambda t: (I["xh"][256 + t * 128: 256 + (t + 1) * 128, :] if t < 8 else I["ctx"][(t - 8) * 128:(t - 7) * 128, :])
        emit_outproj(P, C, I["oT_d"], oTd_b, xsrc, x1, x1b, I["w_out"], G, 10, [0] * 8 + [1] * 2)
    with P.scope():
        C.alloc_wu(3)
        A, B, G = emit_modprep(P, C, "f0", I["norm_ffn"][0], mod_slices(mod_d, 0, 4), mod_slices(mod_d, 0, 3), mod_slices(mod_d, 0, 5), 2)
        emit_ffn(P, C, "f0", x1, x2, 1280, [0] * 8 + [1] * 2, A, B, G, I["w1"], I["w3"], I["w2"])
    hTdb = P.buf("hTd")
    with P.scope():
        hT = P.sbuf("l1_hT", [128, KC, 1280], BF16)
        hTb = P.buf()
        A, B, _ = emit_modprep(P, C, "s5m", I["norm_mix"][1], mod_slices(mod_d, 1, 1), mod_slices(mod_d, 1, 0), None, 2)
        srcs = [(x2[t * 128:(t + 1) * 128, :], 0 if t < 8 else 1, t * 128) for t in range(10)]
        emit_norm_all(P, C, srcs, hT, hTb, A, B)
        P.dma("sp", hTd.rearrange("k p t -> p k t"), hT[:], reads=[hTb], writes=[hTdb])
    with P.scope():
        emit_s5a(P, C, I, hTd, hTdb)
    print("launch1 n_instr", P.n_instr)
    return P.finish_all()


def build_launch2():
    P = Prog()
    nc = P.nc
    P.outb = P.buf("out")

    def inp(name, shape, dt=F32):
        return nc.dram_tensor(name, list(shape), dt, kind="ExternalInput").ap()

    def outp(name, shape, dt=F32, kind="ExternalOutput"):
        return nc.dram_tensor(name, list(shape), dt, kind=kind).ap()
    I = dict(ident=inp("ident", [128, 128]), selT=inp("selT", [128, 64, 128]),
             Vd=inp("Vd", [4, 128, NCH * 3 * NB]), Yd=inp("Yd", [GS, 128, 128]), Ccd=inp("Ccd", [16, 128, 2048], BF16),
             L8d=inp("L8d", [128, 2, 2, GS]), Elist=inp("Elist", [9, 128, 2 * GS]), mod_d=inp("mod_d", [2, 2, NMOD]),
             x2=inp("x2", [1024, D]), w_glu=inp("w_glu", [D, 2 * D]), b_glu=inp("b_glu", [2 * D]),
             norm_ffn=inp("norm_ffn", [D]), norm_final=inp("norm_final", [D]),
             w1=inp("w1", [D, DFF]), w3=inp("w3", [D, DFF]), w2=inp("w2", [DFF, D]))
    out = outp("out", [1024, D])
    x3 = outp("x3", [1024, D], kind="Internal")
    x4 = outp("x4", [1024, D], kind="Internal")
    mod_d = I["mod_d"]
    C = Ctx(P, I["ident"])
    x3b = P.buf("x3")
    with P.scope():
        gyT = P.sbuf("gyT", [128, KC, 1024], BF16)
        gyTb = P.buf()
        with P.scope():
            emit_s5b(P, C, I, gyT, gyTb)
        with P.scope():
            C.alloc_wu(2)
            emit_glu(P, C, gyT, gyTb, I["x2"], x3, x3b, I["w_glu"], I["b_glu"], mod_d[1, 0, 2 * D:3 * D])
    with P.scope():
        C.alloc_wu(3)
        A, B, G = emit_modprep(P, C, "f1", I["norm_ffn"], [mod_d[1, 0, 4 * D:5 * D]], [mod_d[1, 0, 3 * D:4 * D]], [mod_d[1, 0, 5 * D:6 * D]], 1)
        P.outb = P.buf("x4")
        emit_ffn(P, C, "f1", x3, x4, 1024, [0] * 8, A, B, G, I["w1"], I["w3"], I["w2"])
    with P.scope():
        ob = P.buf("final")
        emit_final_norm(P, C, x4, P.outb, out, ob, I["norm_final"])
    print("launch2 n_instr", P.n_instr)
    return P.finish_all()


def kernel(x, c, ctx, c_ctx, ada_w, ada_b, norm_mix, norm_ffn, ffn_w1, ffn_w3, ffn_w2,
           attn_w_in, attn_w_out, attn_rpb, attn_sink,
           ssm_a_re, ssm_a_im, ssm_log_dt, ssm_b_re, ssm_b_im, ssm_c_re, ssm_c_im,
           ssm_d, ssm_w_glu, ssm_b_glu, norm_final):
    from concourse.bass_utils import run_bass_kernel_spmd
    f32 = lambda a: np.ascontiguousarray(np.asarray(a, dtype=np.float32))
    x = f32(x)
    ctx = f32(ctx)
    ncore = 8
    ident, rperm = host_consts()
    sel, selT, maskf, maskb = host_s5_consts()
    rpb_ext = np.concatenate([f32(attn_rpb)[0].reshape(8, 465), np.full((8, 1), -30000.0, np.float32)], axis=1)
    shared1 = dict(ident=ident, rperm=rperm, ctx=ctx[0], c=f32(c)[0], c_ctx=f32(c_ctx), ada_w=f32(ada_w), ada_b=f32(ada_b),
                   norm_mix=f32(norm_mix), norm_ffn=f32(norm_ffn), w_in=f32(attn_w_in)[0], w_out=f32(attn_w_out)[0],
                   sink=f32(attn_sink)[0], w1=f32(ffn_w1)[0], w3=f32(ffn_w3)[0], w2=f32(ffn_w2)[0],
                   a_re=f32(ssm_a_re)[0], a_im=f32(ssm_a_im)[0], log_dt=f32(ssm_log_dt)[0], b_re=f32(ssm_b_re)[0], b_im=f32(ssm_b_im)[0],
                   c_re=f32(ssm_c_re)[0], c_im=f32(ssm_c_im)[0], ssm_d=f32(ssm_d)[0], sel=sel, maskf=maskf, maskb=maskb)
    in1 = []
    for k in range(ncore):
        d = dict(shared1)
        d["xh"] = host_xh(x[0], k)
        d["nab"] = np.ascontiguousarray(rpb_ext[:, host_na_index(k)].reshape(8, 128, 3200))
        d["swm"] = np.ascontiguousarray(host_sw_mask(k).reshape(128, 1152))
        d["ropec"], d["ropes"] = host_rope(k)
        in1.append(d)
    nc1 = build_launch1()
    r1 = run_bass_kernel_spmd(nc1, in1, core_ids=list(range(ncore))).results
    Eown = [np.asarray(r1[k]["Eown"], dtype=np.float32) for k in range(ncore)]
    Ectx = np.asarray(r1[0]["Ectx"], dtype=np.float32)
    zero = np.zeros_like(Ectx)
    shared2 = dict(ident=ident, selT=selT, w_glu=f32(ssm_w_glu)[0], b_glu=f32(ssm_b_glu)[0], norm_ffn=f32(norm_ffn)[1],
                   norm_final=f32(norm_final), w1=f32(ffn_w1)[1], w3=f32(ffn_w3)[1], w2=f32(ffn_w2)[1])
    in2 = []
    for k in range(ncore):
        lf = [Ectx] + [Eown[m] for m in range(k)]
        lb = [Ectx] + [Eown[m] for m in range(ncore - 1, k, -1)]
        lf = [zero] * (9 - len(lf)) + lf
        lb = [zero] * (9 - len(lb)) + lb
        El = np.stack([np.concatenate([lf[s][0:64], lb[s][64:128]], axis=0) for s in range(9)])
        d = dict(shared2)
        d.update(Vd=r1[k]["Vd"], Yd=r1[k]["Yd"], Ccd=r1[k]["Ccd"], L8d=r1[k]["L8d"], Elist=np.ascontiguousarray(El),
                 mod_d=r1[k]["mod_d"], x2=np.ascontiguousarray(np.asarray(r1[k]["x2"])[0:1024]))
        in2.append(d)
    nc2 = build_launch2()
    r2 = run_bass_kernel_spmd(nc2, in2, core_ids=list(range(ncore))).results
    out = np.concatenate([np.asarray(r2[k]["out"], dtype=np.float32) for k in range(ncore)], axis=0)
    return out[None]
```

```python
import contextlib
import numpy as np
import concourse.bass as bass
import concourse.mybir as mybir

F32 = mybir.dt.float32
BF16 = mybir.dt.bfloat16
AF = mybir.ActivationFunctionType
ALU = mybir.AluOpType
AX = mybir.AxisListType

ENGS = ("pe", "act", "dve", "pool", "sp")
SEM_LIMIT = 30000
NDMA_SLOTS = 8


class Buf:
    __slots__ = ("name", "w", "readers", "excl")

    def __init__(self, name):
        self.name = name
        self.excl = False
        self.w = None
        self.readers = {}


class Prog:
    def __init__(self, strict_same_engine=True):
        self.nc = bass.Bass("TRN2", target_bir_lowering=False)
        self.es = contextlib.ExitStack()
        self.q = {e: [] for e in ENGS}
        self.strict = strict_same_engine
        self.sems = {}
        self.epoch = {e: 0 for e in ENGS}
        self.cnt = {e: 0 for e in ENGS}
        self.seen = {e: {} for e in ENGS}
        self.dma_slot = {e: 0 for e in ENGS}
        self.dma_val = {}
        self.n_instr = 0
        self._uid = 0

    def sem(self, key):
        if key not in self.sems:
            self.sems[key] = getattr(self, "sem_es", self.es).enter_context(self.nc.semaphore("s_" + "_".join(str(x) for x in key)))
        return self.sems[key]

    def sbuf(self, name, shape, dtype):
        self._uid += 1
        return self.es.enter_context(self.nc.sbuf_tensor("%s_u%d" % (name, self._uid), list(shape), dtype))

    def psum(self, name, shape, dtype):
        return self.es.enter_context(self.nc.psum_tensor(name, list(shape), dtype))

    def dram(self, name, shape, dtype, kind="Internal"):
        return self.nc.dram_tensor(name, list(shape), dtype, kind=kind)

    def buf(self, name=None):
        self._uid += 1
        return Buf(name or "b%d" % self._uid)

    def _deps(self, eng, reads, writes, relax=False):
        deps = {}

        def add(d):
            if d is None:
                return
            k, v = d[0], d[1]
            if deps.get(k, 0) < v:
                deps[k] = v
        for b in reads:
            add(b.w)
        for b in writes:
            add(b.w)
            for r in b.readers.values():
                add(r)
        out = []
        for k, v in deps.items():
            if k[0] == "e" and k[1] == eng and (eng == "pe" or relax or not self.strict):
                continue
            if self.seen[eng].get(k, 0) >= v:
                continue
            self.seen[eng][k] = v
            out.append((k, v))
        return out

    def op(self, eng, fn, reads=(), writes=(), relax=False, inc=True):
        xs = [b for b in reads if b.excl]
        if xs:
            writes = list(writes) + [b for b in xs if b not in writes]
        waits = self._deps(eng, reads, writes, relax)
        if self.cnt[eng] >= SEM_LIMIT:
            self.epoch[eng] += 1
            self.cnt[eng] = 0
        key = ("e", eng, self.epoch[eng])
        self.sem(key)
        if inc:
            self.cnt[eng] += 1
            val = self.cnt[eng]
            self.q[eng].append((waits, fn, key, 1))
        else:
            val = self.cnt[eng] + 1
            self.q[eng].append((waits, fn, key, 0))
        for b in reads:
            b.readers[eng] = (key, val)
        for b in writes:
            b.w = (key, val, eng)
            b.readers = {}
        self.n_instr += 1

    def dma(self, eng, out, in_, reads=(), writes=(), **kw):
        slot = self.dma_slot[eng]
        self.dma_slot[eng] = (slot + 1) % NDMA_SLOTS
        key = ("d", eng, slot)
        self.sem(key)
        waits = self._deps(eng, reads, writes)
        prev = self.dma_val.get(key, 0)
        if prev > 0 and self.seen[eng].get(key, 0) < prev:
            self.seen[eng][key] = prev
            waits.append((key, prev))
        val = prev + 16
        self.dma_val[key] = val

        def fn(e, out=out, in_=in_, kw=kw):
            return e.dma_start(out=out, in_=in_, **kw)
        self.q[eng].append((waits, fn, key, 16))
        for b in reads:
            b.readers["dma_" + eng + str(slot)] = (key, val)
        for b in writes:
            b.w = (key, val, "dma")
            b.readers = {}
        self.n_instr += 1

    def flush(self):
        for key, val in self.dma_val.items():
            e = key[1]
            if self.seen[e].get(key, 0) < val:
                self.seen[e][key] = val
                self.q[e].append(([(key, val)], None, None, 0))
        nc = self.nc
        engmap = {"pe": "tensor", "act": "scalar", "dve": "vector", "pool": "gpsimd", "sp": "sync"}
        if any(self.q[e] for e in ENGS):
            with nc.Block(no_gpsimd_drain=True) as block:
                for e in ENGS:
                    items = self.q[e]
                    if not items:
                        continue

                    def body(h, items=items):
                        for waits, fn, key, inc in items:
                            for k, v in waits:
                                h.wait_ge(self.sems[k], v)
                            if fn is not None:
                                ins = fn(h)
                                if inc:
                                    ins.then_inc(self.sems[key], inc)
                    getattr(block, engmap[e])(body)
        self.q = {e: [] for e in ENGS}

    @contextlib.contextmanager
    def scope(self):
        outer = self.es
        self.es = contextlib.ExitStack()
        self.sem_es = getattr(self, "sem_es", outer)
        try:
            yield
            self.flush()
        finally:
            self.es.close()
            self.es = outer

    def finish_all(self):
        self.flush()
        self.sem_es.close() if hasattr(self, "sem_es") else None
        self.es.close()
        return self.nc

    def finish(self, final_bufs=()):
        for e in ENGS:
            waits = self._deps(e, list(final_bufs), [])
            if waits:
                self.q[e].append((waits, None, None, 0))
        for key, val in self.dma_val.items():
            e = key[1]
            if self.seen[e].get(key, 0) < val:
                self.seen[e][key] = val
                self.q[e].append(([(key, val)], None, None, 0))
        nc = self.nc
        engmap = {"pe": "tensor", "act": "scalar", "dve": "vector", "pool": "gpsimd", "sp": "sync"}
        with nc.Block() as block:
            for e in ENGS:
                items = self.q[e]
                if not items:
                    continue

                def body(h, items=items):
                    for waits, fn, key, inc in items:
                        for k, v in waits:
                            h.wait_ge(self.sems[k], v)
                        if fn is not None:
                            ins = fn(h)
                            if inc:
                                ins.then_inc(self.sems[key], inc)
                getattr(block, engmap[e])(body)
        self.es.close()
        return nc


D = 2048
KC = D // 128
DFF = 5632
NFF = DFF // 128
EPS = 1e-6
WUNIT = 11264


def load_fm(P, dst, dst_buf, vec_ap, eng="sp"):
    P.dma(eng, dst, vec_ap.rearrange("(k p) -> p k", p=128), writes=[dst_buf],
          allow_slow_non_contiguous=True)


class Ctx:
    def __init__(self, P, ident_ap):
        self.P = P
        nc = P.nc
        self.ident = P.sbuf("ident_sb", [128, 128], F32)
        self.ident_b = P.buf("ident")
        P.dma("sp", self.ident[:], ident_ap, writes=[self.ident_b])
        self.ps = P.psum("ps", [128, 8 * 512], F32)
        self.psb = [P.buf("psb%d" % i) for i in range(8)]
        for b in self.psb:
            b.excl = True
        self.wu = None
        self.wu_gen = 0
        self.rr_n = 4

    def bank(self, i):
        return self.ps[:, i * 512:(i + 1) * 512]

    def rr4(self):
        i = getattr(self, "_rr", 0) % self.rr_n
        self._rr = (i + 1) % self.rr_n
        return i

    def alloc_wu(self, n=3):
        P = self.P
        self.wu_gen += 1
        self.wu = [P.sbuf("wu%d_%d" % (self.wu_gen, i), [128, WUNIT], BF16) for i in range(n)]
        self.wub = [P.buf("wu%d" % i) for i in range(n)]
        self.wu_next = 0

    def next_wu(self):
        i = self.wu_next
        self.wu_next = (i + 1) % len(self.wu)
        return self.wu[i], self.wub[i]


def emit_modprep(P, C, name, gw_ap, sc_ap, sh_ap, gate_ap, ncls):
    gw = P.sbuf(name + "_gw", [128, KC], F32)
    gwb = P.buf()
    load_fm(P, gw[:], gwb, gw_ap)
    A, B, G = [], [], []
    for c in range(ncls):
        a = P.sbuf("%s_A%d" % (name, c), [128, KC], F32)
        ab = P.buf()
        load_fm(P, a[:], ab, sc_ap[c])
        P.op("dve", lambda e, a=a: e.tensor_scalar(out=a[:], in0=a[:], scalar1=1.0, scalar2=None, op0=ALU.add),
             reads=[ab], writes=[ab])
        P.op("dve", lambda e, a=a: e.tensor_tensor(out=a[:], in0=a[:], in1=gw[:], op=ALU.mult),
             reads=[ab, gwb], writes=[ab])
        b = P.sbuf("%s_B%d" % (name, c), [128, KC], F32)
        bb = P.buf()
        load_fm(P, b[:], bb, sh_ap[c])
        g = None
        gb = None
        if gate_ap is not None:
            g = P.sbuf("%s_G%d" % (name, c), [128, D], F32)
            gb = P.buf()
            P.dma("sp", g[:], gate_ap[c].partition_broadcast(128), writes=[gb])
        A.append((a, ab))
        B.append((b, bb))
        G.append((g, gb))
    return A, B, G


def emit_norm_T(P, C, xt, xtb, hT, hTb, col0, A, B, scr):
    ssq, rstd, xn, sb = scr
    P.op("act", lambda e: e.activation(out=xn[:], in_=xt, func=AF.Square, accum_out=ssq[:]),
         reads=[xtb], writes=[sb])
    P.op("dve", lambda e: e.tensor_scalar(out=rstd[:], in0=ssq[:], scalar1=1.0 / D, scalar2=EPS,
                                          op0=ALU.mult, op1=ALU.add), reads=[sb], writes=[sb])
    P.op("act", lambda e: e.activation(out=rstd[:], in_=rstd[:], func=AF.Sqrt), reads=[sb], writes=[sb])
    P.op("dve", lambda e: e.reciprocal(out=rstd[:], in_=rstd[:]), reads=[sb], writes=[sb])
    P.op("act", lambda e: e.activation(out=xn[:], in_=xt, func=AF.Copy, scale=rstd[:]),
         reads=[xtb, sb], writes=[sb])
    a, ab = A
    b, bb = B
    for q in range(KC // 4):
        bank = 6 + (q % 2)
        for kk in range(4):
            k = q * 4 + kk
            P.op("pe", lambda e, k=k, kk=kk, bank=bank: e.transpose(
                out=C.ps[:, bank * 512 + kk * 128: bank * 512 + (kk + 1) * 128],
                in_=xn[:, k * 128:(k + 1) * 128], identity=C.ident[:]),
                reads=[sb, C.ident_b], writes=[C.psb[bank]])
        for kk in range(4):
            k = q * 4 + kk
            P.op("dve", lambda e, k=k, kk=kk, bank=bank: e.tensor_scalar(
                out=hT[:, k, col0:col0 + 128],
                in0=C.ps[:, bank * 512 + kk * 128: bank * 512 + (kk + 1) * 128],
                scalar1=a[:, k:k + 1], scalar2=b[:, k:k + 1], op0=ALU.mult, op1=ALU.add),
                reads=[C.psb[bank], ab, bb], writes=[hTb])


def emit_ffn(P, C, name, xin, xout, T, cls_of_tile, A, B, G, w1, w3, w2):
    ntile = T // 128
    xt = [P.sbuf("%s_xt%d" % (name, i), [128, D], F32) for i in range(4)]
    xtb = [P.buf() for i in range(4)]
    hT = P.sbuf(name + "_hT", [128, KC, 512], BF16)
    hTb = P.buf()
    gT = P.sbuf(name + "_gT", [128, NFF, 512], BF16)
    gTb = [P.buf() for j in range(NFF)]
    ssq = P.sbuf(name + "_ssq", [128, 1], F32)
    rstd = P.sbuf(name + "_rstd", [128, 1], F32)
    xn = P.sbuf(name + "_xn", [128, D], F32)
    scr = (ssq, rstd, xn, P.buf())
    su = [P.sbuf("%s_su%d" % (name, i), [128, 512], F32) for i in range(2)]
    sub = [P.buf() for i in range(2)]
    ot = [P.sbuf("%s_ot%d" % (name, i), [128, 256], F32) for i in range(2)]
    otb = [P.buf() for i in range(2)]
    it = 0
    dn = 0
    for t0 in range(0, ntile, 4):
        nt = min(4, ntile - t0)
        ntok = nt * 128
        for i in range(nt):
            P.dma("sp", xt[i][:], xin[(t0 + i) * 128:(t0 + i + 1) * 128, :], writes=[xtb[i]])
            c = cls_of_tile[t0 + i]
            emit_norm_T(P, C, xt[i][:], xtb[i], hT, hTb, i * 128, A[c], B[c], scr)
        for jg in range(NFF // 4):
            w1u, w1b = C.next_wu()
            w3u, w3b = C.next_wu()
            w1v = w1u[:, 0:KC * 512].rearrange("p (k c) -> p k c", k=KC)
            w3v = w3u[:, 0:KC * 512].rearrange("p (k c) -> p k c", k=KC)
            P.dma("pool", w1v, w1[:, jg * 512:(jg + 1) * 512].rearrange("(k p) c -> p k c", p=128), writes=[w1b])
            P.dma("pool", w3v, w3[:, jg * 512:(jg + 1) * 512].rearrange("(k p) c -> p k c", p=128), writes=[w3b])
            for jj in range(4):
                j = jg * 4 + jj
                bu = it % 2
                bv = 2 + it % 2
                for k in range(KC):
                    P.op("pe", lambda e, k=k, jj=jj, bu=bu, w1v=w1v, ntok=ntok: e.matmul(
                        C.ps[:, bu * 512: bu * 512 + ntok], lhsT=w1v[:, k, jj * 128:(jj + 1) * 128],
                        rhs=hT[:, k, 0:ntok], start=(k == 0), stop=(k == KC - 1)),
                        reads=[w1b, hTb], writes=[C.psb[bu]], inc=(k == KC - 1))
                for k in range(KC):
                    P.op("pe", lambda e, k=k, jj=jj, bv=bv, w3v=w3v, ntok=ntok: e.matmul(
                        C.ps[:, bv * 512: bv * 512 + ntok], lhsT=w3v[:, k, jj * 128:(jj + 1) * 128],
                        rhs=hT[:, k, 0:ntok], start=(k == 0), stop=(k == KC - 1)),
                        reads=[w3b, hTb], writes=[C.psb[bv]], inc=(k == KC - 1))
                s = su[it % 2]
                P.op("act", lambda e, s=s, bu=bu, ntok=ntok: e.activation(out=s[:, 0:ntok], in_=C.ps[:, bu * 512: bu * 512 + ntok],
                                                               func=AF.Silu),
                     reads=[C.psb[bu]], writes=[sub[it % 2]])
                P.op("dve", lambda e, s=s, bv=bv, j=j, ntok=ntok: e.tensor_tensor(
                    out=gT[:, j, 0:ntok], in0=s[:, 0:ntok], in1=C.ps[:, bv * 512: bv * 512 + ntok], op=ALU.mult),
                    reads=[sub[it % 2], C.psb[bv]], writes=[gTb[j]])
                it += 1
        for cb in range(D // 256):
            w2u, w2b = C.next_wu()
            w2v = w2u[:, 0:NFF * 256].rearrange("p (j c) -> p j c", j=NFF)
            P.dma("pool", w2v, w2[:, cb * 256:(cb + 1) * 256].rearrange("(j p) c -> p j c", p=128), writes=[w2b])
            for i in range(nt):
                bank = 4 + dn % 2
                for j in range(NFF):
                    P.op("pe", lambda e, j=j, i=i, bank=bank, w2v=w2v: e.matmul(
                        C.ps[:, bank * 512: bank * 512 + 256], lhsT=gT[:, j, i * 128:(i + 1) * 128],
                        rhs=w2v[:, j, :], start=(j == 0), stop=(j == NFF - 1)),
                        reads=[w2b, gTb[j]], writes=[C.psb[bank]], inc=(j == NFF - 1))
                c = cls_of_tile[t0 + i]
                g, gb = G[c]
                o = ot[dn % 2]
                ob = otb[dn % 2]
                P.op("dve", lambda e, o=o, bank=bank, g=g, cb=cb: e.tensor_tensor(
                    out=o[:], in0=C.ps[:, bank * 512: bank * 512 + 256], in1=g[:, cb * 256:(cb + 1) * 256], op=ALU.mult),
                    reads=[C.psb[bank], gb], writes=[ob])
                P.op("pool", lambda e, o=o, i=i, cb=cb: e.tensor_tensor(
                    out=xt[i][:, cb * 256:(cb + 1) * 256], in0=o[:], in1=xt[i][:, cb * 256:(cb + 1) * 256], op=ALU.add),
                    reads=[ob, xtb[i]], writes=[xtb[i]])
                dn += 1
        for i in range(nt):
            P.dma("sp", xout[(t0 + i) * 128:(t0 + i + 1) * 128, :], xt[i][:], reads=[xtb[i]], writes=[P.outb])


NKT = 14
NQT = 10
SCALE = 128 ** -0.5


def q_hcol(qi):
    return (qi + 2) * 128 if qi < 8 else 1536 + (qi - 8) * 128


def emit_norm_all(P, C, srcs, hT, hTb, A, B):
    xt = [P.sbuf("nx%d" % i, [128, D], F32) for i in range(2)]
    xtb = [P.buf() for i in range(2)]
    ssq = P.sbuf("n_ssq", [128, 1], F32)
    rstd = P.sbuf("n_rstd", [128, 1], F32)
    xn = P.sbuf("n_xn", [128, D], F32)
    scr = (ssq, rstd, xn, P.buf())
    for n, (ap, c, col0) in enumerate(srcs):
        P.dma("sp", xt[n % 2][:], ap, writes=[xtb[n % 2]])
        emit_norm_T(P, C, xt[n % 2][:], xtb[n % 2], hT, hTb, col0, A[c], B[c], scr)


def emit_proj_fm(P, C, dst, dstb, wv, wb, wc0, hT, hTb, ranges, evac="act"):
    for (d0, h0, n) in ranges:
        bank = C.rr4()
        for k in range(KC):
            P.op("pe", lambda e, k=k, bank=bank, h0=h0, n=n: e.matmul(
                C.ps[:, bank * 512: bank * 512 + n], lhsT=wv[:, k, wc0:wc0 + 128],
                rhs=hT[:, k, h0:h0 + n], start=(k == 0), stop=(k == KC - 1)),
                reads=[wb, hTb], writes=[C.psb[bank]])
        P.op(evac, lambda e, bank=bank, d0=d0, n=n: (e.activation(out=dst[:, d0:d0 + n], in_=C.ps[:, bank * 512: bank * 512 + n], func=AF.Copy)
                                                     if evac == "act" else
                                                     e.tensor_copy(out=dst[:, d0:d0 + n], in_=C.ps[:, bank * 512: bank * 512 + n])),
             reads=[C.psb[bank]], writes=[dstb])


def emit_proj_v(P, C, Vh, Vhb, wv, wb, wc0, hT, hTb):
    for g0 in range(0, NKT, 4):
        n = min(4, NKT - g0)
        bank = C.rr4()
        for j in range(n):
            kt = g0 + j
            for k in range(KC):
                P.op("pe", lambda e, k=k, kt=kt, j=j, bank=bank: e.matmul(
                    C.ps[:, bank * 512 + j * 128: bank * 512 + (j + 1) * 128],
                    lhsT=hT[:, k, kt * 128:(kt + 1) * 128], rhs=wv[:, k, wc0:wc0 + 128],
                    start=(k == 0), stop=(k == KC - 1)),
                    reads=[wb, hTb], writes=[C.psb[bank]])
        P.op("dve", lambda e, g0=g0, n=n, bank=bank: e.tensor_copy(
            out=Vh[:, g0:g0 + n, :], in_=C.ps[:, bank * 512: bank * 512 + n * 128].rearrange("p (j d) -> p j d", j=n)),
            reads=[C.psb[bank]], writes=[Vhb])


def emit_attn_core(P, C, S, hidx, qT, qTb, kT, kTb, Vh, Vhb, slots_of, tab, tabb, esink_col, oT_d, oTd_b):
    for qi in range(NQT):
        kts, toff, ntab = slots_of(qi)
        ns = len(kts)
        it = S["it"]
        S["it"] += 1
        b0 = 4 + 2 * (it % 2)
        ob = it % 2
        pe_sb = S["pe_sb"][it % 2]
        pe_b = S["pe_b"][it % 2]
        pm = S["pm"][it % 2]
        pm_b = S["pm_b"][it % 2]
        for si, kt in enumerate(kts):
            bank = b0 + si // 4
            off = bank * 512 + (si % 4) * 128
            P.op("pe", lambda e, kt=kt, off=off, qi=qi: e.matmul(
                C.ps[:, off:off + 128], lhsT=kT[:, kt * 128:(kt + 1) * 128], rhs=qT[:, qi * 128:(qi + 1) * 128],
                start=True, stop=True), reads=[kTb, qTb], writes=[C.psb[bank]])
        for half in range((ns + 3) // 4):
            n = min(4, ns - half * 4)
            bank = b0 + half
            P.op("act", lambda e, half=half, n=n, bank=bank, pe_sb=pe_sb: e.activation(
                out=pe_sb[:, half * 512: half * 512 + n * 128], in_=C.ps[:, bank * 512: bank * 512 + n * 128],
                func=AF.Exp, scale=SCALE), reads=[C.psb[bank]], writes=[pe_b])
        if toff is not None:
            P.op("dve", lambda e, toff=toff, ntab=ntab, pe_sb=pe_sb, pm=pm: e.tensor_tensor(
                out=pm[:, 0:ntab * 128], in0=pe_sb[:, 0:ntab * 128], in1=tab[:, toff:toff + ntab * 128], op=ALU.mult),
                reads=[pe_b, tabb], writes=[pm_b])
        else:
            ntab = 0
        otb = S["otb"][ob]
        denb = S["denb"][ob]
        for si, kt in enumerate(kts):
            src, srcb = (pm, pm_b) if si < ntab else (pe_sb, pe_b)
            P.op("pe", lambda e, kt=kt, si=si, src=src, ob=ob, ns=ns: e.matmul(
                S["ot"][:, ob * 512: ob * 512 + 128], lhsT=Vh[:, kt, :], rhs=src[:, si * 128:(si + 1) * 128],
                start=(si == 0), stop=(si == ns - 1)), reads=[Vhb, srcb], writes=[otb])
        for si, kt in enumerate(kts):
            src, srcb = (pm, pm_b) if si < ntab else (pe_sb, pe_b)
            P.op("pe", lambda e, si=si, src=src, ob=ob, ns=ns: e.matmul(
                S["ot"][:, ob * 512 + 128: ob * 512 + 256], lhsT=S["ones"][:], rhs=src[:, si * 128:(si + 1) * 128],
                start=(si == 0), stop=(si == ns - 1)), reads=[srcb, S["ones_b"]], writes=[denb])
        rec = S["rec"][it % 2]
        recb = S["rec_b"][it % 2]
        if esink_col is not None:
            P.op("dve", lambda e, rec=rec, ob=ob: e.tensor_scalar(
                out=rec[:], in0=S["ot"][:, ob * 512 + 128: ob * 512 + 256], scalar1=esink_col, scalar2=None, op0=ALU.add),
                reads=[denb, S["esink_b"]], writes=[recb])
            P.op("dve", lambda e, rec=rec: e.reciprocal(out=rec[:], in_=rec[:]), reads=[recb], writes=[recb])
        else:
            P.op("dve", lambda e, rec=rec, ob=ob: e.reciprocal(out=rec[:], in_=S["ot"][:, ob * 512 + 128: ob * 512 + 256]),
                 reads=[denb], writes=[recb])
        oh = S["oh"]
        P.op("dve", lambda e, rec=rec, ob=ob, qi=qi: e.tensor_tensor(
            out=oh[:, qi * 128:(qi + 1) * 128], in0=S["ot"][:, ob * 512: ob * 512 + 128], in1=rec[:], op=ALU.mult),
            reads=[otb, recb], writes=[S["oh_b"]])
    P.dma("sp", oT_d[hidx * 128:(hidx + 1) * 128, :], S["oh"][:], reads=[S["oh_b"]], writes=[oTd_b])


def emit_attn(P, C, I):
    hT = P.sbuf("a_hT", [128, KC, NKT * 128], BF16)
    hTb = P.buf()
    A, B, _ = emit_modprep(P, C, "am", I["gw"], I["sc"], I["sh"], None, 2)
    with P.scope():
        srcs = [(I["xh"][t * 128:(t + 1) * 128, :], 0, t * 128) for t in range(12)]
        srcs += [(I["ctx"][t * 128:(t + 1) * 128, :], 1, 1536 + t * 128) for t in range(2)]
        emit_norm_all(P, C, srcs, hT, hTb, A, B)
    oTd_b = P.buf("oTd")
    with P.scope():
        S = {"it": 0}
        S["ot"] = C.ps[:, 2 * 512: 4 * 512]
        S["otb"] = [C.psb[2], C.psb[3]]
        S["denb"] = [C.psb[2], C.psb[3]]
        C.rr_n = 2
        S["pe_sb"] = [P.sbuf("a_pe%d" % i, [128, 1024], BF16) for i in range(2)]
        S["pe_b"] = [P.buf() for i in range(2)]
        S["pm"] = [P.sbuf("a_pm%d" % i, [128, 640], BF16) for i in range(2)]
        S["pm_b"] = [P.buf() for i in range(2)]
        S["rec"] = [P.sbuf("a_rec%d" % i, [128, 128], F32) for i in range(2)]
        S["rec_b"] = [P.buf() for i in range(2)]
        S["oh"] = P.sbuf("a_oh", [128, NQT * 128], BF16)
        S["oh_b"] = P.buf()
        S["ones"] = P.sbuf("a_ones", [128, 128], BF16)
        S["ones_b"] = P.buf()
        P.op("dve", lambda e: e.memset(S["ones"][:], 1.0), writes=[S["ones_b"]])
        esink = P.sbuf("a_esink", [128, 8], F32)
        S["esink_b"] = P.buf()
        P.dma("sp", esink[:], I["sink"].partition_broadcast(128), writes=[S["esink_b"]])
        P.op("act", lambda e: e.activation(out=esink[:], in_=esink[:], func=AF.Exp), reads=[S["esink_b"]], writes=[S["esink_b"]])
        qT = P.sbuf("a_qT", [128, NQT * 128], BF16)
        qTb = P.buf()
        kT = P.sbuf("a_kT", [128, NKT * 128], BF16)
        kTb = P.buf()
        Vh = P.sbuf("a_Vh", [128, NKT, 128], BF16)
        Vhb = P.buf()
        nabf = P.sbuf("a_nabf", [128, 25 * 128], F32)
        nabfb = P.buf()
        Eh = P.sbuf("a_Eh", [128, 25 * 128], BF16)
        Ehb = P.buf()
        swm = P.sbuf("a_swm", [128, 9 * 128], BF16)
        swmb = P.buf()
        P.dma("pool", swm[:], I["swm"], writes=[swmb])
        q_ranges = [(0, 256, 512), (512, 768, 512), (1024, 1536, 256)]
        k_ranges = [(0, 0, 512), (512, 512, 512), (1024, 1024, 512), (1536, 1536, 256)]
        w_in = I["w_in"]

        def load_unit(c0):
            u, ub = C.next_wu()
            v = u[:, 0:KC * 512].rearrange("p (k c) -> p k c", k=KC)
            P.dma("pool", v, w_in[:, c0:c0 + 512].rearrange("(k p) c -> p k c", p=128), writes=[ub])
            return v, ub

        def cls_of(qi):
            return {0: 0, 1: 1, 6: 3, 7: 4}.get(qi, 2)

        def na_slots(qi):
            if qi < 8:
                return [qi + s for s in range(5)] + [12, 13], cls_of(qi) * 640, 5
            return [12, 13], None, 0

        def sw_slots(qi):
            if qi < 8:
                c = 0 if qi == 0 else (2 if qi == 7 else 1)
                return [qi + 1 + s for s in range(3)] + [12, 13], c * 384, 3
            return [12, 13], None, 0

        for hg in range(2):
            wq, wqb = load_unit(hg * 512)
            wk, wkb = load_unit(1024 + hg * 512)
            wvv, wvb = load_unit(2048 + hg * 512)
            for hh in range(4):
                h = hg * 4 + hh
                P.dma("sp", nabf[:], I["nab"][h], writes=[nabfb])
                P.op("act", lambda e: e.activation(out=Eh[:], in_=nabf[:], func=AF.Exp), reads=[nabfb], writes=[Ehb])
                emit_proj_fm(P, C, qT, qTb, wq, wqb, hh * 128, hT, hTb, q_ranges)
                emit_proj_fm(P, C, kT, kTb, wk, wkb, hh * 128, hT, hTb, k_ranges)
                emit_proj_v(P, C, Vh, Vhb, wvv, wvb, hh * 128, hT, hTb)
                emit_attn_core(P, C, S, h, qT, qTb, kT, kTb, Vh, Vhb, na_slots, Eh, Ehb, None, I["oT_d"], oTd_b)
        cosT = P.sbuf("a_cos", [128, 1536], F32)
        sinT = P.sbuf("a_sin", [128, 1536], F32)
        ropeb = P.buf()
        P.dma("sp", cosT[:], I["ropec"], writes=[ropeb])
        P.dma("sp", sinT[:], I["ropes"], writes=[ropeb])
        rperm = P.sbuf("a_rperm", [128, 128], F32)
        rpb_ = P.buf()
        P.dma("sp", rperm[:], I["rperm"], writes=[rpb_])
        qf = P.sbuf("a_qf", [128, 1792], F32)
        qfb = P.buf()
        t1 = P.sbuf("a_t1", [128, 512], F32)
        t1b = P.buf()
        t2 = P.sbuf("a_t2", [128, 512], F32)
        t2b = P.buf()

        def rope(dst, dstb, ncols_rot, tcol0, ncols_all):
            for c0 in range(0, ncols_rot, 512):
                n = min(512, ncols_rot - c0)
                bank = C.rr4()
                P.op("pe", lambda e, c0=c0, n=n, bank=bank: e.matmul(
                    C.ps[:, bank * 512: bank * 512 + n], lhsT=rperm[:], rhs=qf[:, c0:c0 + n], start=True, stop=True),
                    reads=[rpb_, qfb], writes=[C.psb[bank]])
                P.op("dve", lambda e, c0=c0, n=n: e.tensor_tensor(
                    out=t1[:, 0:n], in0=qf[:, c0:c0 + n], in1=cosT[:, tcol0 + c0: tcol0 + c0 + n], op=ALU.mult),
                    reads=[qfb, ropeb], writes=[t1b])
                P.op("dve", lambda e, c0=c0, n=n, bank=bank: e.tensor_tensor(
                    out=t2[:, 0:n], in0=C.ps[:, bank * 512: bank * 512 + n], in1=sinT[:, tcol0 + c0: tcol0 + c0 + n], op=ALU.mult),
                    reads=[C.psb[bank], ropeb], writes=[t2b])
                P.op("pool", lambda e, c0=c0, n=n: e.tensor_tensor(
                    out=dst[:, c0:c0 + n], in0=t1[:, 0:n], in1=t2[:, 0:n], op=ALU.add),
                    reads=[t1b, t2b], writes=[dstb])
            if ncols_all > ncols_rot:
                P.op("act", lambda e: e.activation(out=dst[:, ncols_rot:ncols_all], in_=qf[:, ncols_rot:ncols_all], func=AF.Copy),
                     reads=[qfb], writes=[dstb])

        wkv, wkvb = load_unit(4096)
        for g in range(2):
            wq, wqb = load_unit(3072 + g * 512)
            emit_proj_fm(P, C, qf, qfb, wkv, wkvb, g * 128, hT, hTb, k_ranges)
            rope(kT, kTb, 1536, 0, 1792)
            emit_proj_v(P, C, Vh, Vhb, wkv, wkvb, 256 + g * 128, hT, hTb)
            for hh in range(4):
                hq = g * 4 + hh
                emit_proj_fm(P, C, qf, qfb, wq, wqb, hh * 128, hT, hTb, q_ranges)
                rope(qT, qTb, 1024, 256, 1280)
                emit_attn_core(P, C, S, 8 + hq, qT, qTb, kT, kTb, Vh, Vhb, sw_slots, swm, swmb,
                               esink[:, hq:hq + 1], I["oT_d"], oTd_b)
    C.rr_n = 4
    return oTd_b


def emit_outproj(P, C, oT_d, oTd_b, xsrc_of, x1out, outb, w_out, G, ntile, cls_of_tile, name="op"):
    ncol = ntile * 128
    oT = P.sbuf(name + "_oT", [128, KC, ncol], BF16)
    oTb = P.buf()
    P.dma("sp", oT[:], oT_d.rearrange("(k p) t -> p k t", p=128), reads=[oTd_b], writes=[oTb])
    xs = [P.sbuf("%s_xs%d" % (name, i), [128, 512], F32) for i in range(2)]
    xsb = [P.buf() for i in range(2)]
    tm = [P.sbuf("%s_tm%d" % (name, i), [128, 512], F32) for i in range(2)]
    tmb = [P.buf() for i in range(2)]
    it = 0
    for cb in range(D // 512):
        u, ub = C.next_wu()
        wv = u[:, 0:KC * 512].rearrange("p (k c) -> p k c", k=KC)
        P.dma("pool", wv, w_out[:, cb * 512:(cb + 1) * 512].rearrange("(k p) c -> p k c", p=128), writes=[ub])
        for t in range(ntile):
            bank = C.rr4()
            for k in range(KC):
                P.op("pe", lambda e, k=k, t=t, bank=bank, wv=wv: e.matmul(
                    C.ps[:, bank * 512:(bank + 1) * 512], lhsT=oT[:, k, t * 128:(t + 1) * 128], rhs=wv[:, k, :],
                    start=(k == 0), stop=(k == KC - 1)), reads=[oTb, ub], writes=[C.psb[bank]])
            g, gb = G[cls_of_tile[t]]
            x_, xb_ = xs[it % 2], xsb[it % 2]
            t_, tb_ = tm[it % 2], tmb[it % 2]
            P.dma("sp", x_[:], xsrc_of(t)[:, cb * 512:(cb + 1) * 512], writes=[xb_])
            P.op("dve", lambda e, t_=t_, bank=bank, g=g, cb=cb: e.tensor_tensor(
                out=t_[:], in0=C.ps[:, bank * 512:(bank + 1) * 512], in1=g[:, cb * 512:(cb + 1) * 512], op=ALU.mult),
                reads=[C.psb[bank], gb], writes=[tb_])
            P.op("pool", lambda e, t_=t_, x_=x_: e.tensor_tensor(out=x_[:], in0=t_[:], in1=x_[:], op=ALU.add),
                 reads=[tb_, xb_], writes=[xb_])
            P.dma("sp", x1out[t * 128:(t + 1) * 128, cb * 512:(cb + 1) * 512], x_[:], reads=[xb_], writes=[outb])
            it += 1


GRID_W = 64


def host_consts():
    ident = np.eye(128, dtype=np.float32)
    rperm = np.zeros((128, 128), np.float32)
    for dp in range(128):
        partner = dp + 32 if (dp % 64) < 32 else dp - 32
        rperm[partner, dp] = 1.0
    return ident, rperm


def host_na_index(core):
    idx = np.full((128, 5, 5, 128), 465, np.int64)
    k = np.arange(128)
    q = np.arange(128)
    for ci, i in enumerate([0, 1, 3, 6, 7]):
        m = 8 * core + i
        for s in range(5):
            e = i + s
            Pk = 8 * core - 2 + e
            if core == 0 and e == 0:
                Pk = 3
            if core == 7 and e == 11:
                Pk = 60
            if Pk < 0 or Pk > 63:
                continue
            kr = (2 * Pk + k // 64)[:, None]
            kc = (k % 64)[:, None]
            qr = (2 * m + q // 64)[None, :]
            qc = (q % 64)[None, :]
            kr0 = np.clip(qr - 4, 0, 120)
            ws = np.clip(qc - 8, 0, 48)
            valid = (kr >= kr0) & (kr < kr0 + 8) & (kc >= ws) & (kc < ws + 16)
            ridx = kr - qr + 7
            cidx = np.clip(kc - qc + 15, 0, 30)
            flat = ridx * 31 + cidx
            idx[:, ci, s, :] = np.where(valid, flat, 465)
    return idx


def host_sw_mask(core):
    out = np.zeros((128, 3, 3, 128), np.float32)
    k = np.arange(128)[:, None]
    q = np.arange(128)[None, :]
    for ci, i in enumerate([0, 3, 7]):
        m = 8 * core + i
        for s in range(3):
            kp = (m - 1 + s) * 128 + k
            qp = m * 128 + q
            valid = (np.abs(kp - qp) <= 128) & (kp >= 0) & (kp < 8192)
            out[:, ci, s, :] = valid
    return out


def host_rope(core):
    tok = (8 * core - 2) * 128 + np.arange(1536)
    pos_r = (tok // GRID_W).astype(np.float32)
    pos_c = (tok % GRID_W).astype(np.float32)
    inv = (np.float32(10000.0) ** (-np.arange(32, dtype=np.float32) / np.float32(32))).astype(np.float32)
    cosT = np.zeros((128, 1536), np.float32)
    sinT = np.zeros((128, 1536), np.float32)
    for d in range(128):
        pos = pos_r if d < 64 else pos_c
        ang = (pos * inv[d % 32]).astype(np.float32)
        cosT[d] = np.cos(ang)
        sn = np.sin(ang)
        sinT[d] = -sn if (d % 64) < 32 else sn
    return cosT, sinT


def host_xh(x2d, core):
    out = np.zeros((1536, x2d.shape[1]), np.float32)
    for e in range(12):
        Pk = 8 * core - 2 + e
        if core == 0 and e == 0:
            Pk = 3
        if core == 7 and e == 11:
            Pk = 60
        if 0 <= Pk <= 63:
            out[e * 128:(e + 1) * 128] = x2d[Pk * 128:(Pk + 1) * 128]
    return out


GS = 128
NB = 32
NCH = 128
NCC = 32
I32 = mybir.dt.int32
PI = float(np.pi)


def s5_tt(P, eng, out, a, b, op, bufs):
    P.op(eng, lambda e: e.tensor_tensor(out=out, in0=a, in1=b, op=op), reads=bufs, writes=bufs[:1], relax=(eng == "dve"))


def s5_ts(P, eng, out, a, s1, op0, bufs, s2=None, op1=None):
    if op1 is None:
        P.op(eng, lambda e: e.tensor_scalar(out=out, in0=a, scalar1=s1, scalar2=None, op0=op0), reads=bufs, writes=bufs[:1], relax=(eng == "dve"))
    else:
        P.op(eng, lambda e: e.tensor_scalar(out=out, in0=a, scalar1=s1, scalar2=s2, op0=op0, op1=op1), reads=bufs, writes=bufs[:1],
             relax=(eng == "dve"))


def s5_cmul(P, eng, outr, outi, xr, xi, yr, yi, t1, t2, bufs, neg_i=False):
    s5_tt(P, eng, t1, xr, yr, ALU.mult, bufs)
    s5_tt(P, eng, t2, xi, yi, ALU.mult, bufs)
    s5_tt(P, eng, outr, t1, t2, ALU.subtract, bufs)
    s5_tt(P, eng, t1, xr, yi, ALU.mult, bufs)
    s5_tt(P, eng, t2, xi, yr, ALU.mult, bufs)
    if neg_i:
        s5_tt(P, eng, t1, t1, t2, ALU.add, bufs)
        s5_ts(P, eng, outi, t1, -1.0, ALU.mult, bufs)
    else:
        s5_tt(P, eng, outi, t1, t2, ALU.add, bufs)


def s5_horner(P, out, x, coefs, tmp, bufs):
    n = len(coefs) - 1
    s5_ts(P, "dve", out, x, float(coefs[n]), ALU.mult, bufs, s2=float(coefs[n - 1]), op1=ALU.add)
    for k in range(n - 2, -1, -1):
        s5_tt(P, "dve", tmp, out, x, ALU.mult, bufs)
        s5_ts(P, "dve", out, tmp, float(coefs[k]), ALU.add, bufs)


def s5_exp_small(P, out, x, tmp, bufs, sign=1.0):
    import math
    co = [sign ** k / math.factorial(k) for k in range(9)]
    s5_horner(P, out, x, co, tmp, bufs)


def s5_exp(P, out, x, y, tmp, bufs):
    import math
    s5_ts(P, "dve", y, x, 0.125, ALU.mult, bufs)
    co = [1.0 / math.factorial(k) for k in range(15)]
    s5_horner(P, out, y, co, tmp, bufs)
    for _ in range(3):
        s5_tt(P, "dve", out, out, out, ALU.mult, bufs)


def s5_sincos(P, sn, cs, th, ni, q, r, m, tmp, bufs):
    import math
    C1 = 6.28125
    C2 = 2 * math.pi - C1
    s5_ts(P, "dve", q, th, 1.0 / (2 * PI), ALU.mult, bufs)
    P.op("dve", lambda e: e.tensor_copy(out=ni, in_=q), reads=bufs, writes=bufs[:1])
    P.op("dve", lambda e: e.tensor_copy(out=q, in_=ni), reads=bufs, writes=bufs[:1])
    P.op("dve", lambda e: e.scalar_tensor_tensor(out=r, in0=q, scalar=-C1, in1=th, op0=ALU.mult, op1=ALU.add), reads=bufs, writes=bufs[:1])
    P.op("dve", lambda e: e.scalar_tensor_tensor(out=r, in0=q, scalar=-C2, in1=r, op0=ALU.mult, op1=ALU.add), reads=bufs, writes=bufs[:1])
    for thr, op, add in ((PI, ALU.is_gt, -1.0), (-PI, ALU.is_lt, 1.0)):
        s5_ts(P, "dve", m, r, thr, op, bufs)
        P.op("dve", lambda e, add=add: e.scalar_tensor_tensor(out=r, in0=m, scalar=add * C1, in1=r, op0=ALU.mult, op1=ALU.add),
             reads=bufs, writes=bufs[:1])
        P.op("dve", lambda e, add=add: e.scalar_tensor_tensor(out=r, in0=m, scalar=add * C2, in1=r, op0=ALU.mult, op1=ALU.add),
             reads=bufs, writes=bufs[:1])
    s5_ts(P, "dve", r, r, 0.25, ALU.mult, bufs)
    s5_tt(P, "dve", q, r, r, ALU.mult, bufs)
    sco = [(-1.0) ** k / math.factorial(2 * k + 1) for k in range(7)]
    cco = [(-1.0) ** k / math.factorial(2 * k) for k in range(8)]
    s5_horner(P, sn, q, sco, tmp, bufs)
    s5_tt(P, "dve", sn, sn, r, ALU.mult, bufs)
    s5_horner(P, cs, q, cco, tmp, bufs)
    for _ in range(2):
        s5_tt(P, "dve", tmp, sn, cs, ALU.mult, bufs)
        s5_tt(P, "dve", m, sn, sn, ALU.mult, bufs)
        s5_ts(P, "dve", sn, tmp, 2.0, ALU.mult, bufs)
        s5_ts(P, "dve", cs, m, -2.0, ALU.mult, bufs, s2=1.0, op1=ALU.add)


def emit_s5_params(P, C, I):
    pb = P.buf("s5p")
    B1 = [pb]

    def T(name, shape=(128, GS), dt=F32):
        return P.sbuf("s5_" + name, list(shape), dt)
    PERS = {}
    for nm, shp in (("bbr", (128, GS, 16)), ("bbi", (128, GS, 16)), ("ctr", (128, GS, 16)), ("cti", (128, GS, 16)),
                    ("pwAr", (128, GS, 9)), ("pwAi", (128, GS, 9)), ("pwBr", (128, GS, 9)), ("pwBi", (128, GS, 9)),
                    ("sBr", (128, GS)), ("sBi", (128, GS)), ("sCr", (128, GS)), ("sCi", (128, GS)),
                    ("L8r", (128, GS)), ("L8i", (128, GS)), ("A2", (128, 2, GS)), ("B2", (128, 2, GS)),
                    ("A2l", (128, 7, GS)), ("B2l", (128, 7, 2, GS))):
        PERS[nm] = T(nm, shp)
    with P.scope():
        return _emit_s5_params_inner(P, C, I, PERS, pb, B1, T)


def _emit_s5_params_inner(P, C, I, PERS, pb, B1, T0):
    def T(name, shape=(128, GS), dt=F32):
        if name in PERS:
            return PERS[name]
        return T0(name, shape, dt)
    are, aim, dt_ = T("are"), T("aim"), T("dt")
    an = T("anat")
    for src_, dst_ in ((I["a_re"], are), (I["a_im"], aim)):
        P.dma("sp", an[:].rearrange("g (d p) -> g d p", d=2), src_.rearrange("d g p -> g d p"), reads=B1, writes=B1)
        bank = C.rr4()
        P.op("pe", lambda e, bank=bank: e.transpose(out=C.ps[:, bank * 512: bank * 512 + 128], in_=an[:], identity=C.ident[:]),
             reads=B1 + [C.ident_b], writes=[C.psb[bank]])
        P.op("act", lambda e, dst_=dst_, bank=bank: e.activation(out=dst_[:], in_=C.ps[:, bank * 512: bank * 512 + 128], func=AF.Copy),
             reads=[C.psb[bank]], writes=B1)
    for d in range(2):
        sl = slice(d * 64, (d + 1) * 64)
        P.dma("sp", dt_[sl, :], I["log_dt"][d].partition_broadcast(64), writes=B1)
    q, r, m, tq = T("q"), T("r"), T("m"), T("tq")
    ldt = T("ldt")
    P.op("dve", lambda e: e.tensor_copy(out=ldt[:], in_=dt_[:]), reads=B1, writes=B1)
    s5_exp(P, dt_[:], ldt[:], q[:], tq[:], B1)
    ardt, th = T("ardt"), T("th")
    s5_tt(P, "dve", ardt[:], are[:], dt_[:], ALU.mult, B1)
    s5_tt(P, "dve", th[:], aim[:], dt_[:], ALU.mult, B1)
    mag, magi = T("mag"), T("magi")
    s5_exp_small(P, mag[:], ardt[:], tq[:], B1)
    s5_exp_small(P, magi[:], ardt[:], tq[:], B1, sign=-1.0)
    ni = T("ni", dt=I32)
    sn, cs = T("sn"), T("cs")
    s5_sincos(P, sn[:], cs[:], th[:], ni[:], q[:], r[:], m[:], tq[:], B1)
    lr, li, vr, vi = T("lr"), T("li"), T("vr"), T("vi")
    s5_tt(P, "dve", lr[:], mag[:], cs[:], ALU.mult, B1)
    s5_tt(P, "dve", li[:], mag[:], sn[:], ALU.mult, B1)
    s5_tt(P, "dve", vr[:], magi[:], cs[:], ALU.mult, B1)
    s5_tt(P, "dve", vi[:], magi[:], sn[:], ALU.mult, B1)
    s5_ts(P, "dve", vi[:], vi[:], -1.0, ALU.mult, B1)
    nr, den, cr_, ci_ = T("nr"), T("den"), T("cfr"), T("cfi")
    s5_ts(P, "dve", nr[:], lr[:], -1.0, ALU.add, B1)
    s5_tt(P, "dve", den[:], are[:], are[:], ALU.mult, B1)
    s5_tt(P, "dve", q[:], aim[:], aim[:], ALU.mult, B1)
    s5_tt(P, "dve", den[:], den[:], q[:], ALU.add, B1)
    P.op("dve", lambda e: e.reciprocal(out=den[:], in_=den[:]), reads=B1, writes=B1)
    s5_tt(P, "dve", cr_[:], nr[:], are[:], ALU.mult, B1)
    s5_tt(P, "dve", q[:], li[:], aim[:], ALU.mult, B1)
    s5_tt(P, "dve", cr_[:], cr_[:], q[:], ALU.add, B1)
    s5_tt(P, "dve", cr_[:], cr_[:], den[:], ALU.mult, B1)
    s5_tt(P, "dve", ci_[:], li[:], are[:], ALU.mult, B1)
    s5_tt(P, "dve", q[:], nr[:], aim[:], ALU.mult, B1)
    s5_tt(P, "dve", ci_[:], ci_[:], q[:], ALU.subtract, B1)
    s5_tt(P, "dve", ci_[:], ci_[:], den[:], ALU.mult, B1)
    br, bi = T("br", (128, GS, 16)), T("bi", (128, GS, 16))
    for d in range(2):
        sl = slice(d * 64, (d + 1) * 64)
        for gq in range(8):
            gs_ = slice(gq * 16, (gq + 1) * 16)
            P.dma("sp", br[sl, gs_, :], I["b_re"][d, gs_].rearrange("g p h -> p g h"), writes=B1)
            P.dma("sp", bi[sl, gs_, :], I["b_im"][d, gs_].rearrange("g p h -> p g h"), writes=B1)
    bbr, bbi = T("bbr", (128, GS, 16)), T("bbi", (128, GS, 16))
    t1, t2 = T("t1", (128, GS, 16)), T("t2", (128, GS, 16))
    cfr_b = cr_[:].unsqueeze(2).broadcast_to([128, GS, 16])
    cfi_b = ci_[:].unsqueeze(2).broadcast_to([128, GS, 16])
    s5_cmul(P, "dve", bbr[:], bbi[:], cfr_b, cfi_b, br[:], bi[:], t1[:], t2[:], B1)
    ctr, cti = T("ctr", (128, GS, 16)), T("cti", (128, GS, 16))
    cn = [T("cn%d" % i, (128, 128)) for i in range(2)]
    cnb = [P.buf() for i in range(2)]
    n = 0
    for src, dst in ((I["c_re"], ctr), (I["c_im"], cti)):
        for o in range(GS // 8):
            c_, cb_ = cn[n % 2], cnb[n % 2]
            P.dma("sp", c_[:].rearrange("q (d p) -> q d p", d=2),
                  src[:, o * 8:(o + 1) * 8].rearrange("d g h p -> (g h) d p"), writes=[cb_])
            bank = C.rr4()
            P.op("pe", lambda e, c_=c_, bank=bank: e.transpose(out=C.ps[:, bank * 512: bank * 512 + 128], in_=c_[:], identity=C.ident[:]),
                 reads=[cb_, C.ident_b], writes=[C.psb[bank]])
            P.op("act", lambda e, dst=dst, o=o, bank=bank: e.activation(
                out=dst[:, o * 8:(o + 1) * 8, :].rearrange("p g h -> p (g h)"), in_=C.ps[:, bank * 512: bank * 512 + 128], func=AF.Copy),
                reads=[C.psb[bank]], writes=B1)
            n += 1
    bAr, bAi, bBr, bBi = T("bAr"), T("bAi"), T("bBr"), T("bBi")
    lo, hi = slice(0, 64), slice(64, 128)
    for dst, a, b in ((bAr, vr, lr), (bAi, vi, li), (bBr, lr, vr), (bBi, li, vi)):
        P.op("dve", lambda e, dst=dst, a=a: e.tensor_copy(out=dst[lo, :], in_=a[lo, :]), reads=B1, writes=B1)
        P.op("dve", lambda e, dst=dst, b=b: e.tensor_copy(out=dst[hi, :], in_=b[hi, :]), reads=B1, writes=B1)
    pw = {}
    for nm, (xr, xi) in (("A", (bAr, bAi)), ("B", (bBr, bBi))):
        pr, pi_ = PERS["pw%sr" % nm], PERS["pw%si" % nm]
        P.op("dve", lambda e, pr=pr: e.memset(pr[:, :, 0], 1.0), reads=B1, writes=B1)
        P.op("dve", lambda e, pi_=pi_: e.memset(pi_[:, :, 0], 0.0), reads=B1, writes=B1)
        for k in range(1, 9):
            s5_cmul(P, "dve", pr[:, :, k], pi_[:, :, k], pr[:, :, k - 1], pi_[:, :, k - 1], xr[:], xi[:], q[:], m[:], B1)
        pw[nm] = (pr, pi_)
    sBr, sBi, sCr, sCi, L8r, L8i = T("sBr"), T("sBi"), T("sCr"), T("sCi"), T("L8r"), T("L8i")
    pAr, pAi = pw["A"]
    pBr, pBi = pw["B"]
    cp = lambda dst, sl, src: P.op("dve", lambda e: e.tensor_copy(out=dst[sl, :], in_=src), reads=B1, writes=B1)
    cp(sBr, lo, pBr[lo, :, 7]); cp(sBi, lo, pBi[lo, :, 7])
    P.op("dve", lambda e: e.memset(sBr[hi, :], 1.0), reads=B1, writes=B1)
    P.op("dve", lambda e: e.memset(sBi[hi, :], 0.0), reads=B1, writes=B1)
    cp(sCr, lo, pBr[lo, :, 1]); cp(sCi, lo, pBi[lo, :, 1])
    cp(sCr, hi, pAr[hi, :, 8]); cp(sCi, hi, pAi[hi, :, 8])
    cp(L8r, lo, pBr[lo, :, 8]); cp(L8i, lo, pBi[lo, :, 8])
    cp(L8r, hi, pAr[hi, :, 8]); cp(L8i, hi, pAi[hi, :, 8])
    A2, B2 = T("A2", (128, 2, GS)), T("B2", (128, 2, GS))
    for j in range(2):
        P.op("dve", lambda e, j=j: e.tensor_copy(out=A2[:, j, :], in_=L8r[:]), reads=B1, writes=B1)
    P.op("dve", lambda e: e.tensor_copy(out=B2[:, 1, :], in_=L8i[:]), reads=B1, writes=B1)
    s5_ts(P, "dve", B2[:, 0, :], L8i[:], -1.0, ALU.mult, B1)
    A2l, B2l = PERS["A2l"], PERS["B2l"]
    xr_, xi_, yr_, yi_ = T("lvxr"), T("lvxi"), T("lvyr"), T("lvyi")
    P.op("dve", lambda e: e.tensor_copy(out=xr_[:], in_=L8r[:]), reads=B1, writes=B1)
    P.op("dve", lambda e: e.tensor_copy(out=xi_[:], in_=L8i[:]), reads=B1, writes=B1)
    for l in range(7):
        P.op("dve", lambda e, l=l: e.tensor_copy(out=A2l[:, l, :], in_=xr_[:]), reads=B1, writes=B1)
        P.op("dve", lambda e, l=l: e.tensor_copy(out=B2l[:, l, 1, :], in_=xi_[:]), reads=B1, writes=B1)
        s5_ts(P, "dve", B2l[:, l, 0, :], xi_[:], -1.0, ALU.mult, B1)
        if l < 6:
            s5_cmul(P, "dve", yr_[:], yi_[:], xr_[:], xi_[:], xr_[:], xi_[:], q[:], m[:], B1)
            P.op("dve", lambda e: e.tensor_copy(out=xr_[:], in_=yr_[:]), reads=B1, writes=B1)
            P.op("dve", lambda e: e.tensor_copy(out=xi_[:], in_=yi_[:]), reads=B1, writes=B1)
    return dict(pb=pb, bbr=bbr, bbi=bbi, ctr=ctr, cti=cti, pw=pw, sBr=sBr, sBi=sBi, sCr=sCr, sCi=sCi,
                L8r=L8r, L8i=L8i, A2=A2, B2=B2, A2l=A2l, B2l=B2l)


def s5_recur(P, eng, psl, W, Wb, nsteps, ascending, A2v, B2v, pb, init, tA, tB, tb_):
    order = range(nsteps) if ascending else range(nsteps - 1, -1, -1)
    prev = init
    for c in order:
        cur = W[psl, c]
        if prev is not None:
            P.op(eng, lambda e, prev=prev: e.tensor_tensor(out=tA[psl], in0=prev[:, 0:2, :], in1=A2v[psl], op=ALU.mult),
                 reads=[tb_, Wb, pb], writes=[tb_], relax=True)
            P.op(eng, lambda e, prev=prev: e.tensor_tensor(out=tB[psl], in0=prev[:, 1:3, :], in1=B2v[psl], op=ALU.mult),
                 reads=[tb_, Wb, pb], writes=[tb_], relax=True)
            P.op(eng, lambda e: e.tensor_tensor(out=tA[psl], in0=tA[psl], in1=tB[psl], op=ALU.add), reads=[tb_], writes=[tb_], relax=True)
            P.op(eng, lambda e, cur=cur: e.tensor_tensor(out=cur[:, 0:2, :], in0=cur[:, 0:2, :], in1=tA[psl], op=ALU.add),
                 reads=[tb_, Wb], writes=[Wb], relax=True)
        P.op(eng, lambda e, cur=cur: e.tensor_copy(out=cur[:, 2, :], in_=cur[:, 0, :]), reads=[Wb], writes=[Wb], relax=True)
        prev = cur


def s5_tree(P, eng, psl, W, Wb, nch, ascending, A2l, B2l, gb0, pb, tA, tB, tb_):
    Wv = W[psl]
    P.op(eng, lambda e: e.tensor_copy(out=Wv[:, :, 2, :], in_=Wv[:, :, 0, :]), reads=[Wb], writes=[Wb], relax=True)
    nlev = nch.bit_length() - 1
    for l in range(nlev):
        step = 2 ** (l + 1)
        half = 2 ** l
        npair = nch // step
        V5 = Wv.rearrange("p (m s) t g -> p m s t g", s=step)
        for m0 in range(0, npair, 16):
            n = min(16, npair - m0)
            if ascending:
                mult, other = V5[:, m0:m0 + n, half - 1], V5[:, m0:m0 + n, step - 1]
            else:
                mult, other = V5[:, m0:m0 + n, half], V5[:, m0:m0 + n, 0]
            a2 = A2l[psl, l, gb0:gb0 + NB].unsqueeze(1).unsqueeze(1).broadcast_to([64, n, 2, NB])
            b2 = B2l[psl, l, :, gb0:gb0 + NB].unsqueeze(1).broadcast_to([64, n, 2, NB])
            ta = tA[psl, 0:n]
            P.op(eng, lambda e, mult=mult, a2=a2, ta=ta: e.tensor_tensor(out=ta, in0=mult[:, :, 0:2, :], in1=a2, op=ALU.mult),
                 reads=[tb_, Wb, pb], writes=[tb_], relax=True)
            P.op(eng, lambda e, other=other, ta=ta: e.tensor_tensor(out=other[:, :, 0:2, :], in0=other[:, :, 0:2, :], in1=ta, op=ALU.add),
                 reads=[tb_, Wb], writes=[Wb], relax=True)
            P.op(eng, lambda e, mult=mult, b2=b2, ta=ta: e.tensor_tensor(out=ta, in0=mult[:, :, 1:3, :], in1=b2, op=ALU.mult),
                 reads=[tb_, Wb, pb], writes=[tb_], relax=True)
            P.op(eng, lambda e, other=other, ta=ta: e.tensor_tensor(out=other[:, :, 0:2, :], in0=other[:, :, 0:2, :], in1=ta, op=ALU.add),
                 reads=[tb_, Wb], writes=[Wb], relax=True)
            P.op(eng, lambda e, other=other: e.tensor_copy(out=other[:, :, 2, :], in_=other[:, :, 0, :]), reads=[Wb], writes=[Wb], relax=True)


def emit_s5a(P, C, I, hTd, hTdb):
    Q = emit_s5_params(P, C, I)
    pb = Q["pb"]
    lo, hi = slice(0, 64), slice(64, 128)
    outb = P.buf("s5a_out")
    P.dma("sp", I["L8d"][:, 0], Q["A2"][:], reads=[pb], writes=[outb])
    P.dma("sp", I["L8d"][:, 1], Q["B2"][:], reads=[pb], writes=[outb])
    sel = P.sbuf("s5_sel", [128, 64, 128], BF16)
    selb = P.buf()
    P.dma("pool", sel[:], I["sel"], writes=[selb])
    mf = P.sbuf("s5_mf", [128, 128], F32)
    mb_ = P.sbuf("s5_mb", [128, 128], F32)
    dv = P.sbuf("s5_dv", [128, GS], F32)
    cb_ = P.buf()
    P.dma("sp", mf[:], I["maskf"], writes=[cb_])
    P.dma("sp", mb_[:], I["maskb"], writes=[cb_])
    for j in range(8):
        P.dma("sp", dv[j * 16:(j + 1) * 16, :], I["ssm_d"].rearrange("(g h) -> h g", h=16), writes=[cb_], allow_slow_non_contiguous=True)
    identb = P.sbuf("s5_identb", [128, 128], BF16)
    identbb = P.buf()
    P.op("act", lambda e: e.activation(out=identb[:], in_=C.ident[:], func=AF.Copy), reads=[C.ident_b], writes=[identbb])
    MS = []
    for i in range(2):
        d_ = {nm: P.sbuf("s5_%s%d" % (nm, i), [128, 8, 128], BF16) for nm in ("BLr", "BLi", "CLr", "nCLi", "BcTr", "BcTi")}
        d_["Cc"] = P.sbuf("s5_Cc%d" % i, [128, 2, 8, 128], BF16)
        d_["b"] = P.buf()
        MS.append(d_)
    f32t = {nm: P.sbuf("s5_f_" + nm, [128, 8, 128], F32) for nm in ("Xr", "Xi", "t1", "t2")}
    fb = P.buf()
    W = P.sbuf("s5_W", [128, NCH, 3, NB], F32)
    Wb = [P.buf(), P.buf()]
    Wc = P.sbuf("s5_Wc", [128, NCC, 3, NB], F32)
    Wcb = [P.buf(), P.buf()]
    tb_ = [P.buf() for i in range(2)]
    tT4 = [P.sbuf("s5_tT4%d" % i, [128, 16, 2, NB], F32) for i in range(1)] * 2
    Eo = P.sbuf("s5_Eo", [128, 2, GS], F32)
    Ec = P.sbuf("s5_Ec", [128, 2, GS], F32)
    Eb = [P.buf(), P.buf()]
    Tg = [P.sbuf("s5_Tg%d" % i, [128, 128], BF16) for i in range(2)]
    Tgb = [P.buf() for i in range(2)]
    tT = [P.sbuf("s5_tT%d" % i, [128, 128], F32) for i in range(2)]
    tTb = P.buf()
    Bc = [P.sbuf("s5_Bc%d" % i, [128, 2, 128], BF16) for i in range(2)]
    Bcb = [P.buf() for i in range(2)]
    Ug = [P.sbuf("s5_Ug%d" % i, [128, NCH + NCC], BF16) for i in range(2)]
    Ugb = [P.buf() for i in range(2)]
    Yi = [P.sbuf("s5_Yi%d" % i, [128, 128], F32) for i in range(2)]
    Yib = [P.buf() for i in range(2)]
    pAr, pAi = Q["pw"]["A"]
    pBr, pBi = Q["pw"]["B"]
    C.rr_n = 4
    hk = [P.sbuf("s5_hk%d" % i, [128, NCH * 8 + NCC * 8], BF16) for i in range(2)]
    hkb = [P.buf() for i in range(2)]
    for o in range(GS // 8):
        M = MS[o % 2]
        Mb = M["b"]
        g0 = o * 8
        P.dma("sp", hk[o % 2][:], hTd[o], reads=[hTdb], writes=[hkb[o % 2]])
        eng = "dve"
        bufs = [fb, pb, Mb]
        bc4 = lambda ap: ap.unsqueeze(2).broadcast_to([128, 8, 8, 16])
        pw4 = lambda ap: ap.unsqueeze(3).broadcast_to([128, 8, 8, 16])
        v4 = lambda t: t[:].rearrange("p g (j h) -> p g j h", j=8)
        sc3 = lambda ap: ap.unsqueeze(2).broadcast_to([128, 8, 128])
        s5_cmul(P, eng, v4(f32t["Xr"]), v4(f32t["Xi"]), bc4(Q["bbr"][:, g0:g0 + 8, :]), bc4(Q["bbi"][:, g0:g0 + 8, :]),
                pw4(pAr[:, g0:g0 + 8, 0:8]), pw4(pAi[:, g0:g0 + 8, 0:8]), v4(f32t["t1"]), v4(f32t["t2"]), bufs)
        P.op("act", lambda e, M=M: e.activation(out=M["BLr"][:], in_=f32t["Xr"][:], func=AF.Copy), reads=[fb], writes=[Mb])
        P.op("act", lambda e, M=M: e.activation(out=M["BLi"][:], in_=f32t["Xi"][:], func=AF.Copy), reads=[fb], writes=[Mb])
        s5_tt(P, eng, f32t["t1"][:], f32t["Xr"][:], sc3(Q["sBr"][:, g0:g0 + 8]), ALU.mult, bufs)
        s5_tt(P, eng, f32t["t2"][:], f32t["Xi"][:], sc3(Q["sBi"][:, g0:g0 + 8]), ALU.mult, bufs)
        P.op(eng, lambda e, M=M: e.tensor_tensor(out=M["BcTr"][:], in0=f32t["t1"][:], in1=f32t["t2"][:], op=ALU.subtract),
             reads=[fb], writes=[Mb])
        s5_tt(P, eng, f32t["t1"][:], f32t["Xr"][:], sc3(Q["sBi"][:, g0:g0 + 8]), ALU.mult, bufs)
        s5_tt(P, eng, f32t["t2"][:], f32t["Xi"][:], sc3(Q["sBr"][:, g0:g0 + 8]), ALU.mult, bufs)
        P.op(eng, lambda e, M=M: e.tensor_tensor(out=M["BcTi"][:], in0=f32t["t1"][:], in1=f32t["t2"][:], op=ALU.add),
             reads=[fb], writes=[Mb])
        s5_cmul(P, eng, v4(f32t["Xr"]), v4(f32t["Xi"]), bc4(Q["ctr"][:, g0:g0 + 8, :]), bc4(Q["cti"][:, g0:g0 + 8, :]),
                pw4(pBr[:, g0:g0 + 8, 0:8]), pw4(pBi[:, g0:g0 + 8, 0:8]), v4(f32t["t1"]), v4(f32t["t2"]), bufs)
        P.op("act", lambda e, M=M: e.activation(out=M["CLr"][:], in_=f32t["Xr"][:], func=AF.Copy), reads=[fb], writes=[Mb])
        P.op("act", lambda e, M=M: e.activation(out=M["nCLi"][:], in_=f32t["Xi"][:], func=AF.Copy, scale=-1.0), reads=[fb], writes=[Mb])
        s5_tt(P, eng, f32t["t1"][:], f32t["Xr"][:], sc3(Q["sCr"][:, g0:g0 + 8]), ALU.mult, bufs)
        s5_tt(P, eng, f32t["t2"][:], f32t["Xi"][:], sc3(Q["sCi"][:, g0:g0 + 8]), ALU.mult, bufs)
        P.op(eng, lambda e, M=M: e.tensor_tensor(out=M["Cc"][:, 0], in0=f32t["t1"][:], in1=f32t["t2"][:], op=ALU.subtract),
             reads=[fb], writes=[Mb])
        s5_tt(P, eng, f32t["t1"][:], f32t["Xr"][:], sc3(Q["sCi"][:, g0:g0 + 8]), ALU.mult, bufs)
        s5_tt(P, eng, f32t["t2"][:], f32t["Xi"][:], sc3(Q["sCr"][:, g0:g0 + 8]), ALU.mult, bufs)
        s5_tt(P, eng, f32t["t1"][:], f32t["t1"][:], f32t["t2"][:], ALU.add, bufs)
        P.op(eng, lambda e, M=M: e.tensor_scalar(out=M["Cc"][:, 1], in0=f32t["t1"][:], scalar1=-1.0, scalar2=None, op0=ALU.mult),
             reads=[fb], writes=[Mb])
        P.dma("sp", I["Ccd"][o], M["Cc"][:].rearrange("p a g m -> p (a g m)"), reads=[Mb], writes=[outb])
        ONLY = "TBUYV"
        for gg in range(8):
            g = g0 + gg
            gl = g % NB
            par = g % 2
            if "T" in ONLY:
                tbanks = (4, 5)
                for half, sl in enumerate((lo, hi)):
                    bank = tbanks[half]
                    P.op("pe", lambda e, M=M, gg=gg, sl=sl, bank=bank: e.matmul(
                        C.ps[:, bank * 512: bank * 512 + 128], lhsT=M["BLr"][sl, gg, :], rhs=M["CLr"][sl, gg, :], start=True, stop=False),
                        reads=[Mb], writes=[C.psb[bank]])
                    P.op("pe", lambda e, M=M, gg=gg, sl=sl, bank=bank: e.matmul(
                        C.ps[:, bank * 512: bank * 512 + 128], lhsT=M["BLi"][sl, gg, :], rhs=M["nCLi"][sl, gg, :], start=False, stop=True),
                        reads=[Mb], writes=[C.psb[bank]])
                P.op("dve", lambda e: e.tensor_tensor(out=tT[0][:], in0=C.ps[:, 4 * 512: 4 * 512 + 128], in1=mf[:], op=ALU.mult),
                     reads=[C.psb[4], cb_], writes=[tTb])
                P.op("dve", lambda e: e.tensor_tensor(out=tT[1][:], in0=C.ps[:, 5 * 512: 5 * 512 + 128], in1=mb_[:], op=ALU.mult),
                     reads=[C.psb[5], cb_], writes=[tTb])
                P.op("dve", lambda e: e.tensor_tensor(out=tT[0][:], in0=tT[0][:], in1=tT[1][:], op=ALU.add), reads=[tTb], writes=[tTb])
                P.op("dve", lambda e, g=g, par=par: e.scalar_tensor_tensor(out=Tg[par][:], in0=C.ident[:], scalar=dv[:, g:g + 1], in1=tT[0][:],
                                                                      op0=ALU.mult, op1=ALU.add),
                     reads=[tTb, cb_, C.ident_b], writes=[Tgb[par]])
            if "B" in ONLY:
                bank = C.rr4()
                for a_, nm in enumerate(("BcTr", "BcTi")):
                    P.op("pe", lambda e, M=M, nm=nm, gg=gg, a_=a_, bank=bank: e.transpose(
                        out=C.ps[:, bank * 512 + a_ * 64: bank * 512 + a_ * 64 + 64].bitcast(BF16), in_=M[nm][:, gg, :], identity=identb[:]),
                        reads=[Mb, identbb], writes=[C.psb[bank]])
                P.op("act", lambda e, par=par, bank=bank: e.activation(
                    out=Bc[par][:].rearrange("p a m -> p (a m)"), in_=C.ps[:, bank * 512: bank * 512 + 128].bitcast(BF16), func=AF.Copy),
                    reads=[C.psb[bank]], writes=[Bcb[par]])
            if "U" in ONLY:
                bank = C.rr4()
                hv = hk[o % 2][:].rearrange("p (c j) -> p j c", j=8)
                hTb = hkb[o % 2]
                for j in range(8):
                    P.op("pe", lambda e, j=j, gg=gg, bank=bank, hv=hv: e.matmul(
                        C.ps[:, bank * 512: bank * 512 + NCH + NCC], lhsT=sel[:, gg * 8 + j, :], rhs=hv[:, j, :], start=(j == 0), stop=(j == 7)),
                        reads=[selb, hTb], writes=[C.psb[bank]])
                P.op("act", lambda e, par=par, bank=bank: e.activation(out=Ug[par][:], in_=C.ps[:, bank * 512: bank * 512 + NCH + NCC], func=AF.Copy),
                     reads=[C.psb[bank]], writes=[Ugb[par]])
            if "Y" in ONLY:
                bank = C.rr4()
                P.op("pe", lambda e, par=par, bank=bank: e.matmul(C.ps[:, bank * 512: bank * 512 + NCH], lhsT=Tg[par][:], rhs=Ug[par][:, 0:NCH],
                                                                start=True, stop=True), reads=[Tgb[par], Ugb[par]], writes=[C.psb[bank]])
                P.op("act", lambda e, par=par, bank=bank: e.activation(out=Yi[par][:], in_=C.ps[:, bank * 512: bank * 512 + NCH], func=AF.Copy),
                     reads=[C.psb[bank]], writes=[Yib[par]])
                P.dma("sp", I["Yd"][g], Yi[par][:], reads=[Yib[par]], writes=[outb])
            if "V" in ONLY:
                for a_ in range(2):
                    bank = C.rr4()
                    P.op("pe", lambda e, par=par, a_=a_, bank=bank: e.matmul(C.ps[:, bank * 512: bank * 512 + NCH + NCC], lhsT=Bc[par][:, a_, :],
                                                                           rhs=Ug[par][:], start=True, stop=True),
                         reads=[Bcb[par], Ugb[par]], writes=[C.psb[bank]])
                    P.op("dve", lambda e, a_=a_, gl=gl, bank=bank: e.tensor_copy(out=W[:, :, a_, gl], in_=C.ps[:, bank * 512: bank * 512 + NCH]),
                         reads=[C.psb[bank]], writes=Wb)
                    P.op("act", lambda e, a_=a_, gl=gl, bank=bank: e.activation(out=Wc[:, :, a_, gl], in_=C.ps[:, bank * 512 + NCH: bank * 512 + NCH + NCC],
                                                                              func=AF.Copy),
                         reads=[C.psb[bank]], writes=Wcb)
            if gl == NB - 1:
                bi_ = g // NB
                gb0 = bi_ * NB
                P.dma("sp", I["Vd"][bi_], W[:].rearrange("p c s g -> p (c s g)"), reads=Wb, writes=[outb])
                A2v = Q["A2"][:, :, gb0:gb0 + NB]
                B2v = Q["B2"][:, :, gb0:gb0 + NB]
                s5_tree(P, "dve", lo, W, Wb[0], NCH, True, Q["A2l"], Q["B2l"], gb0, pb, tT4[0], tT4[1], tb_[0])
                s5_tree(P, "pool", hi, W, Wb[1], NCH, False, Q["A2l"], Q["B2l"], gb0, pb, tT4[0], tT4[1], tb_[1])
                s5_tree(P, "dve", lo, Wc, Wcb[0], NCC, True, Q["A2l"], Q["B2l"], gb0, pb, tT4[0], tT4[1], tb_[0])
                s5_tree(P, "pool", hi, Wc, Wcb[1], NCC, False, Q["A2l"], Q["B2l"], gb0, pb, tT4[0], tT4[1], tb_[1])
                P.op("dve", lambda e, gb0=gb0: e.tensor_copy(out=Eo[lo, :, gb0:gb0 + NB], in_=W[lo, NCH - 1, 0:2, :]), reads=[Wb[0]], writes=[Eb[0]])
                P.op("pool", lambda e, gb0=gb0: e.tensor_copy(out=Eo[hi, :, gb0:gb0 + NB], in_=W[hi, 0, 0:2, :]), reads=[Wb[1]], writes=[Eb[1]])
                P.op("dve", lambda e, gb0=gb0: e.tensor_copy(out=Ec[lo, :, gb0:gb0 + NB], in_=Wc[lo, NCC - 1, 0:2, :]), reads=[Wcb[0]], writes=[Eb[0]])
                P.op("pool", lambda e, gb0=gb0: e.tensor_copy(out=Ec[hi, :, gb0:gb0 + NB], in_=Wc[hi, 0, 0:2, :]), reads=[Wcb[1]], writes=[Eb[1]])
    P.dma("sp", I["Eown"], Eo[:].rearrange("p a g -> p (a g)"), reads=Eb, writes=[outb])
    P.dma("sp", I["Ectx"], Ec[:].rearrange("p a g -> p (a g)"), reads=Eb, writes=[outb])
    return outb


def host_s5_consts():
    sel = np.zeros((128, 64, 128), np.float32)
    for gm in range(8):
        for j in range(8):
            for h in range(16):
                sel[16 * gm + h, gm * 8 + j, j * 16 + h] = 1.0
    selT = np.ascontiguousarray(sel.transpose(2, 1, 0))
    jj = np.arange(128) // 16
    maskf = (jj[None, :] >= jj[:, None]).astype(np.float32)
    maskb = (jj[:, None] >= jj[None, :]).astype(np.float32)
    return sel, selT, maskf, maskb


def emit_s5b(P, C, I, gyT, gyTb):
    lo, hi = slice(0, 64), slice(64, 128)
    pb = P.buf("s5b_p")
    B1 = [pb]

    def T(name, shape=(128, GS), dt=F32):
        return P.sbuf("s5b_" + name, list(shape), dt)
    A2, B2 = T("A2", (128, 2, GS)), T("B2", (128, 2, GS))
    P.dma("sp", A2[:], I["L8d"][:, 0], writes=B1)
    P.dma("sp", B2[:], I["L8d"][:, 1], writes=B1)
    Lr, Li, t1, t2, nr_, ni_ = T("Lr"), T("Li"), T("t1"), T("t2"), T("nr"), T("ni")
    P.op("dve", lambda e: e.tensor_copy(out=Lr[:], in_=A2[:, 0, :]), reads=B1, writes=B1)
    P.op("dve", lambda e: e.tensor_copy(out=Li[:], in_=B2[:, 1, :]), reads=B1, writes=B1)
    for _ in range(7):
        s5_cmul(P, "dve", nr_[:], ni_[:], Lr[:], Li[:], Lr[:], Li[:], t1[:], t2[:], B1)
        P.op("dve", lambda e: e.tensor_copy(out=Lr[:], in_=nr_[:]), reads=B1, writes=B1)
        P.op("dve", lambda e: e.tensor_copy(out=Li[:], in_=ni_[:]), reads=B1, writes=B1)
    init3 = T("init3", (128, 3, GS))
    El = T("El", (128, 2, GS))
    P.dma("sp", init3[:, 0:2, :], I["Elist"][0].rearrange("p (a g) -> p a g", a=2), writes=B1)
    for s_ in range(1, 9):
        P.dma("sp", El[:], I["Elist"][s_].rearrange("p (a g) -> p a g", a=2), reads=B1, writes=B1)
        s5_cmul(P, "dve", nr_[:], ni_[:], init3[:, 0, :], init3[:, 1, :], Lr[:], Li[:], t1[:], t2[:], B1)
        s5_tt(P, "dve", init3[:, 0, :], nr_[:], El[:, 0, :], ALU.add, B1)
        s5_tt(P, "dve", init3[:, 1, :], ni_[:], El[:, 1, :], ALU.add, B1)
    P.op("dve", lambda e: e.tensor_copy(out=init3[:, 2, :], in_=init3[:, 0, :]), reads=B1, writes=B1)
    selT = P.sbuf("s5b_selT", [128, 64, 128], BF16)
    selTb = P.buf()
    P.dma("pool", selT[:], I["selT"], writes=[selTb])
    W = P.sbuf("s5b_W", [128, NCH, 3, NB], F32)
    Wb = [P.buf(), P.buf()]
    Inb = P.sbuf("s5b_Inb", [128, 2, NB, NCH], BF16)
    Inbb = P.buf()
    tA = [P.sbuf("s5b_tA%d" % i, [128, 2, NB], F32) for i in range(2)]
    tB = [P.sbuf("s5b_tB%d" % i, [128, 2, NB], F32) for i in range(2)]
    tb_ = [P.buf() for i in range(2)]
    Cc = [P.sbuf("s5b_Cc%d" % i, [128, 2, 8, 128], BF16) for i in range(2)]
    Ccb = [P.buf() for i in range(2)]
    Yi = [P.sbuf("s5b_Yi%d" % i, [128, 128], F32) for i in range(2)]
    Yib = [P.buf() for i in range(2)]
    Yo = [P.sbuf("s5b_Yo%d" % i, [128, 8, 128], BF16) for i in range(2)]
    Yob = [P.buf() for i in range(2)]
    g1_ = P.sbuf("s5b_g1", [128, 512], F32)
    g2_ = P.sbuf("s5b_g2", [128, 512], F32)
    gb_ = P.buf()
    C.rr_n = 4
    for bi_ in range(GS // NB):
        gb0 = bi_ * NB
        P.dma("sp", W[:].rearrange("p c s g -> p (c s g)"), I["Vd"][bi_], writes=Wb)
        A2v = A2[:, :, gb0:gb0 + NB]
        B2v = B2[:, :, gb0:gb0 + NB]
        iv = init3[:, :, gb0:gb0 + NB]
        s5_recur(P, "dve", lo, W, Wb[0], NCH, True, A2v, B2v, pb, iv[lo], tA[0], tB[0], tb_[0])
        s5_recur(P, "pool", hi, W, Wb[1], NCH, False, A2v, B2v, pb, iv[hi], tA[1], tB[1], tb_[1])
        for ri in range(2):
            P.op("dve", lambda e, ri=ri: e.tensor_copy(out=Inb[lo, ri, :, 1:NCH], in_=W[lo, 0:NCH - 1, ri, :].rearrange("p c g -> p g c")),
                 reads=[Wb[0]], writes=[Inbb])
            P.op("dve", lambda e, ri=ri, iv=iv: e.tensor_copy(out=Inb[lo, ri, :, 0], in_=iv[lo, ri, :]), reads=B1, writes=[Inbb])
            P.op("pool", lambda e, ri=ri: e.tensor_copy(out=Inb[hi, ri, :, 0:NCH - 1], in_=W[hi, 1:NCH, ri, :].rearrange("p c g -> p g c")),
                 reads=[Wb[1]], writes=[Inbb])
            P.op("pool", lambda e, ri=ri, iv=iv: e.tensor_copy(out=Inb[hi, ri, :, NCH - 1], in_=iv[hi, ri, :]), reads=B1, writes=[Inbb])
        for oo in range(NB // 8):
            o = bi_ * (NB // 8) + oo
            cc, ccb = Cc[o % 2], Ccb[o % 2]
            yo, yob = Yo[o % 2], Yob[o % 2]
            P.dma("sp", cc[:].rearrange("p a g m -> p (a g m)"), I["Ccd"][o], writes=[ccb])
            for gg in range(8):
                g = o * 8 + gg
                gl = g % NB
                yi, yib = Yi[g % 2], Yib[g % 2]
                P.dma("sp", yi[:], I["Yd"][g], writes=[yib])
                bank = C.rr4()
                for a_ in range(2):
                    P.op("pe", lambda e, a_=a_, gg=gg, gl=gl, cc=cc, bank=bank: e.matmul(
                        C.ps[:, bank * 512: bank * 512 + NCH], lhsT=cc[:, a_, gg, :], rhs=Inb[:, a_, gl, :], start=(a_ == 0), stop=(a_ == 1)),
                        reads=[ccb, Inbb], writes=[C.psb[bank]])
                P.op("dve", lambda e, gg=gg, yo=yo, yi=yi, bank=bank: e.tensor_tensor(
                    out=yo[:, gg, :], in0=C.ps[:, bank * 512: bank * 512 + NCH], in1=yi[:], op=ALU.add),
                    reads=[C.psb[bank], yib], writes=[yob])
            for half in range(2):
                bank = 4 + half
                first = True
                for gg in range(8):
                    for i in range(8):
                        outv = C.ps[:, bank * 512:(bank + 1) * 512].rearrange("p (c i) -> p i c", i=8)[:, i, :]
                        P.op("pe", lambda e, gg=gg, i=i, half=half, outv=outv, yo=yo, first=first: e.matmul(
                            outv, lhsT=selT[:, gg * 8 + i, :], rhs=yo[:, gg, half * 64:(half + 1) * 64],
                            start=first, stop=(gg == 7 and i == 7), skip_group_check=True),
                            reads=[selTb, yob], writes=[C.psb[bank]])
                        first = False
                xp = C.ps[:, bank * 512:(bank + 1) * 512]
                P.op("act", lambda e, xp=xp: e.activation(out=g1_[:], in_=xp, func=AF.Square), reads=[C.psb[bank]], writes=[gb_])
                P.op("dve", lambda e: e.tensor_scalar(out=g1_[:], in0=g1_[:], scalar1=0.044715, scalar2=1.0, op0=ALU.mult, op1=ALU.add),
                     reads=[gb_], writes=[gb_])
                P.op("dve", lambda e, xp=xp: e.tensor_tensor(out=g2_[:], in0=g1_[:], in1=xp, op=ALU.mult), reads=[gb_, C.psb[bank]], writes=[gb_])
                P.op("act", lambda e: e.activation(out=g2_[:], in_=g2_[:], func=AF.Sigmoid, scale=1.5957691216057308), reads=[gb_], writes=[gb_])
                P.op("dve", lambda e, xp=xp, o=o, half=half: e.tensor_tensor(out=gyT[:, o, half * 512:(half + 1) * 512], in0=g2_[:], in1=xp, op=ALU.mult),
                     reads=[gb_, C.psb[bank]], writes=[gyTb])


def emit_glu(P, C, gyT, gyTb, xin, xout, outb, w_glu, b_glu, gate_ap, ntile=8):
    bt = P.sbuf("gl_bt", [128, 2 * D], F32)
    gt = P.sbuf("gl_gt", [128, D], F32)
    tb = P.buf()
    P.dma("sp", bt[:], b_glu.partition_broadcast(128), writes=[tb])
    P.dma("sp", gt[:], gate_ap.partition_broadcast(128), writes=[tb])
    xs = [P.sbuf("gl_xs%d" % i, [128, 512], F32) for i in range(2)]
    xsb = [P.buf() for i in range(2)]
    sv = [P.sbuf("gl_sv%d" % i, [128, 512], F32) for i in range(2)]
    sg = [P.sbuf("gl_sg%d" % i, [128, 512], F32) for i in range(2)]
    svb = [P.buf() for i in range(2)]
    C.rr_n = 4
    it = 0
    for cb in range(D // 512):
        uv, uvb = C.next_wu()
        ug, ugb = C.next_wu()
        wv = uv[:, 0:KC * 512].rearrange("p (k c) -> p k c", k=KC)
        wg = ug[:, 0:KC * 512].rearrange("p (k c) -> p k c", k=KC)
        P.dma("pool", wv, w_glu[:, cb * 512:(cb + 1) * 512].rearrange("(k p) c -> p k c", p=128), writes=[uvb])
        P.dma("pool", wg, w_glu[:, D + cb * 512: D + (cb + 1) * 512].rearrange("(k p) c -> p k c", p=128), writes=[ugb])
        for t in range(ntile):
            bv = C.rr4()
            bg = C.rr4()
            for k in range(KC):
                P.op("pe", lambda e, k=k, t=t, bv=bv, wv=wv: e.matmul(C.ps[:, bv * 512:(bv + 1) * 512], lhsT=gyT[:, k, t * 128:(t + 1) * 128],
                                                                    rhs=wv[:, k, :], start=(k == 0), stop=(k == KC - 1)),
                     reads=[gyTb, uvb], writes=[C.psb[bv]])
            for k in range(KC):
                P.op("pe", lambda e, k=k, t=t, bg=bg, wg=wg: e.matmul(C.ps[:, bg * 512:(bg + 1) * 512], lhsT=gyT[:, k, t * 128:(t + 1) * 128],
                                                                    rhs=wg[:, k, :], start=(k == 0), stop=(k == KC - 1)),
                     reads=[gyTb, ugb], writes=[C.psb[bg]])
            x_, xb_ = xs[it % 2], xsb[it % 2]
            v_, g_, vb_ = sv[it % 2], sg[it % 2], svb[it % 2]
            P.dma("sp", x_[:], xin[t * 128:(t + 1) * 128, cb * 512:(cb + 1) * 512], writes=[xb_])
            P.op("dve", lambda e, g_=g_, bg=bg, cb=cb: e.tensor_tensor(out=g_[:], in0=C.ps[:, bg * 512:(bg + 1) * 512],
                                                                    in1=bt[:, D + cb * 512: D + (cb + 1) * 512], op=ALU.add),
                 reads=[C.psb[bg], tb], writes=[vb_])
            P.op("act", lambda e, g_=g_: e.activation(out=g_[:], in_=g_[:], func=AF.Sigmoid), reads=[vb_], writes=[vb_])
            P.op("dve", lambda e, v_=v_, bv=bv, cb=cb: e.tensor_tensor(out=v_[:], in0=C.ps[:, bv * 512:(bv + 1) * 512],
                                                                    in1=bt[:, cb * 512:(cb + 1) * 512], op=ALU.add),
                 reads=[C.psb[bv], tb], writes=[vb_])
            P.op("pool", lambda e, v_=v_, g_=g_: e.tensor_tensor(out=v_[:], in0=v_[:], in1=g_[:], op=ALU.mult), reads=[vb_], writes=[vb_])
            P.op("pool", lambda e, v_=v_, cb=cb: e.tensor_tensor(out=v_[:], in0=v_[:], in1=gt[:, cb * 512:(cb + 1) * 512], op=ALU.mult),
                 reads=[vb_, tb], writes=[vb_])
            P.op("pool", lambda e, v_=v_, x_=x_: e.tensor_tensor(out=x_[:], in0=v_[:], in1=x_[:], op=ALU.add), reads=[vb_, xb_], writes=[xb_])
            P.dma("sp", xout[t * 128:(t + 1) * 128, cb * 512:(cb + 1) * 512], x_[:], reads=[xb_], writes=[outb])
            it += 1


NMOD = 6 * D


def emit_mod(P, C, c_ap, cctx_ap, ada_w, ada_b, mod_d, modb):
    cf = P.sbuf("md_cf", [128, 2, KC], F32)
    cfb = P.buf()
    load_fm(P, cf[:, 0, :], cfb, c_ap)
    load_fm(P, cf[:, 1, :], cfb, cctx_ap)
    P.op("act", lambda e: e.activation(out=cf[:], in_=cf[:], func=AF.Silu), reads=[cfb], writes=[cfb])
    cT = P.sbuf("md_cT", [128, KC, 2], BF16)
    cTb = P.buf()
    P.op("dve", lambda e: e.tensor_copy(out=cT[:], in_=cf[:].rearrange("p v k -> p k v")), reads=[cfb], writes=[cTb])
    bt = [P.sbuf("md_bt%d" % i, [2, 512], F32) for i in range(2)]
    btb = [P.buf() for i in range(2)]
    ob = [P.sbuf("md_o%d" % i, [2, 512], F32) for i in range(2)]
    obb = [P.buf() for i in range(2)]
    C.rr_n = 4
    it = 0
    for L in range(2):
        for nb in range(NMOD // 512):
            u, ub = C.next_wu()
            wv = u[:, 0:KC * 512].rearrange("p (k c) -> p k c", k=KC)
            P.dma("pool", wv, ada_w[L][:, nb * 512:(nb + 1) * 512].rearrange("(k p) c -> p k c", p=128), writes=[ub])
            bank = C.rr4()
            for k in range(KC):
                P.op("pe", lambda e, k=k, bank=bank, wv=wv: e.matmul(C.ps[0:2, bank * 512:(bank + 1) * 512], lhsT=cT[:, k, :], rhs=wv[:, k, :],
                                                                   start=(k == 0), stop=(k == KC - 1)), reads=[cTb, ub], writes=[C.psb[bank]])
            o_, ob_ = ob[it % 2], obb[it % 2]
            b_, bb_ = bt[it % 2], btb[it % 2]
            P.dma("sp", b_[:], ada_b[L, nb * 512:(nb + 1) * 512].partition_broadcast(2), writes=[bb_])
            P.op("dve", lambda e, o_=o_, bank=bank, b_=b_: e.tensor_tensor(out=o_[:], in0=C.ps[0:2, bank * 512:(bank + 1) * 512],
                                                                         in1=b_[:], op=ALU.add),
                 reads=[C.psb[bank], bb_], writes=[ob_])
            P.dma("sp", mod_d[L, :, nb * 512:(nb + 1) * 512], o_[:], reads=[ob_], writes=[modb])
            it += 1


def mod_slices(mod_d, L, which):
    return [mod_d[L, v, which * D:(which + 1) * D] for v in range(2)]


def emit_final_norm(P, C, xin, xinb, xout, outb, gw_ap, ntile=8):
    gt = P.sbuf("fn_g", [128, D], F32)
    gtb = P.buf()
    P.dma("sp", gt[:], gw_ap.partition_broadcast(128), writes=[gtb])
    xt = [P.sbuf("fn_x%d" % i, [128, D], F32) for i in range(2)]
    xtb = [P.buf() for i in range(2)]
    xn = [P.sbuf("fn_n%d" % i, [128, D], F32) for i in range(2)]
    xnb = [P.buf() for i in range(2)]
    ssq = [P.sbuf("fn_s%d" % i, [128, 1], F32) for i in range(2)]
    for t in range(ntile):
        x_, xb_, n_, nb_, s_ = xt[t % 2], xtb[t % 2], xn[t % 2], xnb[t % 2], ssq[t % 2]
        P.dma("sp", x_[:], xin[t * 128:(t + 1) * 128, :], reads=[xinb], writes=[xb_])
        P.op("act", lambda e, x_=x_, n_=n_, s_=s_: e.activation(out=n_[:], in_=x_[:], func=AF.Square, accum_out=s_[:]), reads=[xb_], writes=[nb_])
        P.op("dve", lambda e, s_=s_: e.tensor_scalar(out=s_[:], in0=s_[:], scalar1=1.0 / D, scalar2=EPS, op0=ALU.mult, op1=ALU.add),
             reads=[nb_], writes=[nb_])
        P.op("act", lambda e, s_=s_: e.activation(out=s_[:], in_=s_[:], func=AF.Sqrt), reads=[nb_], writes=[nb_])
        P.op("dve", lambda e, s_=s_: e.reciprocal(out=s_[:], in_=s_[:]), reads=[nb_], writes=[nb_])
        P.op("act", lambda e, x_=x_, n_=n_, s_=s_: e.activation(out=n_[:], in_=x_[:], func=AF.Copy, scale=s_[:]), reads=[xb_, nb_], writes=[nb_])
        P.op("dve", lambda e, n_=n_: e.tensor_tensor(out=n_[:], in0=n_[:], in1=gt[:], op=ALU.mult), reads=[nb_, gtb], writes=[nb_])
        P.dma("sp", xout[t * 128:(t + 1) * 128, :], n_[:], reads=[nb_], writes=[outb])


S5_KEYS = ("a_re", "a_im", "log_dt", "b_re", "b_im", "c_re", "c_im")


def build_launch1():
    P = Prog()
    nc = P.nc
    P.outb = P.buf("out")

    def inp(name, shape):
        return nc.dram_tensor(name, list(shape), F32, kind="ExternalInput").ap()

    def outp(name, shape, dt=F32, kind="ExternalOutput"):
        return nc.dram_tensor(name, list(shape), dt, kind=kind).ap()
    I = dict(ident=inp("ident", [128, 128]), rperm=inp("rperm", [128, 128]), xh=inp("xh", [1536, D]), ctx=inp("ctx", [256, D]),
             c=inp("c", [D]), c_ctx=inp("c_ctx", [D]), ada_w=inp("ada_w", [2, D, NMOD]), ada_b=inp("ada_b", [2, NMOD]),
             norm_mix=inp("norm_mix", [2, D]), norm_ffn=inp("norm_ffn", [2, D]),
             w_in=inp("w_in", [D, 4608]), w_out=inp("w_out", [D, D]), nab=inp("nab", [8, 128, 3200]),
             swm=inp("swm", [128, 1152]), ropec=inp("ropec", [128, 1536]), ropes=inp("ropes", [128, 1536]), sink=inp("sink", [8]),
             w1=inp("w1", [D, DFF]), w3=inp("w3", [D, DFF]), w2=inp("w2", [DFF, D]),
             a_re=inp("a_re", [2, 128, 64]), a_im=inp("a_im", [2, 128, 64]), log_dt=inp("log_dt", [2, 128]),
             b_re=inp("b_re", [2, 128, 64, 16]), b_im=inp("b_im", [2, 128, 64, 16]),
             c_re=inp("c_re", [2, 128, 16, 64]), c_im=inp("c_im", [2, 128, 16, 64]),
             ssm_d=inp("ssm_d", [D]), sel=inp("sel", [128, 64, 128]), maskf=inp("maskf", [128, 128]), maskb=inp("maskb", [128, 128]))
    I["Vd"] = outp("Vd", [4, 128, NCH * 3 * NB])
    I["Yd"] = outp("Yd", [GS, 128, 128])
    I["Ccd"] = outp("Ccd", [16, 128, 2048], BF16)
    I["Eown"] = outp("Eown", [128, 2 * GS])
    I["Ectx"] = outp("Ectx", [128, 2 * GS])
    I["L8d"] = outp("L8d", [128, 2, 2, GS])
    mod_d = outp("mod_d", [2, 2, NMOD])
    x2 = outp("x2", [1280, D])
    I["oT_d"] = outp("oT_d", [2048, 1280], BF16, kind="Internal")
    x1 = outp("x1", [1280, D], kind="Internal")
    hTd = outp("hTd", [KC, 128, 1280], BF16, kind="Internal")
    modb = P.buf("modd")
    C = Ctx(P, I["ident"])
    with P.scope():
        C.alloc_wu(3)
        emit_mod(P, C, I["c"], I["c_ctx"], I["ada_w"], I["ada_b"], mod_d, modb)
    I["gw"] = I["norm_mix"][0]
    I["sc"] = mod_slices(mod_d, 0, 1)
    I["sh"] = mod_slices(mod_d, 0, 0)
    with P.scope():
        C.alloc_wu(3)
        oTd_b = emit_attn(P, C, I)
    x1b = P.buf("x1")
    with P.scope():
        C.alloc_wu(3)
        _, _, G = emit_modprep(P, C, "om", I["gw"], I["sc"], I["sh"], mod_slices(mod_d, 0, 2), 2)
        xsrc = lambda t: (I["xh"][256 + t * 128: 256 + (t + 1) * 128, :] if t < 8 else I["ctx"][(t - 8) * 128:(t - 7) * 128, :])
        emit_outproj(P, C, I["oT_d"], oTd_b, xsrc, x1, x1b, I["w_out"], G, 10, [0] * 8 + [1] * 2)
    with P.scope():
        C.alloc_wu(3)
        A, B, G = emit_modprep(P, C, "f0", I["norm_ffn"][0], mod_slices(mod_d, 0, 4), mod_slices(mod_d, 0, 3), mod_slices(mod_d, 0, 5), 2)
        emit_ffn(P, C, "f0", x1, x2, 1280, [0] * 8 + [1] * 2, A, B, G, I["w1"], I["w3"], I["w2"])
    hTdb = P.buf("hTd")
    with P.scope():
        hT = P.sbuf("l1_hT", [128, KC, 1280], BF16)
        hTb = P.buf()
        A, B, _ = emit_modprep(P, C, "s5m", I["norm_mix"][1], mod_slices(mod_d, 1, 1), mod_slices(mod_d, 1, 0), None, 2)
        srcs = [(x2[t * 128:(t + 1) * 128, :], 0 if t < 8 else 1, t * 128) for t in range(10)]
        emit_norm_all(P, C, srcs, hT, hTb, A, B)
        P.dma("sp", hTd.rearrange("k p t -> p k t"), hT[:], reads=[hTb], writes=[hTdb])
    with P.scope():
        emit_s5a(P, C, I, hTd, hTdb)
    print("launch1 n_instr", P.n_instr)
    return P.finish_all()


def build_launch2():
    P = Prog()
    nc = P.nc
    P.outb = P.buf("out")

    def inp(name, shape, dt=F32):
        return nc.dram_tensor(name, list(shape), dt, kind="ExternalInput").ap()

    def outp(name, shape, dt=F32, kind="ExternalOutput"):
        return nc.dram_tensor(name, list(shape), dt, kind=kind).ap()
    I = dict(ident=inp("ident", [128, 128]), selT=inp("selT", [128, 64, 128]),
             Vd=inp("Vd", [4, 128, NCH * 3 * NB]), Yd=inp("Yd", [GS, 128, 128]), Ccd=inp("Ccd", [16, 128, 2048], BF16),
             L8d=inp("L8d", [128, 2, 2, GS]), Elist=inp("Elist", [9, 128, 2 * GS]), mod_d=inp("mod_d", [2, 2, NMOD]),
             x2=inp("x2", [1024, D]), w_glu=inp("w_glu", [D, 2 * D]), b_glu=inp("b_glu", [2 * D]),
             norm_ffn=inp("norm_ffn", [D]), norm_final=inp("norm_final", [D]),
             w1=inp("w1", [D, DFF]), w3=inp("w3", [D, DFF]), w2=inp("w2", [DFF, D]))
    out = outp("out", [1024, D])
    x3 = outp("x3", [1024, D], kind="Internal")
    x4 = outp("x4", [1024, D], kind="Internal")
    mod_d = I["mod_d"]
    C = Ctx(P, I["ident"])
    x3b = P.buf("x3")
    with P.scope():
        gyT = P.sbuf("gyT", [128, KC, 1024], BF16)
        gyTb = P.buf()
        with P.scope():
            emit_s5b(P, C, I, gyT, gyTb)
        with P.scope():
            C.alloc_wu(2)
            emit_glu(P, C, gyT, gyTb, I["x2"], x3, x3b, I["w_glu"], I["b_glu"], mod_d[1, 0, 2 * D:3 * D])
    with P.scope():
        C.alloc_wu(3)
        A, B, G = emit_modprep(P, C, "f1", I["norm_ffn"], [mod_d[1, 0, 4 * D:5 * D]], [mod_d[1, 0, 3 * D:4 * D]], [mod_d[1, 0, 5 * D:6 * D]], 1)
        P.outb = P.buf("x4")
        emit_ffn(P, C, "f1", x3, x4, 1024, [0] * 8, A, B, G, I["w1"], I["w3"], I["w2"])
    with P.scope():
        ob = P.buf("final")
        emit_final_norm(P, C, x4, P.outb, out, ob, I["norm_final"])
    print("launch2 n_instr", P.n_instr)
    return P.finish_all()


def kernel(x, c, ctx, c_ctx, ada_w, ada_b, norm_mix, norm_ffn, ffn_w1, ffn_w3, ffn_w2,
           attn_w_in, attn_w_out, attn_rpb, attn_sink,
           ssm_a_re, ssm_a_im, ssm_log_dt, ssm_b_re, ssm_b_im, ssm_c_re, ssm_c_im,
           ssm_d, ssm_w_glu, ssm_b_glu, norm_final):
    from concourse.bass_utils import run_bass_kernel_spmd
    f32 = lambda a: np.ascontiguousarray(np.asarray(a, dtype=np.float32))
    x = f32(x)
    ctx = f32(ctx)
    ncore = 8
    ident, rperm = host_consts()
    sel, selT, maskf, maskb = host_s5_consts()
    rpb_ext = np.concatenate([f32(attn_rpb)[0].reshape(8, 465), np.full((8, 1), -30000.0, np.float32)], axis=1)
    shared1 = dict(ident=ident, rperm=rperm, ctx=ctx[0], c=f32(c)[0], c_ctx=f32(c_ctx), ada_w=f32(ada_w), ada_b=f32(ada_b),
                   norm_mix=f32(norm_mix), norm_ffn=f32(norm_ffn), w_in=f32(attn_w_in)[0], w_out=f32(attn_w_out)[0],
                   sink=f32(attn_sink)[0], w1=f32(ffn_w1)[0], w3=f32(ffn_w3)[0], w2=f32(ffn_w2)[0],
                   a_re=f32(ssm_a_re)[0], a_im=f32(ssm_a_im)[0], log_dt=f32(ssm_log_dt)[0], b_re=f32(ssm_b_re)[0], b_im=f32(ssm_b_im)[0],
                   c_re=f32(ssm_c_re)[0], c_im=f32(ssm_c_im)[0], ssm_d=f32(ssm_d)[0], sel=sel, maskf=maskf, maskb=maskb)
    in1 = []
    for k in range(ncore):
        d = dict(shared1)
        d["xh"] = host_xh(x[0], k)
        d["nab"] = np.ascontiguousarray(rpb_ext[:, host_na_index(k)].reshape(8, 128, 3200))
        d["swm"] = np.ascontiguousarray(host_sw_mask(k).reshape(128, 1152))
        d["ropec"], d["ropes"] = host_rope(k)
        in1.append(d)
    nc1 = build_launch1()
    r1 = run_bass_kernel_spmd(nc1, in1, core_ids=list(range(ncore))).results
    Eown = [np.asarray(r1[k]["Eown"], dtype=np.float32) for k in range(ncore)]
    Ectx = np.asarray(r1[0]["Ectx"], dtype=np.float32)
    zero = np.zeros_like(Ectx)
    shared2 = dict(ident=ident, selT=selT, w_glu=f32(ssm_w_glu)[0], b_glu=f32(ssm_b_glu)[0], norm_ffn=f32(norm_ffn)[1],
                   norm_final=f32(norm_final), w1=f32(ffn_w1)[1], w3=f32(ffn_w3)[1], w2=f32(ffn_w2)[1])
    in2 = []
    for k in range(ncore):
        lf = [Ectx] + [Eown[m] for m in range(k)]
        lb = [Ectx] + [Eown[m] for m in range(ncore - 1, k, -1)]
        lf = [zero] * (9 - len(lf)) + lf
        lb = [zero] * (9 - len(lb)) + lb
        El = np.stack([np.concatenate([lf[s][0:64], lb[s][64:128]], axis=0) for s in range(9)])
        d = dict(shared2)
        d.update(Vd=r1[k]["Vd"], Yd=r1[k]["Yd"], Ccd=r1[k]["Ccd"], L8d=r1[k]["L8d"], Elist=np.ascontiguousarray(El),
                 mod_d=r1[k]["mod_d"], x2=np.ascontiguousarray(np.asarray(r1[k]["x2"])[0:1024]))
        in2.append(d)
    nc2 = build_launch2()
    r2 = run_bass_kernel_spmd(nc2, in2, core_ids=list(range(ncore))).results
    out = np.concatenate([np.asarray(r2[k]["out"], dtype=np.float32) for k in range(ncore)], axis=0)
    return out[None]
```
